# Optimizing a Trainium2 kernel written in Bass

```python
import math
import jax, jax.numpy as jnp
from jax import lax
import numpy as np

D_MODEL = 1024
BATCH = 16
SEQ = 256
DEPTH = 2
DEC_BATCH = 8
DEC_SEQ = 1024
PAST_LEN = 512

GRID_W = 64
MIX_W = D_MODEL
ATT_W = MIX_W // 2
SSM_W = MIX_W - ATT_W
N_HEADS = 4
D_V = ATT_W // N_HEADS
D_SUB = D_V // 2
ROPE_AXIS = D_SUB // 2
ROPE_THETA = 10000.0
Q_BLOCK = 128
SSM_CH = 16
SSM_GROUPS = SSM_W // SSM_CH
SSM_P = 64
D_FF = 2 * D_MODEL
CONV_W = 3
N_MOD = 6
EPS = 1e-6

kernel_name = 'hybrid_diffattn_s5_dit_step'


def rms_norm(x, g):
    xf = x.astype(jnp.float32)
    y = xf * lax.rsqrt(jnp.mean(xf * xf, axis=-1, keepdims=True) + EPS)
    return (y * g.astype(jnp.float32)).astype(x.dtype)


def adaln_mod(cond, w_mod_l, b_mod_l):
    m = jax.nn.silu(cond) @ w_mod_l + b_mod_l
    sh1, sc1, g1, sh2, sc2, g2 = jnp.split(m[:, None, :], N_MOD, axis=-1)
    return sh1, sc1, g1, sh2, sc2, g2


def modulate(x, g, shift, scale):
    return rms_norm(x, g) * (1.0 + scale) + shift


def axial_rope_tables(n_tokens):
    rows = n_tokens // GRID_W
    f32 = jnp.float32
    row = jnp.repeat(jnp.arange(rows, dtype=f32), GRID_W, total_repeat_length=n_tokens)
    col = jnp.tile(jnp.arange(GRID_W, dtype=f32), rows)
    nf = ROPE_AXIS // 2
    inv_freq = 1.0 / (ROPE_THETA ** (jnp.arange(nf, dtype=f32) / nf))
    ang_r = row[:, None] * inv_freq
    ang_c = col[:, None] * inv_freq
    return jnp.cos(ang_r), jnp.sin(ang_r), jnp.cos(ang_c), jnp.sin(ang_c)


def rotate(x, cos, sin):
    cos = cos[:, None, None, :]
    sin = sin[:, None, None, :]
    x1, x2 = jnp.split(x, 2, axis=-1)
    return jnp.concatenate([x1 * cos - x2 * sin, x1 * sin + x2 * cos], axis=-1)


def axial_rope(x, tables):
    cr, sr, cc, sc = tables
    xr, xc = jnp.split(x, 2, axis=-1)
    return jnp.concatenate([rotate(xr, cr, sr), rotate(xc, cc, sc)], axis=-1).astype(x.dtype)


def diff_attention(q, k, v, lam):
    B, Lq = q.shape[0], q.shape[1]
    nb = Lq // Q_BLOCK
    qb = q.reshape(B, nb, Q_BLOCK, N_HEADS, 2, D_SUB).transpose(1, 0, 2, 3, 4, 5)
    vf = v.astype(jnp.float32)
    scale = D_SUB ** -0.5

    def one_block(qblk):
        s = jnp.einsum('bqhsd,bkhsd->bhsqk', qblk, k).astype(jnp.float32) * scale
        p = jax.nn.softmax(s, axis=-1)
        a = p[:, :, 0] - lam * p[:, :, 1]
        return jnp.einsum('bhqk,bkhd->bqhd', a, vf)

    o = lax.map(one_block, qb)
    return o.transpose(1, 0, 2, 3, 4).reshape(B, Lq, N_HEADS, D_V)


def _lin_rec(e1, e2):
    a1, b1 = e1
    a2, b2 = e2
    return a1 * a2, a2 * b1 + b2


def to_complex(s):
    return lax.complex(s[..., 0].astype(jnp.float32), s[..., 1].astype(jnp.float32))


def to_real(h):
    return jnp.stack([jnp.real(h), jnp.imag(h)], axis=-1)


def s5_scan(ug, h0, lam_re, lam_im, log_step, b_re, b_im, c_re, c_im, reverse):
    f32 = jnp.float32
    lam = lax.complex(lam_re.astype(f32), lam_im.astype(f32))
    delta = jnp.exp(log_step.astype(f32))[:, None]
    lam_bar = jnp.exp(lam * delta)
    b_bar = ((lam_bar - 1.0) / lam)[..., None] * lax.complex(b_re.astype(f32), b_im.astype(f32))
    c = lax.complex(c_re.astype(f32), c_im.astype(f32))
    bu = jnp.einsum('blgc,gpc->blgp', ug, b_bar)
    edge = -1 if reverse else 0
    bu = bu.at[:, edge].add(lam_bar * h0)
    a = jnp.broadcast_to(lam_bar, bu.shape)
    _, h = lax.associative_scan(_lin_rec, (a, bu), reverse=reverse, axis=1)
    y = jnp.real(jnp.einsum('blgp,gcp->blgc', h, c))
    return y, h[:, edge]


def s5_bidirectional(u, h0, lam_re, lam_im, log_step, b_re, b_im, c_re, c_im, d):
    B, L, _ = u.shape
    uf = u.astype(jnp.float32)
    ug = uf.reshape(B, L, SSM_GROUPS, SSM_CH).astype(jnp.complex64)
    y_f, h_f = s5_scan(ug, h0[:, 0], lam_re[0], lam_im[0], log_step[0], b_re[0], b_im[0], c_re[0], c_im[0], False)
    y_b, h_b = s5_scan(ug, h0[:, 1], lam_re[1], lam_im[1], log_step[1], b_re[1], b_im[1], c_re[1], c_im[1], True)
    y = (y_f + y_b).reshape(B, L, SSM_W) + d.astype(jnp.float32) * uf
    return y, jnp.stack([h_f, h_b], axis=1)


def half_glu(y, w, b):
    z = jax.nn.gelu(y)
    return z * jax.nn.sigmoid(z @ w.astype(jnp.float32) + b.astype(jnp.float32))


def conv_ffn(h, w_up_l, conv_w_l, conv_b_l, w_down_l):
    up = h @ w_up_l
    L = up.shape[1]
    p = jnp.pad(up, ((0, 0), (1, 1), (0, 0)))
    z = p[:, :L] * conv_w_l[0] + p[:, 1:L + 1] * conv_w_l[1] + p[:, 2:] * conv_w_l[2] + conv_b_l
    val, gate = jnp.split(z, 2, axis=-1)
    return (jax.nn.silu(gate) * val) @ w_down_l


def setup_inputs(seed: int = 0) -> dict:
    key = jax.random.key(seed)
    ks = jax.random.split(key, 40)
    f32 = jnp.float32

    def nrm(i, shape, s):
        return jax.random.normal(ks[i], shape, f32) * s

    F2 = 2 * D_FF
    G, P, CH = SSM_GROUPS, SSM_P, SSM_CH
    return {
        'x_prompt': nrm(0, (BATCH, SEQ, D_MODEL), 1.0),
        'x_sample': nrm(1, (DEC_BATCH, DEC_SEQ, D_MODEL), 1.0),
        'cache_k': nrm(2, (DEC_BATCH, DEPTH, PAST_LEN, N_HEADS, 2, D_SUB), 1.0),
        'cache_v': nrm(3, (DEC_BATCH, DEPTH, PAST_LEN, N_HEADS, D_V), 0.5),
        'state_ssm': nrm(4, (DEC_BATCH, DEPTH, 2, G, P, 2), 0.3),
        'c': nrm(5, (DEC_BATCH, D_MODEL), 1.0),
        'c_ctx': nrm(6, (D_MODEL,), 1.0),
        'w_mod': nrm(7, (DEPTH, D_MODEL, N_MOD * D_MODEL), 0.5 * D_MODEL ** -0.5),
        'b_mod': nrm(8, (DEPTH, N_MOD * D_MODEL), 0.01),
        'g_norm1': 1.0 + nrm(9, (DEPTH, D_MODEL), 0.02),
        'w_in': nrm(10, (DEPTH, D_MODEL, 3 * ATT_W + SSM_W), D_MODEL ** -0.5),
        'q_norm': 1.0 + nrm(11, (DEPTH, D_SUB), 0.02),
        'k_norm': 1.0 + nrm(12, (DEPTH, D_SUB), 0.02),
        'lambda_q1': nrm(13, (DEPTH, D_SUB), 0.1),
        'lambda_k1': nrm(14, (DEPTH, D_SUB), 0.1),
        'lambda_q2': nrm(15, (DEPTH, D_SUB), 0.1),
        'lambda_k2': nrm(16, (DEPTH, D_SUB), 0.1),
        'subln_g': 1.0 + nrm(17, (DEPTH, D_V), 0.02),
        'ssm_lambda_re': -0.5 + nrm(18, (DEPTH, 2, G, P), 0.01),
        'ssm_lambda_im': jnp.pi * jnp.arange(P, dtype=f32) + nrm(19, (DEPTH, 2, G, P), 0.01),
        'ssm_log_step': jax.random.uniform(ks[20], (DEPTH, 2, G), f32, math.log(1e-3), math.log(1e-1)),
        'ssm_b_re': nrm(21, (DEPTH, 2, G, P, CH), (2 * CH) ** -0.5),
        'ssm_b_im': nrm(22, (DEPTH, 2, G, P, CH), (2 * CH) ** -0.5),
        'ssm_c_re': nrm(23, (DEPTH, 2, G, CH, P), (2 * P) ** -0.5),
        'ssm_c_im': nrm(24, (DEPTH, 2, G, CH, P), (2 * P) ** -0.5),
        'ssm_d': nrm(25, (DEPTH, SSM_W), 1.0),
        'w_glu': nrm(26, (DEPTH, SSM_W, SSM_W), SSM_W ** -0.5),
        'b_glu': nrm(27, (DEPTH, SSM_W), 0.01),
        'w_out': nrm(28, (DEPTH, MIX_W, D_MODEL), MIX_W ** -0.5),
        'g_norm2': 1.0 + nrm(29, (DEPTH, D_MODEL), 0.02),
        'w_up': nrm(30, (DEPTH, D_MODEL, F2), D_MODEL ** -0.5),
        'conv_w': nrm(31, (DEPTH, CONV_W, F2), 0.3) + jnp.array([0.0, 1.0, 0.0], f32)[None, :, None],
        'conv_b': nrm(32, (DEPTH, F2), 0.01),
        'w_down': nrm(33, (DEPTH, D_FF, D_MODEL), D_FF ** -0.5),
    }


def reference(x_prompt, x_sample, cache_k, cache_v, state_ssm, c, c_ctx, w_mod, b_mod, g_norm1, w_in,
              q_norm, k_norm, lambda_q1, lambda_k1, lambda_q2, lambda_k2, subln_g,
              ssm_lambda_re, ssm_lambda_im, ssm_log_step, ssm_b_re, ssm_b_im, ssm_c_re, ssm_c_im, ssm_d,
              w_glu, b_glu, w_out, g_norm2, w_up, conv_w, conv_b, w_down):
    f32 = jnp.float32

    def layer(x, cond, l, tables, ctx_k, ctx_v, h0):
        sh1, sc1, gt1, sh2, sc2, gt2 = adaln_mod(cond, w_mod[l], b_mod[l])
        B, L, _ = x.shape
        h = modulate(x, g_norm1[l], sh1, sc1)
        q, k, v, u = jnp.split(h @ w_in[l], [ATT_W, 2 * ATT_W, 3 * ATT_W], axis=-1)
        q = rms_norm(q.reshape(B, L, N_HEADS, 2, D_SUB), q_norm[l])
        k = rms_norm(k.reshape(B, L, N_HEADS, 2, D_SUB), k_norm[l])
        v = v.reshape(B, L, N_HEADS, D_V)
        if tables is None:
            keys, vals = k, v
        else:
            q = axial_rope(q, tables)
            keys = jnp.concatenate([ctx_k.astype(k.dtype), axial_rope(k, tables)], axis=1)
            vals = jnp.concatenate([ctx_v.astype(v.dtype), v], axis=1)
        lam_init = 0.8 - 0.6 * math.exp(-0.3 * l)
        lam = (jnp.exp(jnp.sum(lambda_q1[l].astype(f32) * lambda_k1[l].astype(f32)))
               - jnp.exp(jnp.sum(lambda_q2[l].astype(f32) * lambda_k2[l].astype(f32))) + lam_init)
        o_att = diff_attention(q, keys, vals, lam)
        o_att = (rms_norm(o_att, subln_g[l]) * (1.0 - lam_init)).reshape(B, L, ATT_W).astype(x.dtype)
        y_ssm, h_last = s5_bidirectional(u, h0, ssm_lambda_re[l], ssm_lambda_im[l], ssm_log_step[l],
                                         ssm_b_re[l], ssm_b_im[l], ssm_c_re[l], ssm_c_im[l], ssm_d[l])
        o_ssm = half_glu(y_ssm, w_glu[l], b_glu[l]).astype(x.dtype)
        x = x + gt1 * (jnp.concatenate([o_att, o_ssm], axis=-1) @ w_out[l])
        h2 = modulate(x, g_norm2[l], sh2, sc2)
        x = x + gt2 * conv_ffn(h2, w_up[l], conv_w[l], conv_b[l], w_down[l])
        return x, k, v, h_last

    cond_ctx = c_ctx[None, :]
    h0_zero = jnp.zeros((x_prompt.shape[0], 2, SSM_GROUPS, SSM_P), jnp.complex64)
    xp = x_prompt
    ks_, vs_, ss_ = [], [], []
    for l in range(DEPTH):
        xp, k_l, v_l, h_l = layer(xp, cond_ctx, l, None, None, None, h0_zero)
        ks_.append(k_l)
        vs_.append(v_l)
        ss_.append(to_real(h_l))
    new_k = jnp.stack(ks_, axis=1)
    new_v = jnp.stack(vs_, axis=1)
    new_ssm = jnp.stack(ss_, axis=1)

    tables = axial_rope_tables(x_sample.shape[1])
    xs = x_sample
    for l in range(DEPTH):
        xs, _, _, _ = layer(xs, c, l, tables, cache_k[:, l], cache_v[:, l], to_complex(state_ssm[:, l]))

    return (xp, xs, new_k, new_v, new_ssm)
```

```python
import contextlib
import math
import numpy as np
import concourse.bass as bass
import concourse.mybir as mybir
from concourse.bass_utils import run_bass_kernel_spmd

F32 = mybir.dt.float32
BF16 = mybir.dt.bfloat16
AF = mybir.ActivationFunctionType
ALU = mybir.AluOpType
AX = mybir.AxisListType

ENGS = ("pe", "act", "dve", "pool", "sp")
STRICT_SAME = True
EPS = 1e-6
NCOL = 1536
DEPTH = 2


class Phase:
    def __init__(self, nc, name):
        self.nc = nc
        self.name = name
        self.ops = []
        self.last_w = {}
        self.readers = {}

    def _add(self, eng, fn, reads, writes, is_dma):
        def canon(k):
            return k[:3] if isinstance(k, str) and len(k) >= 3 and k.startswith("ps") and k[2].isdigit() else k
        reads = tuple(canon(k) for k in reads)
        writes = tuple(canon(k) for k in writes)
        writes = writes + tuple(k for k in reads if isinstance(k, str) and len(k) == 3 and k.startswith("ps")
                                and k[2].isdigit() and k not in writes)
        idx = len(self.ops)
        deps = set()
        raw = set()
        for k in reads:
            if k in self.last_w:
                deps.add(self.last_w[k])
                if not (isinstance(k, str) and len(k) == 3 and k.startswith("ps") and k[2].isdigit()
                        and self.ops[self.last_w[k]]["eng"] != "pe"):
                    raw.add(self.last_w[k])
        waw = set()
        war = set()
        for k in writes:
            if k in self.last_w:
                deps.add(self.last_w[k])
                if not (isinstance(k, str) and len(k) == 3 and k.startswith("ps") and k[2].isdigit()):
                    waw.add(self.last_w[k])
            for r in self.readers.get(k, ()):
                deps.add(r)
                if not (isinstance(k, str) and len(k) == 3 and k.startswith("ps") and k[2].isdigit()):
                    war.add(r)
        fd = set()
        for d in deps:
            o = self.ops[d]
            if o["is_dma"] or o["eng"] != eng or is_dma:
                fd.add(d)
            elif STRICT_SAME and eng != "pe" and (d in raw or d in waw or d in war):
                fd.add(d)
        for d in fd:
            self.ops[d]["signal"] = True
        import sys as _s
        self.ops.append(dict(eng=eng, fn=fn, deps=sorted(fd), is_dma=is_dma, signal=False,
                             tag=_s._getframe(3).f_lineno if _DBG.get("trunc") else None))
        for k in writes:
            self.last_w[k] = idx
            self.readers[k] = []
        for k in reads:
            if k not in writes:
                self.readers.setdefault(k, []).append(idx)
        return idx

    def op(self, eng, fn, reads=(), writes=()):
        return self._add(eng, fn, tuple(reads), tuple(writes), False)

    def dma(self, eng, out, in_, reads=(), writes=(), **kw):
        return self._add(eng, lambda e: e.dma_start(out=out, in_=in_, **kw), tuple(reads), tuple(writes), True)

    def emit(self):
        nc = self.nc
        ops = self.ops
        tr = _DBG.get("trunc")
        if tr is not None and tr[0] == self.name:
            print("TRUNC", self.name, "total ops", len(ops), "->", tr[1])
            for i_, o_ in enumerate(ops[:tr[1]][-3:]):
                print("   last ops:", o_["eng"], o_.get("tag"))
            ops = ops[:tr[1]]
            self.ops = ops
        ndma = sum(1 for o in ops if o["is_dma"])
        with contextlib.ExitStack() as st:
            G = getattr(nc, "_phase_sems", None)
            if G is None:
                G = dict(esem={e: nc.alloc_semaphore(name=f"g_{e}") for e in ENGS},
                         dsem=[nc.alloc_semaphore(name=f"g_d{i}") for i in range(88)],
                         ecount={e: 0 for e in ENGS}, dcount=[0] * 88, nxt=0)
                nc._phase_sems = G
            esem = G["esem"]; dsem = G["dsem"]; dcount = G["dcount"]; ecount = G["ecount"]
            NP = len(dsem)
            assert ndma <= NP, (self.name, ndma)
            used = set()
            for o in ops:
                if o["is_dma"]:
                    k = G["nxt"] % NP
                    G["nxt"] += 1
                    o["sem"] = k
                    dcount[k] += 16
                    o["val"] = dcount[k]
                    used.add(k)
                elif o["signal"]:
                    ecount[o["eng"]] += 1
                    o["val"] = ecount[o["eng"]]
            block = st.enter_context(nc.Block())

            def body(ename):
                def f(e):
                    waited = {}
                    for o in ops:
                        if o["eng"] != ename:
                            continue
                        for d in o["deps"]:
                            p = ops[d]
                            if p["is_dma"]:
                                key = ("d", p["sem"])
                                sem = dsem[p["sem"]]
                            else:
                                key = ("e", p["eng"])
                                sem = esem[p["eng"]]
                            if waited.get(key, 0) >= p["val"]:
                                continue
                            waited[key] = p["val"]
                            e.wait_ge(sem, p["val"])
                        ins = o["fn"](e)
                        if o["is_dma"]:
                            ins.then_inc(dsem[o["sem"]], 16)
                        elif o["signal"]:
                            ins.then_inc(esem[ename], 1)
                    if ename == "sp":
                        for i in sorted(used):
                            if waited.get(("d", i), 0) < dcount[i]:
                                e.wait_ge(dsem[i], dcount[i])
                return f

            block.tensor(body("pe"))
            block.scalar(body("act"))
            block.vector(body("dve"))
            block.gpsimd(body("pool"))
            block.sync(body("sp"))


def ACT(ph, out, in_, func, r, w, **kw):
    ph.op("act", lambda e: e.activation(out=out, in_=in_, func=func, **kw), r, w)


def TT(ph, out, in0, in1, op, r, w, eng="dve"):
    ph.op(eng, lambda e: e.tensor_tensor(out=out, in0=in0, in1=in1, op=op), r, w)


def STT(ph, out, in0, scalar, in1, op0, op1, r, w, eng="dve"):
    ph.op(eng, lambda e: e.scalar_tensor_tensor(out=out, in0=in0, scalar=scalar, in1=in1, op0=op0, op1=op1), r, w)


def TS(ph, out, in0, s1, s2, op0, op1, r, w, eng="dve"):
    if s2 is None:
        ph.op(eng, lambda e: e.tensor_scalar(out=out, in0=in0, scalar1=s1, scalar2=None, op0=op0), r, w)
    else:
        ph.op(eng, lambda e: e.tensor_scalar(out=out, in0=in0, scalar1=s1, scalar2=s2, op0=op0, op1=op1), r, w)


def CP(ph, out, in_, r, w, eng="dve"):
    ph.op(eng, lambda e: e.tensor_copy(out=out, in_=in_), r, w)


def MM(ph, out, lhsT, rhs, start, stop, r, w, tp=None):
    if tp is None:
        ph.op("pe", lambda e: e.matmul(out, lhsT, rhs, start=start, stop=stop), r, w)
    else:
        ph.op("pe", lambda e: e.matmul(out, lhsT, rhs, start=start, stop=stop, tile_position=tp), r, w)


def TR(ph, out, in_, ident, r, w):
    ph.op("pe", lambda e: e.transpose(out, in_, ident), r, w)


def MS(ph, ap, val, w, eng="dve"):
    ph.op(eng, lambda e: e.memset(ap, val), (), w)


def bc(ap, shape):
    return ap.broadcast_to(shape)


class _Stop(Exception):
    pass


_DBG = {"stop": None, "trunc": None}


def build_program():
    nc = bass.Bass("TRN2", target_bir_lowering=False)
    try:
        _build(nc)
    except _Stop:
        pass
    return nc


def _build(nc):

    def din(name, shape):
        return nc.dram_tensor(name, list(shape), F32, kind="ExternalInput").ap()

    def dout(name, shape):
        return nc.dram_tensor(name, list(shape), F32, kind="ExternalOutput").ap()

    xp = din("xp", [2, 256, 1024]); xs = din("xs", [1024, 1024])
    ck = din("ck", [2, 512, 512]); cv = din("cv", [2, 512, 512])
    stt_in = din("st", [2, 32, 256]); cvec = din("cvec", [16, 128])
    w_mod = din("w_mod", [2, 1024, 6144]); w_in = din("w_in", [2, 1024, 2048])
    w_out = din("w_out", [2, 1024, 1024]); w_glu = din("w_glu", [2, 512, 512])
    w_up = din("w_up", [2, 1024, 4096]); w_down = din("w_down", [2, 2048, 1024])
    va = din("va", [2, 100, 128]); vb = din("vb", [2, 96, 128])
    qkn = din("qkn", [2, 2, 64]); lamv = din("lamv", [2, 4, 64]); subg = din("subg", [2, 128])
    lre_d = din("lre", [2, 32, 128]); lim_d = din("lim", [2, 32, 128]); lst_d = din("lstep", [2, 32, 2])
    bre_d = din("bre", [2, 2, 32, 64, 16]); bim_d = din("bim", [2, 2, 32, 64, 16])
    cre_d = din("cre", [2, 2, 32, 16, 64]); cim_d = din("cim", [2, 2, 32, 16, 64])
    ssmd_d = din("ssmd", [2, 512])
    ident_d = din("ident", [128, 128]); cos_d = din("cosT", [128, 1024]); sin_d = din("sinT", [128, 1024])
    rsign_d = din("rsign", [128, 128])
    yp = dout("yp", [2, 256, 1024]); ys = dout("ys", [1024, 1024])
    nk = dout("nk", [2, 2, 256, 512]); nv = dout("nv", [2, 2, 256, 512]); nssm = dout("nssm", [2, 2, 32, 256])
    if _DBG["stop"] is not None:
        dbgX = dout("dbgX", [128, 8 * NCOL]); dbgH = dout("dbgH", [128, 8 * NCOL])
        dbgM = dout("dbgM", [128, 2 * 48 * 2])

    es = contextlib.ExitStack()
    with es:
        def sb(name, shape, dt=F32):
            return es.enter_context(nc.sbuf_tensor("s_" + name, list(shape), dt))

        ps = [es.enter_context(nc.psum_tensor(f"ps{i}", [128, 512], F32)) for i in range(8)]
        X = sb("X", [128, 8, NCOL])
        ident = sb("ident", [128, 128]); identb = sb("identb", [128, 128], BF16)
        onesD = sb("onesD", [128, 128], BF16); blk64 = sb("blk64", [128, 128], BF16)
        ones1 = sb("ones1", [128, 128], BF16); onesV = sb("onesV", [128, 128], BF16)
        ones32 = sb("ones32", [128, 128]); zerob = sb("zerob", [128, 128], BF16)
        ones32q = sb("ones32q", [128, 2, 128])
        epsc = sb("epsc", [128, 1]); negpi = sb("negpi", [128, 1])
        cosT = sb("cosT", [128, 1024]); sinT = sb("sinT", [128, 1024]); rsign = sb("rsign", [128, 128])
        VT = sb("VT", [128, 2, 100]); VTB = sb("VTB", [128, 2, 96])
        scond = sb("scond", [128, 8, 2]); scondb = sb("scondb", [128, 8, 2], BF16)
        modsb = sb("modsb", [128, 2, 48, 2])
        A1 = sb("A1", [128, 2, 8, 2]); A2 = sb("A2", [128, 2, 8, 2])
        gq = sb("gq", [128, 2, 2])
        gsub = sb("gsub", [128, 2])
        neglam = sb("neglam", [128, 2])
        Rq = sb("Rq", [128, 2, 2, 128], BF16)
        hmod = sb("hmod", [128, 8, NCOL], BF16)

        def psk(i):
            return f"ps{i}"

        def checkpoint(name):
            if _DBG["stop"] != name:
                return
            ph = Phase(nc, "dbg")
            ph.dma("sp", dbgX[:, :], X[:].rearrange("p c n -> p (c n)"))
            ph.dma("pool", dbgH[:, :], hmod[:].rearrange("p c n -> p (c n)"))
            ph.dma("sp", dbgM[:, :], modsb[:].rearrange("p l j c -> p (l j c)"))
            ph.emit()
            raise _Stop()

        def mod_half(ph, l, hb, wbuf, wkey, pbank):
            wmv = w_mod[l].rearrange("(kc p) f -> p kc f", p=128)
            ph.dma("pool", wbuf[:], wmv[:, :, 512 * hb:512 * hb + 512], reads=[wkey + "r"], writes=[wkey])

            def compute():
                for j4 in range(4):
                    for kc in range(8):
                        last = (j4 == 3 and kc == 7)
                        MM(ph, ps[pbank][:, 2 * j4:2 * j4 + 2], wbuf[:, kc, 128 * j4:128 * j4 + 128], scondb[:, kc, :],
                           kc == 0, kc == 7, [wkey], [psk(pbank), wkey + "r"] if last else [psk(pbank)])
                j0 = 4 * hb
                TT(ph, modsb[:, l, j0:j0 + 4, :], ps[pbank][:, 0:8].rearrange("p (j c) -> p j c", c=2),
                   bc(VT[:, l, 16 + j0:16 + j0 + 4].unsqueeze(2), [128, 4, 2]), ALU.add, [psk(pbank)], [("modsb", l, hb)])
            return compute

        def mod_A(ph, l, Aout, goff, scoff):
            rk = [("modsb", l, hb) for hb in range(12)]
            TS(ph, Aout[:, l], modsb[:, l, scoff:scoff + 8, :], 1.0, None, ALU.add, None, rk, [("A", l, goff)])
            TT(ph, Aout[:, l], Aout[:, l], bc(VT[:, l, goff:goff + 8].unsqueeze(2), [128, 8, 2]), ALU.mult,
               [("A", l, goff)], [("A", l, goff)])

        with contextlib.ExitStack() as s0:
            xtile = [s0.enter_context(nc.sbuf_tensor(f"xtile{i}", [128, 1024], F32)) for i in range(4)]
            wm = [s0.enter_context(nc.sbuf_tensor(f"wm{i}", [128, 8, 512], BF16)) for i in range(2)]
            vtmp = s0.enter_context(nc.sbuf_tensor("vtmp", [128, 128], F32))
            ctmp = s0.enter_context(nc.sbuf_tensor("ctmp", [128, 16], F32))
            lt = s0.enter_context(nc.sbuf_tensor("lt", [64, 2, 4], F32))
            lp = s0.enter_context(nc.sbuf_tensor("lp", [64, 2, 2], F32))
            le = s0.enter_context(nc.sbuf_tensor("le", [128, 2, 2], F32))
            ph = Phase(nc, "p0")
            ph.dma("sp", ident[:], ident_d[:, :], writes=["ident"])
            ph.dma("sp", cosT[:], cos_d[:, :], writes=["cos"])
            ph.dma("sp", sinT[:], sin_d[:, :], writes=["sin"])
            ph.dma("sp", rsign[:], rsign_d[:, :], writes=["rsign"])
            CP(ph, identb[:], ident[:], ["ident"], ["identb"])
            MS(ph, onesD[:], 1.0 / 1024.0, ["onesD"])
            MS(ph, ones1[:], 1.0, ["ones1"])
            MS(ph, onesV[:], 1.0 / 128.0, ["onesV"])
            MS(ph, ones32[:], 1.0, ["ones32"])
            MS(ph, ones32q[:], 0.0, ["ones32q"])
            MS(ph, ones32q[0:64, 0, :], 1.0, ["ones32q"])
            MS(ph, ones32q[64:128, 1, :], 1.0, ["ones32q"])
            MS(ph, zerob[:], 0.0, ["zerob"])
            MS(ph, blk64[:], 0.0, ["blk64"])
            MS(ph, blk64[0:64, 0:64], 1.0 / 64.0, ["blk64"])
            MS(ph, blk64[64:128, 64:128], 1.0 / 64.0, ["blk64"])
            MS(ph, epsc[:], EPS, ["epsc"])
            MS(ph, negpi[:], -math.pi, ["negpi"])
            for l in range(2):
                ph.dma("sp", vtmp[0:100, :], va[l], reads=["vtmp_r"], writes=["vtmp"])
                TR(ph, ps[0][:, 0:100], vtmp[0:100, :], ident[0:100, 0:100], ["vtmp", "ident"], ["ps0"])
                CP(ph, VT[:, l, :], ps[0][:, 0:100], ["ps0"], ["VT", "vtmp_r"])
                ph.dma("sp", vtmp[0:96, :], vb[l], reads=["vtmp_r"], writes=["vtmp"])
                TR(ph, ps[0][:, 0:96], vtmp[0:96, :], ident[0:96, 0:96], ["vtmp", "ident"], ["ps0"])
                CP(ph, VTB[:, l, :], ps[0][:, 0:96], ["ps0"], ["VTB", "vtmp_r"])
            ph.dma("sp", vtmp[0:16, :], cvec[:, :], reads=["vtmp_r"], writes=["vtmp"])
            TR(ph, ps[0][:, 0:16], vtmp[0:16, :], ident[0:16, 0:16], ["vtmp", "ident"], ["ps0"])
            ACT(ph, ctmp[:], ps[0][:, 0:16], AF.Silu, ["ps0"], ["ctmp", "vtmp_r"])
            CP(ph, scond[:], ctmp[:].rearrange("p (c k) -> p k c", c=2), ["ctmp"], ["scond"])
            CP(ph, scondb[:], scond[:], ["scond"], ["scondb"])
            for l in range(2):
                for j in range(2):
                    for half in range(2):
                        ph.dma("sp", gq[64 * half:64 * half + 64, l, j:j + 1],
                               qkn[l, j].rearrange("(p o) -> p o", o=1), writes=["gq"])
                ph.dma("sp", gsub[:, l:l + 1], subg[l].rearrange("(p o) -> p o", o=1), writes=["gsub"])
                ph.dma("sp", lt[:, l, :], lamv[l].rearrange("k p -> p k"), writes=["lt"],
                       allow_slow_non_contiguous=True)
            for l in range(2):
                lam_init = 0.8 - 0.6 * math.exp(-0.3 * l)
                TS(ph, gsub[:, l:l + 1], gsub[:, l:l + 1], 1.0 - lam_init, None, ALU.mult, None, ["gsub"], ["gsub"])
                TT(ph, lp[:, l, 0:1], lt[:, l, 0:1], lt[:, l, 1:2], ALU.mult, ["lt"], ["lp"])
                TT(ph, lp[:, l, 1:2], lt[:, l, 2:3], lt[:, l, 3:4], ALU.mult, ["lt"], ["lp"])
                MM(ph, ps[1][:, 0:2], ones32[0:64, :], lp[:, l, :], True, True, ["ones32", "lp"], ["ps1"])
                ACT(ph, le[:, l, :], ps[1][:, 0:2], AF.Exp, ["ps1"], ["le"])
                TT(ph, neglam[:, l:l + 1], le[:, l, 1:2], le[:, l, 0:1], ALU.subtract, ["le"], ["neglam"])
                TS(ph, neglam[:, l:l + 1], neglam[:, l:l + 1], -lam_init, None, ALU.add, None, ["neglam"], ["neglam"])
                for j in range(2):
                    TS(ph, Rq[:, l, j, :], rsign[:], gq[:, l, j:j + 1], None, ALU.mult, None, ["rsign", "gq"], ["Rq"])
            mod_pend0 = None
            for tt in range(12):
                if tt % 2 == 0:
                    hb_ = tt // 2
                    c_ = mod_half(ph, 0, hb_, wm[hb_ % 2], f"wm{hb_ % 2}", 6 + hb_ % 2)
                    if mod_pend0 is not None:
                        mod_pend0()
                    mod_pend0 = c_
                xt = xtile[tt % 4]
                xk = f"xt{tt % 4}"
                xq = "sp" if tt % 2 == 0 else "act"
                if tt < 4:
                    for q in range(2):
                        ph.dma(xq, xt[64 * q:64 * q + 64, :],
                               xp[q].rearrange("(j s) d -> s j d", s=4)[tt], reads=[xk + "r"], writes=[xk])
                else:
                    s4, hf = (tt - 4) // 2, (tt - 4) % 2
                    ph.dma(xq, xt[:, :], xs.rearrange("(j s) d -> s j d", s=4)[s4, 128 * hf:128 * hf + 128, :],
                           reads=[xk + "r"], writes=[xk])
                for half in range(2):
                    bank = 2 + (2 * tt + half) % 4
                    for c4 in range(4):
                        c = 4 * half + c4
                        TR(ph, ps[bank][:, 128 * c4:128 * c4 + 128], xt[:, 128 * c:128 * c + 128], ident[:],
                           [xk, "ident"], [psk(bank)])
                    dst = X[:, 4 * half:4 * half + 4, 128 * tt:128 * tt + 128]
                    src = ps[bank][:, :].rearrange("p (c t) -> p c t", c=4)
                    if half == 0:
                        CP(ph, dst, src, [psk(bank)], [("X", tt), xk + "r"])
                    else:
                        ACT(ph, dst, src, AF.Copy, [psk(bank)], [("X", tt), xk + "r"])
            mod_pend0()
            mod_A(ph, 0, A1, 0, 8)
            ph.emit()
            checkpoint("p0")

        MOD_REST = [(0, hb) for hb in range(6, 12)] + [(1, hb) for hb in range(12)]

        def norm_mod(ph, l, Atab, shoff, sq, lnv, rstd, tmp):
            for ct in range(3):
                cd = 0 if ct == 0 else 1
                cols = slice(512 * ct, 512 * ct + 512)
                for c in range(8):
                    ACT(ph, sq[c % 2][:], X[:, c, cols], AF.Square, [("X", ct)], [f"sq{c % 2}"])
                    MM(ph, ps[0][:], onesD[:], sq[c % 2][:], c == 0, c == 7, [f"sq{c % 2}"], ["ps0"])
                ACT(ph, lnv[:], ps[0][:], AF.Ln, ["ps0"], ["lnv"], bias=epsc[:])
                ACT(ph, rstd[:], lnv[:], AF.Exp, ["lnv"], ["rstd"], scale=-0.5)
                for c in range(8):
                    STT(ph, tmp[c % 2][:], X[:, c, cols], Atab[:, l, c, cd:cd + 1], rstd[:], ALU.mult, ALU.mult,
                        [("X", ct), "rstd"], [f"tmp{c % 2}"])
                    ACT(ph, hmod[:, c, cols], tmp[c % 2][:], AF.Identity, [f"tmp{c % 2}"], [("hmod", ct)],
                        bias=modsb[:, l, shoff + c, cd:cd + 1])

        for l in range(DEPTH):
            sL = contextlib.ExitStack()
            sL.__enter__()
            U4 = sL.enter_context(nc.sbuf_tensor(f"U4_{l}", [128, 16, 384], BF16))
            uedge = sL.enter_context(nc.sbuf_tensor(f"uedge_{l}", [128, 4, 512], F32))
            sA = contextlib.ExitStack()
            with sA:
                def sba(name, shape, dt=F32):
                    return sA.enter_context(nc.sbuf_tensor(f"{name}_{l}", list(shape), dt))

                qT = sba("qT", [128, 4, NCOL], BF16)
                kT = sba("kT", [128, 4, 2048], BF16)
                Vtok = sba("Vtok", [128, 16, 512], BF16)
                with contextlib.ExitStack() as sA1:
                    def sb1(name, shape, dt=F32):
                        return sA1.enter_context(nc.sbuf_tensor(f"{name}_{l}", list(shape), dt))
                    win = sb1("win", [128, 8, 2048], BF16)
                    sq = [sb1(f"sq{i}", [128, 512], BF16) for i in range(2)]
                    tmp = [sb1(f"tmp{i}", [128, 512]) for i in range(2)]
                    tmpB = [sb1(f"tmpB{i}", [128, 512]) for i in range(2)]
                    lnv = sb1("lnv", [128, 512]); rstd = sb1("rstd", [128, 512])
                    lrs = [lnv, rstd]; lrk = ["lnv", "rstd"]
                    qf = sb1("qf", [128, 512])
                    qbs = [sb1(f"qb{i}", [128, 512], BF16) for i in range(2)]
                    kst = sb1("kst", [128, 4, 128]); vst = qf
                    ckt = sb1("ckt", [128, 512])
                    hrep = tmpB[0][:].bitcast(BF16).rearrange("p (k n) -> p k n", k=8)
                    ph = Phase(nc, f"A{l}")
                    for kc in range(8):
                        ph.dma("pool", win[:, kc, :], w_in[l, 128 * kc:128 * kc + 128, :], writes=["win"])
                    for i in range(4):
                        ph.dma("pool", Vtok[:, 12 + i, :], cv[l, 128 * i:128 * i + 128, :], writes=[("V", 12 + i)])
                    norm_mod(ph, l, A1, 0, sq, lnv, rstd, tmp)
                    for i in range(4):
                        ph.dma("sp", ckt[:], ck[l, 128 * i:128 * i + 128, :], reads=["cktr"], writes=["ckt"])
                        for h in range(4):
                            TR(ph, ps[1][:, 128 * h:128 * h + 128], ckt[:, 128 * h:128 * h + 128], ident[:],
                               ["ckt", "ident"], ["ps1"])
                        CP(ph, kT[:, :, 1536 + 128 * i:1536 + 128 * i + 128],
                           ps[1][:, :].rearrange("p (h t) -> p h t", h=4), ["ps1"], [("kT", 3), "cktr"])
                    def qk_post(ct, mo, bank, sqt, sqk, par):
                        cols = slice(512 * ct, 512 * ct + 512)
                        isk = mo // 4
                        h = mo % 4

                        def run():
                            lr = lrs[par]; lk = lrk[par]
                            tA = tmp[par]; tAk = f"tmp{par}"
                            tB = tmpB[par]; tBk = f"tmpB{par}"
                            qb = qbs[par]; qbk = f"qb{par}"
                            MM(ph, ps[4][:], blk64[:], sqt[:], True, True, [sqk, "blk64"], ["ps4"])
                            ACT(ph, lr[:], ps[4][:], AF.Ln, ["ps4"], [lk], bias=epsc[:])
                            ACT(ph, lr[:], lr[:], AF.Exp, [lk], [lk], scale=-0.5)
                            gcol = gq[:, l, isk:isk + 1]
                            if ct == 0:
                                if isk == 0:
                                    for q in range(2):
                                        dst = qT[:, h, 0:512].rearrange("p (q s j) -> p q s j", q=2, s=4)[:, q]
                                        src = ps[bank][:, :].rearrange("p (s q j) -> p q s j", q=2, s=4)[:, q]
                                        rs = lr[:].rearrange("p (s q j) -> p q s j", q=2, s=4)[:, q]
                                        STT(ph, dst, src, gcol, rs, ALU.mult, ALU.mult, [psk(bank), lk], [("qT", 0)])
                                else:
                                    STT(ph, qf[:], ps[bank][:], gcol, lr[:], ALU.mult, ALU.mult,
                                        [psk(bank), lk], ["qf"])
                                    CP(ph, kT[:, h, 0:512], qf[:], ["qf"], [("kT", 0)])
                                    for s4 in range(4):
                                        TR(ph, ps[5][:, 128 * s4:128 * s4 + 128], qf[:, 128 * s4:128 * s4 + 128],
                                           ident[:], ["qf", "ident"], ["ps5"])
                                    ACT(ph, kst[:, :, :],
                                        ps[5][:, :].rearrange("p (s f) -> p s f", s=4), AF.Copy, ["ps5"], ["kst"])
                                    for s4 in range(4):
                                        for q in range(2):
                                            ph.dma("sp", nk[q, l].rearrange("(j s) f -> s j f", s=4)[s4, :, 128 * h:128 * h + 128],
                                                   kst[64 * q:64 * q + 64, s4, :], reads=["kst"])
                            else:
                                sc = slice(512 * (ct - 1), 512 * (ct - 1) + 512)
                                CP(ph, qb[:], ps[bank][:], [psk(bank)], [qbk])
                                MM(ph, ps[5][:], Rq[:, l, isk, :], qb[:], True, True, [qbk, "Rq"], ["ps5"])
                                STT(ph, tA[:], ps[bank][:], gcol, cosT[:, sc], ALU.mult, ALU.mult, [psk(bank), "cos"], [tAk])
                                TT(ph, tB[:], ps[5][:], sinT[:, sc], ALU.mult, ["ps5", "sin"], [tBk])
                                TT(ph, tA[:], tA[:], tB[:], ALU.add, [tAk, tBk], [tAk])
                                dst = (kT if isk else qT)[:, h, cols]
                                TT(ph, dst, tA[:], lr[:], ALU.mult, [tAk, lk], [("kT" if isk else "qT", ct)])
                        return run

                    pending = None
                    n_ = 0
                    for ct in range(3):
                        cols = slice(512 * ct, 512 * ct + 512)
                        for mo in range(8):
                            bank = 2 + n_ % 2
                            sqt = sq[n_ % 2]
                            sqk = f"sq{n_ % 2}"
                            for kc in range(8):
                                MM(ph, ps[bank][:], win[:, kc, 128 * mo:128 * mo + 128], hmod[:, kc, cols],
                                   kc == 0, kc == 7, ["win", ("hmod", ct)], [psk(bank)])
                            ACT(ph, sqt[:], ps[bank][:], AF.Square, [psk(bank)], [sqk])
                            if pending is not None:
                                pending()
                            pending = qk_post(ct, mo, bank, sqt, sqk, n_ % 2)
                            n_ += 1
                    pending()
                    for tt in range(12):
                        bank = 2 + tt % 2
                        for kc in range(8):
                            MM(ph, ps[bank][:], hmod[:, kc, 128 * tt:128 * tt + 128], win[:, kc, 1024:1536],
                               kc == 0, kc == 7, ["win", ("hmod", tt // 4)], [psk(bank)])
                        CP(ph, Vtok[:, tt, :], ps[bank][:], [psk(bank)], [("V", tt)])
                        if tt < 4:
                            ACT(ph, vst[:], ps[bank][:], AF.Copy, [psk(bank)], ["qf"])
                            for q in range(2):
                                ph.dma("sp", nv[q, l].rearrange("(j s) f -> s j f", s=4)[tt],
                                       vst[64 * q:64 * q + 64, :], reads=["qf"])
                    for gp in range(16):
                        bank = 6 + gp % 2
                        for s4 in range(4):
                            for kc in range(8):
                                lw = win[:, kc, 1536 + 32 * gp:1536 + 32 * gp + 32]
                                MM(ph, ps[bank][32 * s4:32 * s4 + 32, 0:128], lw, hmod[:, kc, 128 * s4:128 * s4 + 128],
                                   kc == 0, kc == 7, ["win", ("hmod", 0)], [psk(bank)], tp=(0, 32 * s4))
                            for kc in range(8):
                                lw = win[:, kc, 1536 + 32 * gp:1536 + 32 * gp + 32]
                                MM(ph, ps[bank][32 * s4:32 * s4 + 32, 128:384], lw,
                                   hmod[:, kc, 512 + 256 * s4:512 + 256 * s4 + 256],
                                   kc == 0, kc == 7, ["win", ("hmod", 1), ("hmod", 2)], [psk(bank)], tp=(0, 32 * s4))
                        if gp % 2 == 0:
                            CP(ph, U4[:, gp, :], ps[bank][:, 0:384], [psk(bank)], ["U4"])
                        else:
                            ACT(ph, U4[:, gp, :], ps[bank][:, 0:384], AF.Copy, [psk(bank)], ["U4"])
                    for q in range(2):
                        for e_ in range(2):
                            col = 64 * q if e_ == 0 else 384 + 64 * q + 63
                            CP(ph, hrep, bc(hmod[:, :, col:col + 1], [128, 8, 128]), [("hmod", 0)], ["tmpB0"])
                            for kc in range(8):
                                MM(ph, ps[4][:], hrep[:, kc, :], win[:, kc, 1536:2048], kc == 0, kc == 7,
                                   ["tmpB0", "win"], ["ps4"])
                            CP(ph, uedge[:, 2 * q + e_, :], ps[4][:], ["ps4"], ["uedge"])
                    ph.emit()
                    checkpoint(f"A{l}")

                with contextlib.ExitStack() as sB:
                    def sbb(name, shape, dt=F32):
                        return sB.enter_context(nc.sbuf_tensor(f"{name}_{l}", list(shape), dt))
                    Et = [sbb(f"E{i}", [128, 512], BF16) for i in range(8)]
                    Esum = [sbb(f"Esum{i}", [128, 512]) for i in range(2)]
                    r0 = sbb("r0", [128, 512]); r1 = sbb("r1", [128, 512])
                    Osb = [sbb(f"Osb{i}", [128, 512]) for i in range(4)]
                    if l == 0:
                        wmB = sbb("wmB", [128, 8, 512], BF16)
                    oa = sbb("oa", [128, 512]); ob = sbb("ob", [128, 512])
                    sqb = sbb("sqb", [128, 512], BF16)
                    lnv = sbb("lnvB", [128, 512]); rstd = sbb("rstdB", [128, 512])
                    ph = Phase(nc, f"B{l}")
                    ecnt = [0]
                    BRE_v = hmod[:, 4, 0:1024].bitcast(F32).rearrange("p (a c) -> p a c", c=16)
                    BIM_v = hmod[:, 5, 0:1024].bitcast(F32).rearrange("p (a c) -> p a c", c=16)
                    ph.dma("sp", BRE_v, bre_d[l].rearrange("d (gp g2) p c -> (g2 p) (d gp) c", g2=2), writes=["BREv"])
                    ph.dma("act", BIM_v, bim_d[l].rearrange("d (gp g2) p c -> (g2 p) (d gp) c", g2=2), writes=["BIMv"])

                    def attn_unit(h, qcols, N, keychunks, outdst, outkey):
                        nkc = len(keychunks)
                        tiles = [(ci, s) for ci in range(nkc) for s in range(2)]
                        slots = {}

                        def issue_score(t):
                            ci, s = tiles[t]
                            kcol = keychunks[ci][0]
                            sbank = (0, 1, 2, 5, 7)[ecnt[0] % 5]
                            ei = ecnt[0] % 8
                            ecnt[0] += 1
                            slots[t] = ei
                            MM(ph, ps[sbank][:, 0:N], kT[64 * s:64 * s + 64, h, kcol:kcol + 128],
                               qT[64 * s:64 * s + 64, h, qcols:qcols + N], True, True, [], [psk(sbank)])
                            ACT(ph, Et[ei][:, 0:N], ps[sbank][:, 0:N], AF.Exp, [psk(sbank)], [f"E{ei}"], scale=0.125)

                        def issue_av(t):
                            ci, s = tiles[t]
                            (kcol, vt, plo, phi) = keychunks[ci]
                            ei = slots[t]
                            MM(ph, ps[4 + 2 * s][:, 0:N], Vtok[plo:phi, vt, 128 * h:128 * h + 128],
                               Et[ei][plo:phi, 0:N], ci == 0, ci == nkc - 1, [f"E{ei}"], [psk(4 + 2 * s)])
                            if ci == 0:
                                CP(ph, Esum[s][:, 0:N], Et[ei][:, 0:N], [f"E{ei}"], [f"Esum{s}"])
                            else:
                                TT(ph, Esum[s][:, 0:N], Esum[s][:, 0:N], Et[ei][:, 0:N], ALU.add,
                                   [f"E{ei}", f"Esum{s}"], [f"Esum{s}"])
                        LOOK = 4
                        for t in range(len(tiles) + LOOK):
                            if t < len(tiles):
                                issue_score(t)
                            if t >= LOOK:
                                issue_av(t - LOOK)
                        def part1():
                          ACT(ph, Osb[0][:, 0:N], ps[4][:, 0:N], AF.Copy, ["ps4"], ["Os0"])
                          ACT(ph, Osb[2][:, 0:N], ps[6][:, 0:N], AF.Copy, ["ps6"], ["Os2"])
                          plo_, phi_ = keychunks[0][2], keychunks[0][3]
                          lw = ones32[:] if phi_ - plo_ == 128 else ones32q[:, plo_ // 64, :]
                          for s_, rr in ((0, r0), (1, r1)):
                              MM(ph, ps[3][:, 0:N], lw, Esum[s_][:, 0:N], True, True, [f"Esum{s_}"], ["ps3"])
                              ACT(ph, rr[:, 0:N], ps[3][:, 0:N], AF.Ln, ["ps3"], [f"r{s_}"])
                              ACT(ph, rr[:, 0:N], rr[:, 0:N], AF.Exp, [f"r{s_}"], [f"r{s_}"], scale=-1.0)
                          TT(ph, oa[:, 0:N], Osb[0][:, 0:N], r0[:, 0:N], ALU.mult, ["Os0", "r0"], ["oa"])
                          TT(ph, ob[:, 0:N], Osb[2][:, 0:N], r1[:, 0:N], ALU.mult, ["Os2", "r1"], ["ob"])
                          STT(ph, oa[:, 0:N], ob[:, 0:N], neglam[:, l:l + 1], oa[:, 0:N], ALU.mult, ALU.add,
                              ["oa", "ob"], ["oa"])

                        def tail():
                            ACT(ph, sqb[:, 0:N], oa[:, 0:N], AF.Square, ["oa"], ["sqb"])
                            MM(ph, ps[3][:, 0:N], onesV[:], sqb[:, 0:N], True, True, ["sqb"], ["ps3"])
                            ACT(ph, lnv[:, 0:N], ps[3][:, 0:N], AF.Ln, ["ps3"], ["lnv"], bias=epsc[:])
                            ACT(ph, rstd[:, 0:N], lnv[:, 0:N], AF.Exp, ["lnv"], ["rstd"], scale=-0.5)
                            src = oa[:, 0:N]
                            rs = rstd[:, 0:N]
                            if N == 256:
                                src = src.rearrange("p (s j) -> p s j", s=4)
                                rs = rs.rearrange("p (s j) -> p s j", s=4)
                            STT(ph, outdst, src, gsub[:, l:l + 1], rs, ALU.mult, ALU.mult, ["oa", "rstd"], [outkey])
                        return part1, tail

                    units = []
                    for h in range(4):
                        for q in range(2):
                            kch = [(128 * s4, s4, 64 * q, 64 * q + 64) for s4 in range(4)]
                            dst = hmod[:, h, 0:512].rearrange("p (s q j) -> p q s j", s=4, q=2)[:, q]
                            units.append((h, 256 * q, 256, kch, dst, ("mix", h, q)))
                        for qt in range(2):
                            kch = [(512 + 128 * i, 4 + i, 0, 128) for i in range(8)] + \
                                  [(1536 + 128 * i, 12 + i, 0, 128) for i in range(4)]
                            dst = hmod[:, h, 512 + 512 * qt:512 + 512 * qt + 512]
                            units.append((h, 512 + 512 * qt, 512, kch, dst, ("mix", h, 2 + qt)))
                    pending = None
                    mod_todo = list(MOD_REST) if l == 0 else []
                    mod_pend = None
                    for ui, u_ in enumerate(units):
                        p1_, t_ = attn_unit(*u_)
                        if pending is not None:
                            pending()
                        p1_()
                        pending = t_
                        for _ in range(2 if ui < 2 else 1):
                            if mod_pend is not None:
                                mod_pend()
                                mod_pend = None
                            if mod_todo:
                                ml_, mhb_ = mod_todo.pop(0)
                                mod_pend = mod_half(ph, ml_, mhb_, wmB, "wmB", 3)
                    pending()
                    if l == 0:
                        if mod_pend is not None:
                            mod_pend()
                        while mod_todo:
                            ml_, mhb_ = mod_todo.pop(0)
                            mod_half(ph, ml_, mhb_, wmB, "wmB", 3)()
                        mod_A(ph, 0, A2, 8, 32)
                        mod_A(ph, 1, A1, 0, 8)
                        mod_A(ph, 1, A2, 8, 32)
                    ph.emit()
                    checkpoint(f"B{l}")
            sC = contextlib.ExitStack()
            with sC:
                def sbc(name, shape, dt=F32):
                    return sC.enter_context(nc.sbuf_tensor(f"{name}_{l}", list(shape), dt))
                T4 = sbc("T4", [128, 32, 128], BF16)
                B4 = sbc("B4", [128, 32, 2, 128], BF16)
                C4 = sbc("C4", [128, 32, 2, 128], BF16)
                BB = sbc("BB", [128, 2, 32, 16])
                LP = sbc("LP", [128, 5, 2, 32])
                A1b = sbc("A1b", [128, 2, 32]); A2b = sbc("A2b", [128, 2, 32])
                H0 = sbc("H0", [128, 2, 32])
                with contextlib.ExitStack() as sC1:
                    def sc1(name, shape, dt=F32):
                        return sC1.enter_context(nc.sbuf_tensor(f"{name}_{l}", list(shape), dt))
                    BP = sc1("BP", [128, 32, 4, 2, 32], BF16)
                    CPd = sc1("CPd", [128, 32, 2, 32], BF16)
                    rowt = sc1("rowt", [32, 128]); rowt2 = sc1("rowt2", [32, 128]); rowt3 = sc1("rowt3", [32, 128])
                    ls2 = sc1("ls2", [32, 2]); h0t = sc1("h0t", [32, 256]); h0r = sc1("h0r", [32, 2, 128])
                    LRE = sc1("LRE", [128, 32]); LIM = sc1("LIM", [128, 32]); DEL = sc1("DEL", [128, 32])
                    ta = sc1("ta", [128, 32]); tb = sc1("tb", [128, 32]); tcc = sc1("tcc", [128, 32])
                    ea = sc1("ea", [128, 32]); sinb = sc1("sinb", [128, 32]); cosb = sc1("cosb", [128, 32])
                    cr = sc1("cr", [128, 32]); ci = sc1("ci", [128, 32])
                    BRE = hmod[:, 4, 0:1024].bitcast(F32).rearrange("p (a c) -> p a c", c=16)
                    BIM = hmod[:, 5, 0:1024].bitcast(F32).rearrange("p (a c) -> p a c", c=16)
                    CRE = sc1("CRE", [128, 32, 16]); CIM = sc1("CIM", [128, 32, 16])
                    rowsC = sc1("rowsC", [32, 2048])
                    w1 = sc1("w1", [128, 32, 16]); w2 = sc1("w2", [128, 32, 16]); w3 = sc1("w3", [128, 32, 16])
                    dvec = sc1("dvec", [128, 16])
                    v1 = sc1("v1", [128, 16, 16]); v2 = sc1("v2", [128, 16, 16]); v3 = sc1("v3", [128, 16, 16])
                    ph = Phase(nc, f"C1{l}")
                    MS(ph, BP[:], 0.0, ["BP"]); MS(ph, CPd[:], 0.0, ["CPd"]); MS(ph, C4[:], 0.0, ["C4"])
                    ph.dma("sp", rowt[:], lre_d[l], writes=["rowt"])
                    ph.dma("sp", rowt2[:], lim_d[l], writes=["rowt2"])
                    ph.dma("sp", ls2[:], lst_d[l], writes=["ls2"])
                    for j in range(4):
                        ph.dma("sp", dvec[32 * j:32 * j + 32, :], ssmd_d[l].rearrange("(gp r) -> r gp", r=32),
                               writes=["dvec"], allow_slow_non_contiguous=True)
                    ph.dma("sp", h0t[:], stt_in[l], writes=["h0t"])
                    TR(ph, ps[0][:, 0:32], rowt[:], ident[0:32, 0:32], ["rowt", "ident"], ["ps0"])
                    CP(ph, LRE[:], ps[0][:, 0:32], ["ps0"], ["LRE"])
                    TR(ph, ps[0][:, 32:64], rowt2[:], ident[0:32, 0:32], ["rowt2", "ident"], ["ps0b"])
                    CP(ph, LIM[:], ps[0][:, 32:64], ["ps0b"], ["LIM"])
                    CP(ph, rowt3[:].rearrange("r (g p) -> r g p", g=2), bc(ls2[:].unsqueeze(2), [32, 2, 64]),
                       ["ls2"], ["rowt3"])
                    TR(ph, ps[0][:, 64:96], rowt3[:], ident[0:32, 0:32], ["rowt3", "ident"], ["ps0c"])
                    ACT(ph, DEL[:], ps[0][:, 64:96], AF.Exp, ["ps0c"], ["DEL"])
                    CP(ph, h0r[:], h0t[:].rearrange("r (gp ri) -> r ri gp", ri=2), ["h0t"], ["h0r"])
                    for ri in range(2):
                        TR(ph, ps[1][:, 32 * ri:32 * ri + 32], h0r[:, ri, :], ident[0:32, 0:32], ["h0r", "ident"], ["ps1"])
                    CP(ph, H0[:], ps[1][:, 0:64].rearrange("p (r d) -> p r d", r=2), ["ps1"], ["H0"])
                    TT(ph, ta[:], LRE[:], DEL[:], ALU.mult, ["LRE", "DEL"], ["ta"])
                    TT(ph, tb[:], LIM[:], DEL[:], ALU.mult, ["LIM", "DEL"], ["tb"])
                    ACT(ph, ea[:], ta[:], AF.Exp, ["ta"], ["ea"])
                    def range_reduce(add):
                        TS(ph, tcc[:], tb[:], add, None, ALU.add, None, ["tb", "sinb"], ["tcc"])
                        for _ in range(4):
                            TS(ph, cr[:], tcc[:], 2 * math.pi, None, ALU.is_ge, None, ["tcc"], ["cr"])
                            STT(ph, tcc[:], cr[:], -2 * math.pi, tcc[:], ALU.mult, ALU.add, ["cr", "tcc"], ["tcc"])
                    range_reduce(math.pi)
                    ACT(ph, sinb[:], tcc[:], AF.Sin, ["tcc"], ["sinb"], bias=negpi[:])
                    range_reduce(1.5 * math.pi)
                    ACT(ph, cosb[:], tcc[:], AF.Sin, ["tcc"], ["cosb"], bias=negpi[:])
                    MS(ph, LP[:, 0, 0, :], 1.0, ["LP"]); MS(ph, LP[:, 0, 1, :], 0.0, ["LP"])
                    TT(ph, LP[:, 1, 0, :], ea[:], cosb[:], ALU.mult, ["ea", "cosb"], ["LP"])
                    TT(ph, LP[:, 1, 1, :], ea[:], sinb[:], ALU.mult, ["ea", "sinb"], ["LP"])
                    for k in range(2, 5):
                        TT(ph, ta[:], LP[:, k - 1, 0, :], LP[:, 1, 0, :], ALU.mult, ["LP"], ["ta"])
                        TT(ph, tb[:], LP[:, k - 1, 1, :], LP[:, 1, 1, :], ALU.mult, ["LP"], ["tb"])
                        TT(ph, LP[:, k, 0, :], ta[:], tb[:], ALU.subtract, ["ta", "tb"], ["LP"])
                        TT(ph, ta[:], LP[:, k - 1, 0, :], LP[:, 1, 1, :], ALU.mult, ["LP"], ["ta"])
                        TT(ph, tb[:], LP[:, k - 1, 1, :], LP[:, 1, 0, :], ALU.mult, ["LP"], ["tb"])
                        TT(ph, LP[:, k, 1, :], ta[:], tb[:], ALU.add, ["ta", "tb"], ["LP"])
                    TT(ph, ta[:], LRE[:], LRE[:], ALU.mult, ["LRE"], ["ta"])
                    TT(ph, tb[:], LIM[:], LIM[:], ALU.mult, ["LIM"], ["tb"])
                    TT(ph, ta[:], ta[:], tb[:], ALU.add, ["ta", "tb"], ["ta"])
                    ph.op("dve", lambda e: e.reciprocal(out=tcc[:], in_=ta[:]), ["ta"], ["tcc"])
                    TS(ph, ea[:], LP[:, 1, 0, :], -1.0, None, ALU.add, None, ["LP"], ["ea"])
                    TT(ph, ta[:], ea[:], LRE[:], ALU.mult, ["ea", "LRE"], ["ta"])
                    TT(ph, tb[:], LP[:, 1, 1, :], LIM[:], ALU.mult, ["LP", "LIM"], ["tb"])
                    TT(ph, ta[:], ta[:], tb[:], ALU.add, ["ta", "tb"], ["ta"])
                    TT(ph, cr[:], ta[:], tcc[:], ALU.mult, ["ta", "tcc"], ["cr"])
                    TT(ph, ta[:], LP[:, 1, 1, :], LRE[:], ALU.mult, ["LP", "LRE"], ["ta"])
                    TT(ph, tb[:], ea[:], LIM[:], ALU.mult, ["ea", "LIM"], ["tb"])
                    TT(ph, ta[:], ta[:], tb[:], ALU.subtract, ["ta", "tb"], ["ta"])
                    TT(ph, ci[:], ta[:], tcc[:], ALU.mult, ["ta", "tcc"], ["ci"])
                    crb = bc(cr[:].unsqueeze(2), [128, 32, 16]); cib = bc(ci[:].unsqueeze(2), [128, 32, 16])
                    TT(ph, w1[:], BRE[:], crb, ALU.mult, ["BRE", "cr"], ["w1"])
                    TT(ph, w2[:], BIM[:], cib, ALU.mult, ["BIM", "ci"], ["w2"])
                    TT(ph, BB[:, 0], w1[:], w2[:], ALU.subtract, ["w1", "w2"], ["BB"])
                    TT(ph, w1[:], BIM[:], crb, ALU.mult, ["BIM", "cr"], ["w1"])
                    TT(ph, w2[:], BRE[:], cib, ALU.mult, ["BRE", "ci"], ["w2"])
                    TT(ph, BB[:, 1], w1[:], w2[:], ALU.add, ["w1", "w2"], ["BB"])
                    for ti_, (src_d, dstC, nm) in enumerate(((cre_d, CRE, "CRE"), (cim_d, CIM, "CIM"))):
                        srcv = src_d[l].rearrange("d (gp g2) c p -> (d gp) c g2 p", g2=2)
                        for g2 in range(2):
                            ph.dma("sp" if g2 == 0 else "act",
                                   rowsC[:].rearrange("r (c g p) -> r c g p", c=16, g=2)[:, :, g2, :], srcv[:, :, g2, :],
                                   reads=["rowsCr"], writes=["rowsC"])
                        for c_ in range(16):
                            TR(ph, ps[2][:, 32 * c_:32 * c_ + 32], rowsC[:, 128 * c_:128 * c_ + 128], ident[0:32, 0:32],
                               ["rowsC", "ident"], ["ps2"])
                        CP(ph, dstC[:], ps[2][:, :].rearrange("p (c a) -> p a c", c=16), ["ps2"], [nm, "rowsCr"])

                    def padcopy(dst_of_half, src, rkeys, wkey):
                        for g2 in range(2):
                            CP(ph, dst_of_half(g2), src[64 * g2:64 * g2 + 64], rkeys, [wkey])

                    for k in range(4):
                        if k == 0:
                            res = (BB[:, 0], BB[:, 1])
                            rk = ["BB"]
                        else:
                            lr = bc(LP[:, k, 0, :].unsqueeze(2), [128, 32, 16])
                            li = bc(LP[:, k, 1, :].unsqueeze(2), [128, 32, 16])
                            TT(ph, w1[:], BB[:, 0], lr, ALU.mult, ["BB", "LP"], ["w1"])
                            TT(ph, w2[:], BB[:, 1], li, ALU.mult, ["BB", "LP"], ["w2"])
                            TT(ph, w1[:], w1[:], w2[:], ALU.subtract, ["w1", "w2"], ["w1"])
                            TT(ph, w2[:], BB[:, 1], lr, ALU.mult, ["BB", "LP"], ["w2"])
                            TT(ph, w3[:], BB[:, 0], li, ALU.mult, ["BB", "LP"], ["w3"])
                            TT(ph, w2[:], w2[:], w3[:], ALU.add, ["w2", "w3"], ["w2"])
                            res = (w1[:], w2[:])
                            rk = ["w1", "w2"]
                        for ri in range(2):
                            padcopy(lambda g2, k=k, ri=ri: BP[64 * g2:64 * g2 + 64, :, k, ri, 16 * g2:16 * g2 + 16],
                                    res[ri], rk, "BP")
                    padcopy(lambda g2: CPd[64 * g2:64 * g2 + 64, :, 0, 16 * g2:16 * g2 + 16], CRE[:], ["CRE"], "CPd")
                    TS(ph, w3[:], CIM[:], -1.0, None, ALU.mult, None, ["CIM"], ["w3"])
                    padcopy(lambda g2: CPd[64 * g2:64 * g2 + 64, :, 1, 16 * g2:16 * g2 + 16], w3[:], ["w3"], "CPd")
                    for d in range(2):
                        dgs = slice(16 * d, 16 * d + 16)
                        for t4 in range(4):
                            k = t4 + 1 if d == 0 else 4 - t4
                            lr = bc(LP[:, k, 0, dgs].unsqueeze(2), [128, 16, 16])
                            li = bc(LP[:, k, 1, dgs].unsqueeze(2), [128, 16, 16])
                            TT(ph, v1[:, 0:16], CRE[:, dgs], lr, ALU.mult, ["CRE", "LP"], ["v1"])
                            TT(ph, v2[:, 0:16], CIM[:, dgs], li, ALU.mult, ["CIM", "LP"], ["v2"])
                            TT(ph, v1[:, 0:16], v1[:, 0:16], v2[:, 0:16], ALU.subtract, ["v1", "v2"], ["v1"])
                            TT(ph, v2[:, 0:16], CRE[:, dgs], li, ALU.mult, ["CRE", "LP"], ["v2"])
                            TT(ph, v3[:, 0:16], CIM[:, dgs], lr, ALU.mult, ["CIM", "LP"], ["v3"])
                            TT(ph, v2[:, 0:16], v2[:, 0:16], v3[:, 0:16], ALU.add, ["v2", "v3"], ["v2"])
                            TS(ph, v2[:, 0:16], v2[:, 0:16], -1.0, None, ALU.mult, None, ["v2"], ["v2"])
                            for ri, srcw in enumerate((v1, v2)):
                                for g2 in range(2):
                                    CP(ph, C4[64 * g2:64 * g2 + 64, dgs, ri, 32 * t4 + 16 * g2:32 * t4 + 16 * g2 + 16],
                                       srcw[64 * g2:64 * g2 + 64, 0:16], ["v1", "v2"], ["C4"])
                    for dg in range(32):
                        d = dg // 16
                        gp = dg % 16
                        bank = 4 + dg % 2
                        for ri in range(2):
                            for s4 in range(4):
                                k = 3 - s4 if d == 0 else s4
                                MM(ph, ps[bank][32 * s4:32 * s4 + 32, 128 * ri:128 * ri + 128], BP[:, dg, k, ri, :],
                                   identb[:], True, True, ["BP", "identb"], [psk(bank)], tp=(0, 32 * s4))
                        MM(ph, ps[bank][:, 256:384], zerob[:], identb[:], True, True, ["zerob", "identb"], [psk(bank)])
                        for s4 in range(4):
                            for t4 in range(4):
                                tau = t4 - s4 if d == 0 else s4 - t4
                                if tau < 0:
                                    continue
                                for ri in range(2):
                                    ph.op("pe", lambda e, bank=bank, s4=s4, t4=t4, dg=dg, tau=tau, ri=ri: e.matmul(
                                        ps[bank][32 * s4:32 * s4 + 32, 256 + 32 * t4:256 + 32 * t4 + 32],
                                        BP[:, dg, tau, ri, :], CPd[:, dg, ri, :], start=False, stop=True,
                                        tile_position=(0, 32 * s4), skip_group_check=True),
                                        ["BP", "CPd"], [psk(bank)])
                        ACT(ph, B4[:, dg, :, :], ps[bank][:, 0:256].rearrange("p (r m) -> p r m", r=2), AF.Copy,
                            [psk(bank)], ["B4"])
                        if d == 0:
                            STT(ph, T4[:, dg, :], ident[:], dvec[:, gp:gp + 1], ps[bank][:, 256:384], ALU.mult, ALU.add,
                                [psk(bank), "dvec", "ident"], ["T4"])
                        else:
                            CP(ph, T4[:, dg, :], ps[bank][:, 256:384], [psk(bank)], ["T4"])
                    CP(ph, A1b[:, 0, :], LP[:, 4, 0, :], ["LP"], ["A1b"]); CP(ph, A1b[:, 1, :], LP[:, 4, 0, :], ["LP"], ["A1b"])
                    TS(ph, A2b[:, 0, :], LP[:, 4, 1, :], -1.0, None, ALU.mult, None, ["LP"], ["A2b"])
                    CP(ph, A2b[:, 1, :], LP[:, 4, 1, :], ["LP"], ["A2b"])
                    ph.emit()
                    checkpoint(f"C1{l}")

                with contextlib.ExitStack() as sC2:
                    def sc2(name, shape, dt=F32):
                        return sC2.enter_context(nc.sbuf_tensor(f"{name}_{l}", list(shape), dt))
                    VZ = sc2("VZ", [128, 2, 32, 384], BF16)
                    with contextlib.ExitStack() as sC2a:
                        hE = sC2a.enter_context(nc.sbuf_tensor(f"hE_{l}", [128, 2, 2, 32], F32))
                        pr = sC2a.enter_context(nc.sbuf_tensor(f"pr_{l}", [128, 16, 16], F32))
                        hst = sC2a.enter_context(nc.sbuf_tensor(f"hst_{l}", [32, 2, 128, 2], F32))
                        ph = Phase(nc, f"C2{l}")
                        for dg in range(32):
                            gp = dg % 16
                            for ri in range(2):
                                bank = (2 * dg + ri) % 4
                                MM(ph, ps[bank][:, 0:384], B4[:, dg, ri, :], U4[:, gp, :], True, True, [], [psk(bank)])
                                if ri == 0:
                                    CP(ph, VZ[:, ri, dg, :], ps[bank][:, 0:384], [psk(bank)], [])
                                else:
                                    ACT(ph, VZ[:, ri, dg, :], ps[bank][:, 0:384], AF.Copy, [psk(bank)], [])
                        for q in range(2):
                            for e_ in range(2):
                                for ri in range(2):
                                    for g2 in range(2):
                                        hs = slice(64 * g2, 64 * g2 + 64)
                                        uev = uedge[hs, 2 * q + e_, :].rearrange("p (gp g c) -> p gp g c", g=2, c=16)[:, :, g2, :]
                                        TT(ph, pr[hs], BB[hs, ri, 16 * e_:16 * e_ + 16, :], uev, ALU.mult, [], ["pr"])
                                        ph.op("dve", lambda e, hs=hs, q=q, ri=ri, e_=e_: e.tensor_reduce(
                                            out=hE[hs, q, ri, 16 * e_:16 * e_ + 16], in_=pr[hs], axis=AX.X, op=ALU.add),
                                            ["pr"], ["hE"])
                        for q in range(2):
                            for ri in range(2):
                                TR(ph, ps[4][0:32, 128 * ri:128 * ri + 128], hE[:, q, ri, :], ident[:], ["hE", "ident"], ["ps4"])
                            CP(ph, hst[:, q], ps[4][0:32, 0:256].rearrange("r (ri gp) -> r gp ri", ri=2), ["ps4"], ["hst"])
                            ph.dma("sp", nssm[q, l], hst[:, q].rearrange("r gp ri -> r (gp ri)"), reads=["hst"])
                        ph.emit()
                        checkpoint(f"C2{l}")
                    with contextlib.ExitStack() as sC3:
                        def sc3(name, shape, dt=F32):
                            return sC3.enter_context(nc.sbuf_tensor(f"{name}_{l}", list(shape), dt))
                        zS = [sc3(f"zS{i}", [128, 3, 32]) for i in range(3)]
                        tS1 = sc3("tS1", [128, 2, 32]); tS2 = sc3("tS2", [128, 2, 32])
                        zP = {(d, i): sc3(f"zP{d}{i}", [128, 3, 16, 2]) for d in range(2) for i in range(3)}
                        tP1 = [sc3(f"tP1{d}", [128, 2, 16, 2]) for d in range(2)]
                        tP2 = [sc3(f"tP2{d}", [128, 2, 16, 2]) for d in range(2)]
                        ph = Phase(nc, f"C3{l}")
                        CP(ph, zS[0][:, 0:2, :], H0[:], [], ["zS0"])
                        for d in range(2):
                            MS(ph, zP[(d, 0)][:], 0.0, [f"zP{d}0"], eng="pool")
                        vz0 = VZ[:, 0:1, 0:1, 0:1]
                        pstep = vz0.ap[0][0]

                        def vz_merged(jf, jb_):
                            return bass.AP(vz0.tensor, vz0.offset + jf,
                                           [[pstep, 128], [32 * 384, 2], [16 * 384 + (jb_ - jf), 2], [384, 16]])
                        NBS = 256
                        for i in range(NBS):
                            vj = vz_merged(128 + i, 128 + NBS - 1 - i)
                            zc = zS[i % 3]; zn = zS[(i + 1) % 3]
                            kc_ = f"zS{i % 3}"; kn_ = f"zS{(i + 1) % 3}"
                            TT(ph, tS1[:], zc[:, 0:2, :], A1b[:], ALU.mult, [kc_], ["tS1"])
                            zv = zc[:, 1:2, :]
                            zsw = bass.AP(zv.tensor, zv.offset, [[zv.ap[0][0], 128], [-32, 2], [1, 32]])
                            TT(ph, tS2[:], zsw, A2b[:], ALU.mult, [kc_], ["tS2"])
                            TT(ph, tS1[:], tS1[:], tS2[:], ALU.add, ["tS1", "tS2"], ["tS1"])
                            TT(ph, zn[:, 0:2, :].rearrange("p r (d g) -> p r d g", d=2),
                               tS1[:].rearrange("p r (d g) -> p r d g", d=2), vj, ALU.add,
                               ["tS1", ("vzS", i)], [kn_])
                            ACT(ph, vj, zc[:, 0:2, :].rearrange("p r (d g) -> p r d g", d=2), AF.Copy,
                                [kc_], [("vzS", i)])
                            if i < 64:
                                for d in range(2):
                                    dgs = slice(16 * d, 16 * d + 16)
                                    j = i if d == 0 else 63 - i
                                    pc = zP[(d, i % 3)]; pn = zP[(d, (i + 1) % 3)]
                                    pk = f"zP{d}{i % 3}"; pkn = f"zP{d}{(i + 1) % 3}"
                                    vp = VZ[:, :, dgs, j:j + 65:64]
                                    a1 = bc(A1b[:, :, dgs].unsqueeze(3), [128, 2, 16, 2])
                                    a2 = bc(A2b[:, :, dgs].unsqueeze(3), [128, 2, 16, 2])
                                    TT(ph, tP1[d][:], pc[:, 0:2], a1, ALU.mult, [pk], [f"tP1{d}"], eng="pool")
                                    TT(ph, tP2[d][:], pc[:, 1:3], a2, ALU.mult, [pk], [f"tP2{d}"], eng="pool")
                                    TT(ph, tP1[d][:], tP1[d][:], tP2[d][:], ALU.add, [f"tP1{d}", f"tP2{d}"],
                                       [f"tP1{d}"], eng="pool")
                                    TT(ph, pn[:, 0:2], tP1[d][:], vp, ALU.add, [f"tP1{d}", ("vzP", d, j)], [pkn],
                                       eng="pool")
                                    CP(ph, pn[:, 2], pn[:, 0], [pkn], [pkn], eng="pool")
                                    CP(ph, vp, pc[:, 0:2], [pk], [("vzP", d, j)], eng="pool")
                        ph.emit()
                        checkpoint(f"C3{l}")
                    with contextlib.ExitStack() as sC4:
                        def sc4(name, shape, dt=F32):
                            return sC4.enter_context(nc.sbuf_tensor(f"{name}_{l}", list(shape), dt))
                        zf = B4[:, 0:16].rearrange("p a r m -> p (a r m)").bitcast(F32).rearrange("p (c n) -> p c n", c=4)
                        zb = B4[:, 16:24].rearrange("p a r m -> p (a r m)").rearrange("p (c n) -> p c n", c=4)
                        wg = B4[:, 24:32].rearrange("p a r m -> p (a r m)").rearrange("p (c n) -> p c n", c=4)
                        sg = sc4("sg", [128, 512])
                        ph = Phase(nc, f"C4{l}")
                        for kc in range(4):
                            ph.dma("pool", wg[:, kc, :], w_glu[l, 128 * kc:128 * kc + 128, :], writes=["wg"])
                        for ct in range(3):
                            cols = slice(512 * ct, 512 * ct + 512)
                            for ch in range(4):
                                bank = ch
                                for g4 in range(4):
                                    gp = 4 * ch + g4
                                    if ct == 0:
                                        regions = [(t4, 128 * t4, 128, slice(0, 128)) for t4 in range(4)]
                                    else:
                                        regions = [(t4, 256 * (t4 % 2), 256, slice(128, 384)) for t4 in
                                                   (2 * (ct - 1), 2 * (ct - 1) + 1)]
                                    for (t4, off, N, ucols) in regions:
                                        outap = ps[bank][32 * g4:32 * g4 + 32, off:off + N]
                                        n = 0
                                        for d in range(2):
                                            dg = 16 * d + gp
                                            for (lw, rh) in ((T4[:, dg, 32 * t4:32 * t4 + 32], U4[:, gp, ucols]),
                                                             (C4[:, dg, 0, 32 * t4:32 * t4 + 32], VZ[:, 0, dg, ucols]),
                                                             (C4[:, dg, 1, 32 * t4:32 * t4 + 32], VZ[:, 1, dg, ucols])):
                                                MM(ph, outap, lw, rh, n == 0, n == 5, [], [psk(bank)], tp=(0, 32 * g4))
                                                n += 1
                                ACT(ph, zf[:, ch, :], ps[bank][:], AF.Gelu_apprx_tanh, [psk(bank)], [("zf", ch)])
                                CP(ph, zb[:, ch, :], zf[:, ch, :], [("zf", ch)], [("zb", ch)])
                            for m in range(4):
                                bank = 4 + m % 2
                                for kc in range(4):
                                    MM(ph, ps[bank][:], wg[:, kc, 128 * m:128 * m + 128], zb[:, kc, :], kc == 0, kc == 3,
                                       ["wg"] + [("zb", kc)], [psk(bank)])
                                ACT(ph, sg[:], ps[bank][:], AF.Sigmoid, [psk(bank)], ["sg"], bias=VT[:, l, 96 + m:97 + m])
                                TT(ph, hmod[:, 4 + m, cols], zf[:, m, :], sg[:], ALU.mult, [("zf", m), "sg"], [("mix2", m, ct)])
                        ph.emit()
                        checkpoint(f"C4{l}")
            sL.__exit__(None, None, None)

            with contextlib.ExitStack() as sD:
                wo = sD.enter_context(nc.sbuf_tensor(f"wo_{l}", [128, 8, 1024], BF16))
                ph = Phase(nc, f"D{l}")
                for kc in range(8):
                    ph.dma("pool", wo[:, kc, :], w_out[l, 128 * kc:128 * kc + 128, :], writes=["wo"])
                for ct in range(3):
                    cd = 0 if ct == 0 else 1
                    cols = slice(512 * ct, 512 * ct + 512)
                    for m in range(8):
                        bank = m % 4
                        for kc in range(8):
                            MM(ph, ps[bank][:], wo[:, kc, 128 * m:128 * m + 128], hmod[:, kc, cols], kc == 0, kc == 7,
                               ["wo"], [psk(bank)])
                        STT(ph, X[:, m, cols], ps[bank][:], modsb[:, l, 16 + m, cd:cd + 1], X[:, m, cols],
                            ALU.mult, ALU.add, [psk(bank)], [("X", m, ct)])
                ph.emit()
                checkpoint(f"D{l}")

            with contextlib.ExitStack() as sE:
                gact = sE.enter_context(nc.sbuf_tensor(f"gact_{l}", [128, 16, NCOL], BF16))
                with contextlib.ExitStack() as sE1:
                    def se1(name, shape, dt=F32):
                        return sE1.enter_context(nc.sbuf_tensor(f"{name}_{l}", list(shape), dt))
                    sq = [se1(f"sqE{i}", [128, 512], BF16) for i in range(2)]
                    tmp = [se1(f"tmpE{i}", [128, 512]) for i in range(2)]
                    lnv = se1("lnvE", [128, 512]); rstd = se1("rstdE", [128, 512])
                    wu = [se1(f"wu{i}", [128, 8, 1024], BF16) for i in range(2)]
                    zzs = [[se1(f"zz{a}{i}", [128, NCOL]) for i in range(2)] for a in range(2)]
                    ph = Phase(nc, f"E{l}")
                    norm_mod(ph, l, A2, 24, sq, lnv, rstd, tmp)
                    wupv = w_up[l].rearrange("(kc p) f -> p kc f", p=128)
                    for j in range(16):
                        qd, jj = j // 4, j % 4
                        wb = wu[qd % 2]
                        wk = f"wu{qd % 2}"
                        if jj == 0:
                            for qn in ([0, 1] if qd == 0 else ([qd + 1] if qd + 1 < 4 else [])):
                                wbn = wu[qn % 2]
                                wkn = f"wu{qn % 2}"
                                ph.dma("pool", wbn[:, :, 0:512], wupv[:, :, 512 * qn:512 * qn + 512],
                                       reads=[wkn + "r"], writes=[wkn])
                                ph.dma("pool", wbn[:, :, 512:1024], wupv[:, :, 2048 + 512 * qn:2048 + 512 * qn + 512],
                                       reads=[wkn + "r"], writes=[wkn])
                        zz = zzs[j % 2]
                        for hf in range(2):
                            f = j + 16 * hf
                            z = zz[hf]
                            zk = f"zz{j % 2}{hf}"
                            bks = [3 * hf + ct for ct in range(3)]
                            for ct in range(3):
                                for kc in range(8):
                                    MM(ph, ps[bks[ct]][:], wb[:, kc, 512 * hf + 128 * jj:512 * hf + 128 * jj + 128],
                                       hmod[:, kc, 512 * ct:512 * ct + 512], kc == 0, kc == 7,
                                       [wk, ("hmod", ct)],
                                       [psk(bks[ct])] + ([wk + "r"] if (jj == 3 and hf == 1 and ct == 2 and kc == 7) else []))
                            w0 = VTB[:, l, f:f + 1]; w1c = VTB[:, l, 32 + f:33 + f]; w2c = VTB[:, l, 64 + f:65 + f]
                            bcol = VT[:, l, 64 + f:65 + f]
                            RK = {n_: (zk, n_) for n_ in ("A", "B", "C", "D", "E1", "E2", "F")}
                            ctk = [[RK["A"], RK["B"], RK["C"]], [RK["D"], RK["E1"]], [RK["E2"], RK["F"]]]
                            for ct in range(3):
                                ACT(ph, z[:, 512 * ct:512 * ct + 512], ps[bks[ct]][:], AF.Identity, [psk(bks[ct])], ctk[ct],
                                    scale=w1c, bias=bcol)
                            pP, pS1, pS2 = ps[bks[0]], ps[bks[1]], ps[bks[2]]

                            def tap(dst, src, wcol, pbank, regs):
                                ks = [RK[r_] for r_ in regs]
                                STT(ph, dst, src, wcol, dst, ALU.mult, ALU.add, [psk(pbank)] + ks, ks)
                            tap(z[:, 128:512], pP[:, 0:384], w0, bks[0], ["B", "C"])
                            tap(z[:, 768:1280], pS1[:, 0:512], w0, bks[1], ["E1", "E2"])
                            tap(z[:, 0:128].rearrange("p (q j) -> p q j", q=2)[:, :, 1:64],
                                pP[:, 384:512].rearrange("p (q j) -> p q j", q=2)[:, :, 0:63], w0, bks[0], ["A"])
                            tap(z[:, 1280:1536], pS2[:, 0:256], w0, bks[2], ["F"])
                            tap(z[:, 513:768], pS2[:, 256:511], w0, bks[2], ["D"])
                            tap(z[:, 0:384], pP[:, 128:512], w2c, bks[0], ["A", "B"])
                            tap(z[:, 768:1280], pS2[:, 0:512], w2c, bks[2], ["E1", "E2"])
                            tap(z[:, 384:512].rearrange("p (q j) -> p q j", q=2)[:, :, 0:63],
                                pP[:, 0:128].rearrange("p (q j) -> p q j", q=2)[:, :, 1:64], w2c, bks[0], ["C"])
                            tap(z[:, 512:768], pS1[:, 256:512], w2c, bks[1], ["D"])
                            tap(z[:, 1280:1535], pS1[:, 1:256], w2c, bks[1], ["F"])
                        allk = lambda zk_: [(zk_, n_) for n_ in ("A", "B", "C", "D", "E1", "E2", "F")]
                        ACT(ph, zz[1][:], zz[1][:], AF.Silu, allk(f"zz{j % 2}1"), allk(f"zz{j % 2}1"))
                        TT(ph, gact[:, j, :], zz[0][:], zz[1][:], ALU.mult, allk(f"zz{j % 2}0") + allk(f"zz{j % 2}1"),
                           [("gact", j)], eng="pool")
                    for kc in range(16):
                        wbn = wu[kc // 8]
                        wkn = f"wu{kc // 8}"
                        ph.dma("pool", wbn[:, kc % 8, :], w_down[l, 128 * kc:128 * kc + 128, :],
                               reads=[wkn + "r"], writes=[("wd", kc)])
                    for ct in range(3):
                        cd = 0 if ct == 0 else 1
                        cols = slice(512 * ct, 512 * ct + 512)
                        for m in range(8):
                            bank = 6 + m % 2
                            for kc in range(16):
                                MM(ph, ps[bank][:], wu[kc // 8][:, kc % 8, 128 * m:128 * m + 128], gact[:, kc, cols],
                                   kc == 0, kc == 15, [("wd", kc), ("gact", kc)], [psk(bank)])
                            STT(ph, X[:, m, cols], ps[bank][:], modsb[:, l, 40 + m, cd:cd + 1], X[:, m, cols],
                                ALU.mult, ALU.add, [psk(bank)], [("X", m, ct)])
                    ph.emit()
                    checkpoint(f"E{l}")
                    checkpoint(f"F{l}")

        with contextlib.ExitStack() as sZ:
            ot = [sZ.enter_context(nc.sbuf_tensor(f"ot{i}", [128, 1024], F32)) for i in range(2)]
            ph = Phase(nc, "Z")
            for tt in range(12):
                o = ot[tt % 2]
                ok_ = f"ot{tt % 2}"
                for half in range(2):
                    bank = (2 * tt + half) % 4
                    for c4 in range(4):
                        c = 4 * half + c4
                        TR(ph, ps[bank][:, 128 * c4:128 * c4 + 128], X[:, c, 128 * tt:128 * tt + 128], ident[:], [],
                           [psk(bank)])
                    if half == 0:
                        CP(ph, o[:, 0:512], ps[bank][:], [psk(bank)], [ok_])
                    else:
                        ACT(ph, o[:, 512:1024], ps[bank][:], AF.Copy, [psk(bank)], [ok_])
                if tt < 4:
                    for q in range(2):
                        ph.dma("sp", yp[q].rearrange("(j s) d -> s j d", s=4)[tt], o[64 * q:64 * q + 64, :], reads=[ok_])
                else:
                    s4, hf = (tt - 4) // 2, (tt - 4) % 2
                    ph.dma("sp", ys.rearrange("(j s) d -> s j d", s=4)[s4, 128 * hf:128 * hf + 128, :], o[:, :], reads=[ok_])
            ph.emit()


_CACHE = {}


def _consts():
    ident = np.eye(128, dtype=np.float32)
    n = np.arange(1024)
    tok = 4 * (n % 256) + n // 256
    row = (tok // 64).astype(np.float64)
    colp = (tok % 64).astype(np.float64)
    p = np.arange(128)
    dd = p % 64
    axis = dd // 32
    half = (dd % 32) // 16
    f = dd % 16
    inv = 1.0 / (10000.0 ** (np.arange(8, dtype=np.float32) / np.float32(8)))
    inv16 = np.concatenate([inv, inv])
    nf = 16 // 2
    invf = (1.0 / (np.float32(10000.0) ** (np.arange(16, dtype=np.float32) / np.float32(16)))).astype(np.float32)
    pos = np.where(axis[:, None] == 0, row[None, :], colp[None, :]).astype(np.float32)
    ang = pos * invf[f][:, None]
    cosT = np.cos(ang).astype(np.float32)
    sinT = np.sin(ang).astype(np.float32)
    rsign = np.zeros((128, 128), np.float32)
    for m in range(128):
        if half[m] == 0:
            rsign[m + 16, m] = -1.0
        else:
            rsign[m - 16, m] = 1.0
    return ident, cosT, sinT, rsign


def make_inmaps(inp):
    f32 = np.float32
    g = {k: np.ascontiguousarray(np.asarray(v, dtype=f32)) for k, v in inp.items()}
    ident, cosT, sinT, rsign = _consts()
    va = np.concatenate([g["g_norm1"].reshape(2, 8, 128), g["g_norm2"].reshape(2, 8, 128),
                         g["b_mod"].reshape(2, 48, 128), g["conv_b"].reshape(2, 32, 128),
                         g["b_glu"].reshape(2, 4, 128)], axis=1)
    vb = g["conv_w"].reshape(2, 96, 128)
    qkn = np.stack([g["q_norm"], g["k_norm"]], axis=1)
    lamv = np.stack([g["lambda_q1"], g["lambda_k1"], g["lambda_q2"], g["lambda_k2"]], axis=1)
    shared = dict(w_mod=g["w_mod"], w_in=g["w_in"], w_out=g["w_out"], w_glu=g["w_glu"], w_up=g["w_up"],
                  w_down=g["w_down"], va=np.ascontiguousarray(va), vb=np.ascontiguousarray(vb),
                  qkn=np.ascontiguousarray(qkn), lamv=np.ascontiguousarray(lamv), subg=g["subln_g"],
                  lre=g["ssm_lambda_re"].reshape(2, 32, 128), lim=g["ssm_lambda_im"].reshape(2, 32, 128),
                  lstep=g["ssm_log_step"].reshape(2, 32, 2), bre=g["ssm_b_re"], bim=g["ssm_b_im"],
                  cre=g["ssm_c_re"], cim=g["ssm_c_im"], ssmd=g["ssm_d"],
                  ident=ident, cosT=cosT, sinT=sinT, rsign=rsign)
    in_maps = []
    for c in range(8):
        m = dict(shared)
        m["xp"] = np.ascontiguousarray(g["x_prompt"][2 * c:2 * c + 2])
        m["xs"] = np.ascontiguousarray(g["x_sample"][c])
        m["ck"] = np.ascontiguousarray(g["cache_k"][c].reshape(2, 512, 512))
        m["cv"] = np.ascontiguousarray(g["cache_v"][c].reshape(2, 512, 512))
        m["st"] = np.ascontiguousarray(g["state_ssm"][c].reshape(2, 32, 256))
        m["cvec"] = np.ascontiguousarray(np.stack([g["c_ctx"], g["c"][c]], axis=0).reshape(16, 128))
        in_maps.append(m)
    return in_maps


def kernel(**inp):
    f32 = np.float32
    if "nc" not in _CACHE:
        _CACHE["nc"] = build_program()
    nc = _CACHE["nc"]
    in_maps = make_inmaps(inp)
    res = run_bass_kernel_spmd(nc, in_maps, core_ids=list(range(8)))
    R = res.results
    y_prompt = np.concatenate([R[c]["yp"] for c in range(8)], axis=0).astype(f32)
    y_sample = np.stack([R[c]["ys"] for c in range(8)], axis=0).astype(f32)
    new_k = np.concatenate([R[c]["nk"] for c in range(8)], axis=0).reshape(16, 2, 256, 4, 2, 64).astype(f32)
    new_v = np.concatenate([R[c]["nv"] for c in range(8)], axis=0).reshape(16, 2, 256, 4, 128).astype(f32)
    new_ssm = np.concatenate([R[c]["nssm"] for c in range(8)], axis=0).reshape(16, 2, 2, 32, 64, 2).astype(f32)
    return (y_prompt, y_sample, new_k, new_v, new_ssm)
```

```python
import contextlib
import math
import numpy as np
import concourse.bass as bass
import concourse.mybir as mybir
from concourse.bass_utils import run_bass_kernel_spmd

F32 = mybir.dt.float32
BF16 = mybir.dt.bfloat16
AF = mybir.ActivationFunctionType
ALU = mybir.AluOpType
AX = mybir.AxisListType

ENGS = ("pe", "act", "dve", "pool", "sp")
STRICT_SAME = True
EPS = 1e-6
NCOL = 1536
DEPTH = 2


class Phase:
    def __init__(self, nc, name):
        self.nc = nc
        self.name = name
        self.ops = []
        self.last_w = {}
        self.readers = {}

    def _add(self, eng, fn, reads, writes, is_dma):
        def canon(k):
            return k[:3] if isinstance(k, str) and len(k) >= 3 and k.startswith("ps") and k[2].isdigit() else k
        reads = tuple(canon(k) for k in reads)
        writes = tuple(canon(k) for k in writes)
        writes = writes + tuple(k for k in reads if isinstance(k, str) and len(k) == 3 and k.startswith("ps")
                                and k[2].isdigit() and k not in writes)
        idx = len(self.ops)
        deps = set()
        raw = set()
        for k in reads:
            if k in self.last_w:
                deps.add(self.last_w[k])
                if not (isinstance(k, str) and len(k) == 3 and k.startswith("ps") and k[2].isdigit()
                        and self.ops[self.last_w[k]]["eng"] != "pe"):
                    raw.add(self.last_w[k])
        for k in writes:
            if k in self.last_w:
                deps.add(self.last_w[k])
            for r in self.readers.get(k, ()):
                deps.add(r)
        fd = set()
        for d in deps:
            o = self.ops[d]
            if o["is_dma"] or o["eng"] != eng or is_dma:
                fd.add(d)
            elif STRICT_SAME and eng != "pe" and d in raw:
                fd.add(d)
        for d in fd:
            self.ops[d]["signal"] = True
        import sys as _s
        self.ops.append(dict(eng=eng, fn=fn, deps=sorted(fd), is_dma=is_dma, signal=False,
                             tag=_s._getframe(3).f_lineno if _DBG.get("trunc") else None))
        for k in writes:
            self.last_w[k] = idx
            self.readers[k] = []
        for k in reads:
            if k not in writes:
                self.readers.setdefault(k, []).append(idx)
        return idx

    def op(self, eng, fn, reads=(), writes=()):
        return self._add(eng, fn, tuple(reads), tuple(writes), False)

    def dma(self, eng, out, in_, reads=(), writes=(), **kw):
        return self._add(eng, lambda e: e.dma_start(out=out, in_=in_, **kw), tuple(reads), tuple(writes), True)

    def emit(self):
        nc = self.nc
        ops = self.ops
        tr = _DBG.get("trunc")
        if tr is not None and tr[0] == self.name:
            print("TRUNC", self.name, "total ops", len(ops), "->", tr[1])
            for i_, o_ in enumerate(ops[:tr[1]][-3:]):
                print("   last ops:", o_["eng"], o_.get("tag"))
            ops = ops[:tr[1]]
            self.ops = ops
        ndma = sum(1 for o in ops if o["is_dma"])
        with contextlib.ExitStack() as st:
            G = getattr(nc, "_phase_sems", None)
            if G is None:
                G = dict(esem={e: nc.alloc_semaphore(name=f"g_{e}") for e in ENGS},
                         dsem=[nc.alloc_semaphore(name=f"g_d{i}") for i in range(88)],
                         ecount={e: 0 for e in ENGS}, dcount=[0] * 88, nxt=0)
                nc._phase_sems = G
            esem = G["esem"]; dsem = G["dsem"]; dcount = G["dcount"]; ecount = G["ecount"]
            NP = len(dsem)
            assert ndma <= NP, (self.name, ndma)
            used = set()
            for o in ops:
                if o["is_dma"]:
                    k = G["nxt"] % NP
                    G["nxt"] += 1
                    o["sem"] = k
                    dcount[k] += 16
                    o["val"] = dcount[k]
                    used.add(k)
                elif o["signal"]:
                    ecount[o["eng"]] += 1
                    o["val"] = ecount[o["eng"]]
            block = st.enter_context(nc.Block())

            def body(ename):
                def f(e):
                    waited = {}
                    for o in ops:
                        if o["eng"] != ename:
                            continue
                        for d in o["deps"]:
                            p = ops[d]
                            if p["is_dma"]:
                                key = ("d", p["sem"])
                                sem = dsem[p["sem"]]
                            else:
                                key = ("e", p["eng"])
                                sem = esem[p["eng"]]
                            if waited.get(key, 0) >= p["val"]:
                                continue
                            waited[key] = p["val"]
                            e.wait_ge(sem, p["val"])
                        ins = o["fn"](e)
                        if o["is_dma"]:
                            ins.then_inc(dsem[o["sem"]], 16)
                        elif o["signal"]:
                            ins.then_inc(esem[ename], 1)
                    if ename == "sp":
                        for i in sorted(used):
                            if waited.get(("d", i), 0) < dcount[i]:
                                e.wait_ge(dsem[i], dcount[i])
                return f

            block.tensor(body("pe"))
            block.scalar(body("act"))
            block.vector(body("dve"))
            block.gpsimd(body("pool"))
            block.sync(body("sp"))


def ACT(ph, out, in_, func, r, w, **kw):
    ph.op("act", lambda e: e.activation(out=out, in_=in_, func=func, **kw), r, w)


def TT(ph, out, in0, in1, op, r, w, eng="dve"):
    ph.op(eng, lambda e: e.tensor_tensor(out=out, in0=in0, in1=in1, op=op), r, w)


def STT(ph, out, in0, scalar, in1, op0, op1, r, w, eng="dve"):
    ph.op(eng, lambda e: e.scalar_tensor_tensor(out=out, in0=in0, scalar=scalar, in1=in1, op0=op0, op1=op1), r, w)


def TS(ph, out, in0, s1, s2, op0, op1, r, w, eng="dve"):
    if s2 is None:
        ph.op(eng, lambda e: e.tensor_scalar(out=out, in0=in0, scalar1=s1, scalar2=None, op0=op0), r, w)
    else:
        ph.op(eng, lambda e: e.tensor_scalar(out=out, in0=in0, scalar1=s1, scalar2=s2, op0=op0, op1=op1), r, w)


def CP(ph, out, in_, r, w, eng="dve"):
    ph.op(eng, lambda e: e.tensor_copy(out=out, in_=in_), r, w)


def MM(ph, out, lhsT, rhs, start, stop, r, w, tp=None):
    if tp is None:
        ph.op("pe", lambda e: e.matmul(out, lhsT, rhs, start=start, stop=stop), r, w)
    else:
        ph.op("pe", lambda e: e.matmul(out, lhsT, rhs, start=start, stop=stop, tile_position=tp), r, w)


def TR(ph, out, in_, ident, r, w):
    ph.op("pe", lambda e: e.transpose(out, in_, ident), r, w)


def MS(ph, ap, val, w, eng="dve"):
    ph.op(eng, lambda e: e.memset(ap, val), (), w)


def bc(ap, shape):
    return ap.broadcast_to(shape)


class _Stop(Exception):
    pass


_DBG = {"stop": None, "trunc": None}


def build_program():
    nc = bass.Bass("TRN2", target_bir_lowering=False)
    try:
        _build(nc)
    except _Stop:
        pass
    return nc


def _build(nc):

    def din(name, shape):
        return nc.dram_tensor(name, list(shape), F32, kind="ExternalInput").ap()

    def dout(name, shape):
        return nc.dram_tensor(name, list(shape), F32, kind="ExternalOutput").ap()

    xp = din("xp", [2, 256, 1024]); xs = din("xs", [1024, 1024])
    ck = din("ck", [2, 512, 512]); cv = din("cv", [2, 512, 512])
    stt_in = din("st", [2, 32, 256]); cvec = din("cvec", [16, 128])
    w_mod = din("w_mod", [2, 1024, 6144]); w_in = din("w_in", [2, 1024, 2048])
    w_out = din("w_out", [2, 1024, 1024]); w_glu = din("w_glu", [2, 512, 512])
    w_up = din("w_up", [2, 1024, 4096]); w_down = din("w_down", [2, 2048, 1024])
    va = din("va", [2, 100, 128]); vb = din("vb", [2, 96, 128])
    qkn = din("qkn", [2, 2, 64]); lamv = din("lamv", [2, 4, 64]); subg = din("subg", [2, 128])
    lre_d = din("lre", [2, 32, 128]); lim_d = din("lim", [2, 32, 128]); lst_d = din("lstep", [2, 32, 2])
    bre_d = din("bre", [2, 2, 32, 64, 16]); bim_d = din("bim", [2, 2, 32, 64, 16])
    cre_d = din("cre", [2, 2, 32, 16, 64]); cim_d = din("cim", [2, 2, 32, 16, 64])
    ssmd_d = din("ssmd", [2, 512])
    ident_d = din("ident", [128, 128]); cos_d = din("cosT", [128, 1024]); sin_d = din("sinT", [128, 1024])
    rsign_d = din("rsign", [128, 128])
    yp = dout("yp", [2, 256, 1024]); ys = dout("ys", [1024, 1024])
    nk = dout("nk", [2, 2, 256, 512]); nv = dout("nv", [2, 2, 256, 512]); nssm = dout("nssm", [2, 2, 32, 256])
    if _DBG["stop"] is not None:
        dbgX = dout("dbgX", [128, 8 * NCOL]); dbgH = dout("dbgH", [128, 8 * NCOL])
        dbgM = dout("dbgM", [128, 2 * 48 * 2])

    es = contextlib.ExitStack()
    with es:
        def sb(name, shape, dt=F32):
            return es.enter_context(nc.sbuf_tensor("s_" + name, list(shape), dt))

        ps = [es.enter_context(nc.psum_tensor(f"ps{i}", [128, 512], F32)) for i in range(8)]
        X = sb("X", [128, 8, NCOL])
        ident = sb("ident", [128, 128]); identb = sb("identb", [128, 128], BF16)
        onesD = sb("onesD", [128, 128], BF16); blk64 = sb("blk64", [128, 128], BF16)
        ones1 = sb("ones1", [128, 128], BF16); onesV = sb("onesV", [128, 128], BF16)
        ones32 = sb("ones32", [128, 128]); zerob = sb("zerob", [128, 128], BF16)
        ones32q = sb("ones32q", [128, 2, 128])
        epsc = sb("epsc", [128, 1]); negpi = sb("negpi", [128, 1])
        cosT = sb("cosT", [128, 1024]); sinT = sb("sinT", [128, 1024]); rsign = sb("rsign", [128, 128])
        VT = sb("VT", [128, 2, 100]); VTB = sb("VTB", [128, 2, 96])
        scond = sb("scond", [128, 8, 2]); scondb = sb("scondb", [128, 8, 2], BF16)
        modsb = sb("modsb", [128, 2, 48, 2])
        A1 = sb("A1", [128, 2, 8, 2]); A2 = sb("A2", [128, 2, 8, 2])
        gq = sb("gq", [128, 2, 2])
        gsub = sb("gsub", [128, 2])
        neglam = sb("neglam", [128, 2])
        Rq = sb("Rq", [128, 2, 2, 128], BF16)
        hmod = sb("hmod", [128, 8, NCOL], BF16)

        def psk(i):
            return f"ps{i}"

        def checkpoint(name):
            if _DBG["stop"] != name:
                return
            ph = Phase(nc, "dbg")
            ph.dma("sp", dbgX[:, :], X[:].rearrange("p c n -> p (c n)"))
            ph.dma("pool", dbgH[:, :], hmod[:].rearrange("p c n -> p (c n)"))
            ph.dma("sp", dbgM[:, :], modsb[:].rearrange("p l j c -> p (l j c)"))
            ph.emit()
            raise _Stop()

        def mod_half(ph, l, hb, wbuf, wkey, pbank):
            wmv = w_mod[l].rearrange("(kc p) f -> p kc f", p=128)
            ph.dma("pool", wbuf[:], wmv[:, :, 512 * hb:512 * hb + 512], reads=[wkey + "r"], writes=[wkey])

            def compute():
                for j4 in range(4):
                    for kc in range(8):
                        last = (j4 == 3 and kc == 7)
                        MM(ph, ps[pbank][:, 2 * j4:2 * j4 + 2], wbuf[:, kc, 128 * j4:128 * j4 + 128], scondb[:, kc, :],
                           kc == 0, kc == 7, [wkey], [psk(pbank), wkey + "r"] if last else [psk(pbank)])
                j0 = 4 * hb
                TT(ph, modsb[:, l, j0:j0 + 4, :], ps[pbank][:, 0:8].rearrange("p (j c) -> p j c", c=2),
                   bc(VT[:, l, 16 + j0:16 + j0 + 4].unsqueeze(2), [128, 4, 2]), ALU.add, [psk(pbank)], [("modsb", l, hb)])
            return compute

        def mod_A(ph, l, Aout, goff, scoff):
            rk = [("modsb", l, hb) for hb in range(12)]
            TS(ph, Aout[:, l], modsb[:, l, scoff:scoff + 8, :], 1.0, None, ALU.add, None, rk, [("A", l, goff)])
            TT(ph, Aout[:, l], Aout[:, l], bc(VT[:, l, goff:goff + 8].unsqueeze(2), [128, 8, 2]), ALU.mult,
               [("A", l, goff)], [("A", l, goff)])

        with contextlib.ExitStack() as s0:
            xtile = [s0.enter_context(nc.sbuf_tensor(f"xtile{i}", [128, 1024], F32)) for i in range(4)]
            wm = [s0.enter_context(nc.sbuf_tensor(f"wm{i}", [128, 8, 512], BF16)) for i in range(2)]
            vtmp = s0.enter_context(nc.sbuf_tensor("vtmp", [128, 128], F32))
            ctmp = s0.enter_context(nc.sbuf_tensor("ctmp", [128, 16], F32))
            lt = s0.enter_context(nc.sbuf_tensor("lt", [64, 2, 4], F32))
            lp = s0.enter_context(nc.sbuf_tensor("lp", [64, 2, 2], F32))
            le = s0.enter_context(nc.sbuf_tensor("le", [128, 2, 2], F32))
            ph = Phase(nc, "p0")
            ph.dma("sp", ident[:], ident_d[:, :], writes=["ident"])
            ph.dma("sp", cosT[:], cos_d[:, :], writes=["cos"])
            ph.dma("sp", sinT[:], sin_d[:, :], writes=["sin"])
            ph.dma("sp", rsign[:], rsign_d[:, :], writes=["rsign"])
            CP(ph, identb[:], ident[:], ["ident"], ["identb"])
            MS(ph, onesD[:], 1.0 / 1024.0, ["onesD"])
            MS(ph, ones1[:], 1.0, ["ones1"])
            MS(ph, onesV[:], 1.0 / 128.0, ["onesV"])
            MS(ph, ones32[:], 1.0, ["ones32"])
            MS(ph, ones32q[:], 0.0, ["ones32q"])
            MS(ph, ones32q[0:64, 0, :], 1.0, ["ones32q"])
            MS(ph, ones32q[64:128, 1, :], 1.0, ["ones32q"])
            MS(ph, zerob[:], 0.0, ["zerob"])
            MS(ph, blk64[:], 0.0, ["blk64"])
            MS(ph, blk64[0:64, 0:64], 1.0 / 64.0, ["blk64"])
            MS(ph, blk64[64:128, 64:128], 1.0 / 64.0, ["blk64"])
            MS(ph, epsc[:], EPS, ["epsc"])
            MS(ph, negpi[:], -math.pi, ["negpi"])
            for l in range(2):
                ph.dma("sp", vtmp[0:100, :], va[l], reads=["vtmp_r"], writes=["vtmp"])
                TR(ph, ps[0][:, 0:100], vtmp[0:100, :], ident[0:100, 0:100], ["vtmp", "ident"], ["ps0"])
                CP(ph, VT[:, l, :], ps[0][:, 0:100], ["ps0"], ["VT", "vtmp_r"])
                ph.dma("sp", vtmp[0:96, :], vb[l], reads=["vtmp_r"], writes=["vtmp"])
                TR(ph, ps[0][:, 0:96], vtmp[0:96, :], ident[0:96, 0:96], ["vtmp", "ident"], ["ps0"])
                CP(ph, VTB[:, l, :], ps[0][:, 0:96], ["ps0"], ["VTB", "vtmp_r"])
            ph.dma("sp", vtmp[0:16, :], cvec[:, :], reads=["vtmp_r"], writes=["vtmp"])
            TR(ph, ps[0][:, 0:16], vtmp[0:16, :], ident[0:16, 0:16], ["vtmp", "ident"], ["ps0"])
            ACT(ph, ctmp[:], ps[0][:, 0:16], AF.Silu, ["ps0"], ["ctmp", "vtmp_r"])
            CP(ph, scond[:], ctmp[:].rearrange("p (c k) -> p k c", c=2), ["ctmp"], ["scond"])
            CP(ph, scondb[:], scond[:], ["scond"], ["scondb"])
            for l in range(2):
                for j in range(2):
                    for half in range(2):
                        ph.dma("sp", gq[64 * half:64 * half + 64, l, j:j + 1],
                               qkn[l, j].rearrange("(p o) -> p o", o=1), writes=["gq"])
                ph.dma("sp", gsub[:, l:l + 1], subg[l].rearrange("(p o) -> p o", o=1), writes=["gsub"])
                ph.dma("sp", lt[:, l, :], lamv[l].rearrange("k p -> p k"), writes=["lt"],
                       allow_slow_non_contiguous=True)
            for l in range(2):
                lam_init = 0.8 - 0.6 * math.exp(-0.3 * l)
                TS(ph, gsub[:, l:l + 1], gsub[:, l:l + 1], 1.0 - lam_init, None, ALU.mult, None, ["gsub"], ["gsub"])
                TT(ph, lp[:, l, 0:1], lt[:, l, 0:1], lt[:, l, 1:2], ALU.mult, ["lt"], ["lp"])
                TT(ph, lp[:, l, 1:2], lt[:, l, 2:3], lt[:, l, 3:4], ALU.mult, ["lt"], ["lp"])
                MM(ph, ps[1][:, 0:2], ones32[0:64, :], lp[:, l, :], True, True, ["ones32", "lp"], ["ps1"])
                ACT(ph, le[:, l, :], ps[1][:, 0:2], AF.Exp, ["ps1"], ["le"])
                TT(ph, neglam[:, l:l + 1], le[:, l, 1:2], le[:, l, 0:1], ALU.subtract, ["le"], ["neglam"])
                TS(ph, neglam[:, l:l + 1], neglam[:, l:l + 1], -lam_init, None, ALU.add, None, ["neglam"], ["neglam"])
                for j in range(2):
                    TS(ph, Rq[:, l, j, :], rsign[:], gq[:, l, j:j + 1], None, ALU.mult, None, ["rsign", "gq"], ["Rq"])
            mod_pend0 = None
            for tt in range(12):
                if tt % 2 == 0 and tt // 2 < 4:
                    hb_ = tt // 2
                    c_ = mod_half(ph, 0, hb_, wm[hb_ % 2], f"wm{hb_ % 2}", 6 + hb_ % 2)
                    if mod_pend0 is not None:
                        mod_pend0()
                    mod_pend0 = c_
                xt = xtile[tt % 4]
                xk = f"xt{tt % 4}"
                xq = "sp" if tt % 2 == 0 else "act"
                if tt < 4:
                    for q in range(2):
                        ph.dma(xq, xt[64 * q:64 * q + 64, :],
                               xp[q].rearrange("(j s) d -> s j d", s=4)[tt], reads=[xk + "r"], writes=[xk])
                else:
                    s4, hf = (tt - 4) // 2, (tt - 4) % 2
                    ph.dma(xq, xt[:, :], xs.rearrange("(j s) d -> s j d", s=4)[s4, 128 * hf:128 * hf + 128, :],
                           reads=[xk + "r"], writes=[xk])
                for half in range(2):
                    bank = 2 + (2 * tt + half) % 4
                    for c4 in range(4):
                        c = 4 * half + c4
                        TR(ph, ps[bank][:, 128 * c4:128 * c4 + 128], xt[:, 128 * c:128 * c + 128], ident[:],
                           [xk, "ident"], [psk(bank)])
                    dst = X[:, 4 * half:4 * half + 4, 128 * tt:128 * tt + 128]
                    src = ps[bank][:, :].rearrange("p (c t) -> p c t", c=4)
                    if half == 0:
                        CP(ph, dst, src, [psk(bank)], [("X", tt), xk + "r"])
                    else:
                        ACT(ph, dst, src, AF.Copy, [psk(bank)], [("X", tt), xk + "r"])
            mod_pend0()
            mod_A(ph, 0, A1, 0, 8)
            ph.emit()
            checkpoint("p0")

        MOD_REST = [(0, hb) for hb in range(4, 12)] + [(1, hb) for hb in range(12)]

        def norm_mod(ph, l, Atab, shoff, sq, lnv, rstd, tmp):
            for ct in range(3):
                cd = 0 if ct == 0 else 1
                cols = slice(512 * ct, 512 * ct + 512)
                for c in range(8):
                    ACT(ph, sq[c % 2][:], X[:, c, cols], AF.Square, [("X", ct)], [f"sq{c % 2}"])
                    MM(ph, ps[0][:], onesD[:], sq[c % 2][:], c == 0, c == 7, [f"sq{c % 2}"], ["ps0"])
                ACT(ph, lnv[:], ps[0][:], AF.Ln, ["ps0"], ["lnv"], bias=epsc[:])
                ACT(ph, rstd[:], lnv[:], AF.Exp, ["lnv"], ["rstd"], scale=-0.5)
                for c in range(8):
                    STT(ph, tmp[c % 2][:], X[:, c, cols], Atab[:, l, c, cd:cd + 1], rstd[:], ALU.mult, ALU.mult,
                        [("X", ct), "rstd"], [f"tmp{c % 2}"])
                    ACT(ph, hmod[:, c, cols], tmp[c % 2][:], AF.Identity, [f"tmp{c % 2}"], [("hmod", ct)],
                        bias=modsb[:, l, shoff + c, cd:cd + 1])

        for l in range(DEPTH):
            sL = contextlib.ExitStack()
            sL.__enter__()
            U4 = sL.enter_context(nc.sbuf_tensor(f"U4_{l}", [128, 16, 384], BF16))
            uedge = sL.enter_context(nc.sbuf_tensor(f"uedge_{l}", [128, 4, 512], F32))
            sA = contextlib.ExitStack()
            with sA:
                def sba(name, shape, dt=F32):
                    return sA.enter_context(nc.sbuf_tensor(f"{name}_{l}", list(shape), dt))

                qT = sba("qT", [128, 4, NCOL], BF16)
                kT = sba("kT", [128, 4, 2048], BF16)
                Vtok = sba("Vtok", [128, 16, 512], BF16)
                with contextlib.ExitStack() as sA1:
                    def sb1(name, shape, dt=F32):
                        return sA1.enter_context(nc.sbuf_tensor(f"{name}_{l}", list(shape), dt))
                    win = sb1("win", [128, 8, 2048], BF16)
                    sq = [sb1(f"sq{i}", [128, 512], BF16) for i in range(2)]
                    tmp = [sb1(f"tmp{i}", [128, 512]) for i in range(2)]
                    tmpB = [sb1(f"tmpB{i}", [128, 512]) for i in range(2)]
                    lnv = sb1("lnv", [128, 512]); rstd = sb1("rstd", [128, 512])
                    lrs = [lnv, rstd]; lrk = ["lnv", "rstd"]
                    qf = sb1("qf", [128, 512])
                    qbs = [sb1(f"qb{i}", [128, 512], BF16) for i in range(2)]
                    kst = sb1("kst", [128, 4, 128]); vst = qf
                    ckt = sb1("ckt", [128, 512])
                    hrep = tmpB[0][:].bitcast(BF16).rearrange("p (k n) -> p k n", k=8)
                    ph = Phase(nc, f"A{l}")
                    for kc in range(8):
                        ph.dma("pool", win[:, kc, :], w_in[l, 128 * kc:128 * kc + 128, :], writes=["win"])
                    for i in range(4):
                        ph.dma("pool", Vtok[:, 12 + i, :], cv[l, 128 * i:128 * i + 128, :], writes=[("V", 12 + i)])
                    norm_mod(ph, l, A1, 0, sq, lnv, rstd, tmp)
                    for i in range(4):
                        ph.dma("sp", ckt[:], ck[l, 128 * i:128 * i + 128, :], reads=["cktr"], writes=["ckt"])
                        for h in range(4):
                            TR(ph, ps[1][:, 128 * h:128 * h + 128], ckt[:, 128 * h:128 * h + 128], ident[:],
                               ["ckt", "ident"], ["ps1"])
                        CP(ph, kT[:, :, 1536 + 128 * i:1536 + 128 * i + 128],
                           ps[1][:, :].rearrange("p (h t) -> p h t", h=4), ["ps1"], [("kT", 3), "cktr"])
                    def qk_post(ct, mo, bank, sqt, sqk, par):
                        cols = slice(512 * ct, 512 * ct + 512)
                        isk = mo // 4
                        h = mo % 4

                        def run():
                            lr = lrs[par]; lk = lrk[par]
                            tA = tmp[par]; tAk = f"tmp{par}"
                            tB = tmpB[par]; tBk = f"tmpB{par}"
                            qb = qbs[par]; qbk = f"qb{par}"
                            MM(ph, ps[4][:], blk64[:], sqt[:], True, True, [sqk, "blk64"], ["ps4"])
                            ACT(ph, lr[:], ps[4][:], AF.Ln, ["ps4"], [lk], bias=epsc[:])
                            ACT(ph, lr[:], lr[:], AF.Exp, [lk], [lk], scale=-0.5)
                            gcol = gq[:, l, isk:isk + 1]
                            if ct == 0:
                                if isk == 0:
                                    for q in range(2):
                                        dst = qT[:, h, 0:512].rearrange("p (q s j) -> p q s j", q=2, s=4)[:, q]
                                        src = ps[bank][:, :].rearrange("p (s q j) -> p q s j", q=2, s=4)[:, q]
                                        rs = lr[:].rearrange("p (s q j) -> p q s j", q=2, s=4)[:, q]
                                        STT(ph, dst, src, gcol, rs, ALU.mult, ALU.mult, [psk(bank), lk], [("qT", 0)])
                                else:
                                    STT(ph, qf[:], ps[bank][:], gcol, lr[:], ALU.mult, ALU.mult,
                                        [psk(bank), lk], ["qf"])
                                    CP(ph, kT[:, h, 0:512], qf[:], ["qf"], [("kT", 0)])
                                    for s4 in range(4):
                                        TR(ph, ps[5][:, 128 * s4:128 * s4 + 128], qf[:, 128 * s4:128 * s4 + 128],
                                           ident[:], ["qf", "ident"], ["ps5"])
                                    ACT(ph, kst[:, :, :],
                                        ps[5][:, :].rearrange("p (s f) -> p s f", s=4), AF.Copy, ["ps5"], ["kst"])
                                    for s4 in range(4):
                                        for q in range(2):
                                            ph.dma("sp", nk[q, l].rearrange("(j s) f -> s j f", s=4)[s4, :, 128 * h:128 * h + 128],
                                                   kst[64 * q:64 * q + 64, s4, :], reads=["kst"])
                            else:
                                sc = slice(512 * (ct - 1), 512 * (ct - 1) + 512)
                                CP(ph, qb[:], ps[bank][:], [psk(bank)], [qbk])
                                MM(ph, ps[5][:], Rq[:, l, isk, :], qb[:], True, True, [qbk, "Rq"], ["ps5"])
                                STT(ph, tA[:], ps[bank][:], gcol, cosT[:, sc], ALU.mult, ALU.mult, [psk(bank), "cos"], [tAk])
                                TT(ph, tB[:], ps[5][:], sinT[:, sc], ALU.mult, ["ps5", "sin"], [tBk])
                                TT(ph, tA[:], tA[:], tB[:], ALU.add, [tAk, tBk], [tAk])
                                dst = (kT if isk else qT)[:, h, cols]
                                TT(ph, dst, tA[:], lr[:], ALU.mult, [tAk, lk], [("kT" if isk else "qT", ct)])
                        return run

                    pending = None
                    n_ = 0
                    for ct in range(3):
                        cols = slice(512 * ct, 512 * ct + 512)
                        for mo in range(8):
                            bank = 2 + n_ % 2
                            sqt = sq[n_ % 2]
                            sqk = f"sq{n_ % 2}"
                            for kc in range(8):
                                MM(ph, ps[bank][:], win[:, kc, 128 * mo:128 * mo + 128], hmod[:, kc, cols],
                                   kc == 0, kc == 7, ["win", ("hmod", ct)], [psk(bank)])
                            ACT(ph, sqt[:], ps[bank][:], AF.Square, [psk(bank)], [sqk])
                            if pending is not None:
                                pending()
                            pending = qk_post(ct, mo, bank, sqt, sqk, n_ % 2)
                            n_ += 1
                    pending()
                    for tt in range(12):
                        bank = 2 + tt % 2
                        for kc in range(8):
                            MM(ph, ps[bank][:], hmod[:, kc, 128 * tt:128 * tt + 128], win[:, kc, 1024:1536],
                               kc == 0, kc == 7, ["win", ("hmod", tt // 4)], [psk(bank)])
                        CP(ph, Vtok[:, tt, :], ps[bank][:], [psk(bank)], [("V", tt)])
                        if tt < 4:
                            ACT(ph, vst[:], ps[bank][:], AF.Copy, [psk(bank)], ["qf"])
                            for q in range(2):
                                ph.dma("sp", nv[q, l].rearrange("(j s) f -> s j f", s=4)[tt],
                                       vst[64 * q:64 * q + 64, :], reads=["qf"])
                    for gp in range(16):
                        bank = 6 + gp % 2
                        for s4 in range(4):
                            for kc in range(8):
                                lw = win[:, kc, 1536 + 32 * gp:1536 + 32 * gp + 32]
                                MM(ph, ps[bank][32 * s4:32 * s4 + 32, 0:128], lw, hmod[:, kc, 128 * s4:128 * s4 + 128],
                                   kc == 0, kc == 7, ["win", ("hmod", 0)], [psk(bank)], tp=(0, 32 * s4))
                            for kc in range(8):
                                lw = win[:, kc, 1536 + 32 * gp:1536 + 32 * gp + 32]
                                MM(ph, ps[bank][32 * s4:32 * s4 + 32, 128:384], lw,
                                   hmod[:, kc, 512 + 256 * s4:512 + 256 * s4 + 256],
                                   kc == 0, kc == 7, ["win", ("hmod", 1), ("hmod", 2)], [psk(bank)], tp=(0, 32 * s4))
                        if gp % 2 == 0:
                            CP(ph, U4[:, gp, :], ps[bank][:, 0:384], [psk(bank)], ["U4"])
                        else:
                            ACT(ph, U4[:, gp, :], ps[bank][:, 0:384], AF.Copy, [psk(bank)], ["U4"])
                    for q in range(2):
                        for e_ in range(2):
                            col = 64 * q if e_ == 0 else 384 + 64 * q + 63
                            CP(ph, hrep, bc(hmod[:, :, col:col + 1], [128, 8, 128]), [("hmod", 0)], ["tmpB0"])
                            for kc in range(8):
                                MM(ph, ps[4][:], hrep[:, kc, :], win[:, kc, 1536:2048], kc == 0, kc == 7,
                                   ["tmpB0", "win"], ["ps4"])
                            CP(ph, uedge[:, 2 * q + e_, :], ps[4][:], ["ps4"], ["uedge"])
                    ph.emit()
                    checkpoint(f"A{l}")

                with contextlib.ExitStack() as sB:
                    def sbb(name, shape, dt=F32):
                        return sB.enter_context(nc.sbuf_tensor(f"{name}_{l}", list(shape), dt))
                    Et = [sbb(f"E{i}", [128, 512], BF16) for i in range(8)]
                    Esum = [sbb(f"Esum{i}", [128, 512]) for i in range(2)]
                    r0 = sbb("r0", [128, 512]); r1 = sbb("r1", [128, 512])
                    Osb = [sbb(f"Osb{i}", [128, 512]) for i in range(4)]
                    if l == 0:
                        wmB = sbb("wmB", [128, 8, 512], BF16)
                    oa = sbb("oa", [128, 512]); ob = sbb("ob", [128, 512])
                    sqb = sbb("sqb", [128, 512], BF16)
                    lnv = sbb("lnvB", [128, 512]); rstd = sbb("rstdB", [128, 512])
                    ph = Phase(nc, f"B{l}")
                    ecnt = [0]
                    BRE_v = hmod[:, 4, 0:1024].bitcast(F32).rearrange("p (a c) -> p a c", c=16)
                    BIM_v = hmod[:, 5, 0:1024].bitcast(F32).rearrange("p (a c) -> p a c", c=16)
                    ph.dma("sp", BRE_v, bre_d[l].rearrange("d (gp g2) p c -> (g2 p) (d gp) c", g2=2), writes=["BREv"])
                    ph.dma("act", BIM_v, bim_d[l].rearrange("d (gp g2) p c -> (g2 p) (d gp) c", g2=2), writes=["BIMv"])

                    def attn_unit(h, qcols, N, keychunks, outdst, outkey):
                        nkc = len(keychunks)
                        tiles = [(ci, s) for ci in range(nkc) for s in range(2)]
                        slots = {}

                        def issue_score(t):
                            ci, s = tiles[t]
                            kcol = keychunks[ci][0]
                            sbank = (0, 1, 2, 5, 7)[ecnt[0] % 5]
                            ei = ecnt[0] % 8
                            ecnt[0] += 1
                            slots[t] = ei
                            MM(ph, ps[sbank][:, 0:N], kT[64 * s:64 * s + 64, h, kcol:kcol + 128],
                               qT[64 * s:64 * s + 64, h, qcols:qcols + N], True, True, [], [psk(sbank)])
                            ACT(ph, Et[ei][:, 0:N], ps[sbank][:, 0:N], AF.Exp, [psk(sbank)], [f"E{ei}"], scale=0.125)

                        def issue_av(t):
                            ci, s = tiles[t]
                            (kcol, vt, plo, phi) = keychunks[ci]
                            ei = slots[t]
                            MM(ph, ps[4 + 2 * s][:, 0:N], Vtok[plo:phi, vt, 128 * h:128 * h + 128],
                               Et[ei][plo:phi, 0:N], ci == 0, ci == nkc - 1, [f"E{ei}"], [psk(4 + 2 * s)])
                            if ci == 0:
                                CP(ph, Esum[s][:, 0:N], Et[ei][:, 0:N], [f"E{ei}"], [f"Esum{s}"])
                            else:
                                TT(ph, Esum[s][:, 0:N], Esum[s][:, 0:N], Et[ei][:, 0:N], ALU.add,
                                   [f"E{ei}", f"Esum{s}"], [f"Esum{s}"])
                        LOOK = 4
                        for t in range(len(tiles) + LOOK):
                            if t < len(tiles):
                                issue_score(t)
                            if t >= LOOK:
                                issue_av(t - LOOK)
                        def part1():
                          ACT(ph, Osb[0][:, 0:N], ps[4][:, 0:N], AF.Copy, ["ps4"], ["Os0"])
                          ACT(ph, Osb[2][:, 0:N], ps[6][:, 0:N], AF.Copy, ["ps6"], ["Os2"])
                          plo_, phi_ = keychunks[0][2], keychunks[0][3]
                          lw = ones32[:] if phi_ - plo_ == 128 else ones32q[:, plo_ // 64, :]
                          for s_, rr in ((0, r0), (1, r1)):
                              MM(ph, ps[3][:, 0:N], lw, Esum[s_][:, 0:N], True, True, [f"Esum{s_}"], ["ps3"])
                              ACT(ph, rr[:, 0:N], ps[3][:, 0:N], AF.Ln, ["ps3"], [f"r{s_}"])
                              ACT(ph, rr[:, 0:N], rr[:, 0:N], AF.Exp, [f"r{s_}"], [f"r{s_}"], scale=-1.0)
                          TT(ph, oa[:, 0:N], Osb[0][:, 0:N], r0[:, 0:N], ALU.mult, ["Os0", "r0"], ["oa"])
                          TT(ph, ob[:, 0:N], Osb[2][:, 0:N], r1[:, 0:N], ALU.mult, ["Os2", "r1"], ["ob"])
                          STT(ph, oa[:, 0:N], ob[:, 0:N], neglam[:, l:l + 1], oa[:, 0:N], ALU.mult, ALU.add,
                              ["oa", "ob"], ["oa"])

                        def tail():
                            ACT(ph, sqb[:, 0:N], oa[:, 0:N], AF.Square, ["oa"], ["sqb"])
                            MM(ph, ps[3][:, 0:N], onesV[:], sqb[:, 0:N], True, True, ["sqb"], ["ps3"])
                            ACT(ph, lnv[:, 0:N], ps[3][:, 0:N], AF.Ln, ["ps3"], ["lnv"], bias=epsc[:])
                            ACT(ph, rstd[:, 0:N], lnv[:, 0:N], AF.Exp, ["lnv"], ["rstd"], scale=-0.5)
                            src = oa[:, 0:N]
                            rs = rstd[:, 0:N]
                            if N == 256:
                                src = src.rearrange("p (s j) -> p s j", s=4)
                                rs = rs.rearrange("p (s j) -> p s j", s=4)
                            STT(ph, outdst, src, gsub[:, l:l + 1], rs, ALU.mult, ALU.mult, ["oa", "rstd"], [outkey])
                        return part1, tail

                    units = []
                    for h in range(4):
                        for q in range(2):
                            kch = [(128 * s4, s4, 64 * q, 64 * q + 64) for s4 in range(4)]
                            dst = hmod[:, h, 0:512].rearrange("p (s q j) -> p q s j", s=4, q=2)[:, q]
                            units.append((h, 256 * q, 256, kch, dst, ("mix", h, q)))
                        for qt in range(2):
                            kch = [(512 + 128 * i, 4 + i, 0, 128) for i in range(8)] + \
                                  [(1536 + 128 * i, 12 + i, 0, 128) for i in range(4)]
                            dst = hmod[:, h, 512 + 512 * qt:512 + 512 * qt + 512]
                            units.append((h, 512 + 512 * qt, 512, kch, dst, ("mix", h, 2 + qt)))
                    pending = None
                    mod_todo = list(MOD_REST) if l == 0 else []
                    mod_pend = None
                    for ui, u_ in enumerate(units):
                        p1_, t_ = attn_unit(*u_)
                        if pending is not None:
                            pending()
                        p1_()
                        pending = t_
                        for _ in range(2 if ui < 4 else 1):
                            if mod_pend is not None:
                                mod_pend()
                                mod_pend = None
                            if mod_todo:
                                ml_, mhb_ = mod_todo.pop(0)
                                mod_pend = mod_half(ph, ml_, mhb_, wmB, "wmB", 3)
                    pending()
                    if l == 0:
                        if mod_pend is not None:
                            mod_pend()
                        while mod_todo:
                            ml_, mhb_ = mod_todo.pop(0)
                            mod_half(ph, ml_, mhb_, wmB, "wmB", 3)()
                        mod_A(ph, 0, A2, 8, 32)
                        mod_A(ph, 1, A1, 0, 8)
                        mod_A(ph, 1, A2, 8, 32)
                    ph.emit()
                    checkpoint(f"B{l}")
            sC = contextlib.ExitStack()
            with sC:
                def sbc(name, shape, dt=F32):
                    return sC.enter_context(nc.sbuf_tensor(f"{name}_{l}", list(shape), dt))
                T4 = sbc("T4", [128, 32, 128], BF16)
                B4 = sbc("B4", [128, 32, 2, 128], BF16)
                C4 = sbc("C4", [128, 32, 2, 128], BF16)
                BB = sbc("BB", [128, 2, 32, 16])
                LP = sbc("LP", [128, 5, 2, 32])
                A1b = sbc("A1b", [128, 2, 32]); A2b = sbc("A2b", [128, 2, 32])
                H0 = sbc("H0", [128, 2, 32])
                with contextlib.ExitStack() as sC1:
                    def sc1(name, shape, dt=F32):
                        return sC1.enter_context(nc.sbuf_tensor(f"{name}_{l}", list(shape), dt))
                    BP = sc1("BP", [128, 32, 4, 2, 32], BF16)
                    CPd = sc1("CPd", [128, 32, 2, 32], BF16)
                    rowt = sc1("rowt", [32, 128]); rowt2 = sc1("rowt2", [32, 128]); rowt3 = sc1("rowt3", [32, 128])
                    ls2 = sc1("ls2", [32, 2]); h0t = sc1("h0t", [32, 256]); h0r = sc1("h0r", [32, 2, 128])
                    LRE = sc1("LRE", [128, 32]); LIM = sc1("LIM", [128, 32]); DEL = sc1("DEL", [128, 32])
                    ta = sc1("ta", [128, 32]); tb = sc1("tb", [128, 32]); tcc = sc1("tcc", [128, 32])
                    ea = sc1("ea", [128, 32]); sinb = sc1("sinb", [128, 32]); cosb = sc1("cosb", [128, 32])
                    cr = sc1("cr", [128, 32]); ci = sc1("ci", [128, 32])
                    BRE = hmod[:, 4, 0:1024].bitcast(F32).rearrange("p (a c) -> p a c", c=16)
                    BIM = hmod[:, 5, 0:1024].bitcast(F32).rearrange("p (a c) -> p a c", c=16)
                    CRE = sc1("CRE", [128, 32, 16]); CIM = sc1("CIM", [128, 32, 16])
                    rowsC = sc1("rowsC", [32, 2048])
                    w1 = sc1("w1", [128, 32, 16]); w2 = sc1("w2", [128, 32, 16]); w3 = sc1("w3", [128, 32, 16])
                    dvec = sc1("dvec", [128, 16])
                    v1 = sc1("v1", [128, 16, 16]); v2 = sc1("v2", [128, 16, 16]); v3 = sc1("v3", [128, 16, 16])
                    ph = Phase(nc, f"C1{l}")
                    MS(ph, BP[:], 0.0, ["BP"]); MS(ph, CPd[:], 0.0, ["CPd"]); MS(ph, C4[:], 0.0, ["C4"])
                    ph.dma("sp", rowt[:], lre_d[l], writes=["rowt"])
                    ph.dma("sp", rowt2[:], lim_d[l], writes=["rowt2"])
                    ph.dma("sp", ls2[:], lst_d[l], writes=["ls2"])
                    for j in range(4):
                        ph.dma("sp", dvec[32 * j:32 * j + 32, :], ssmd_d[l].rearrange("(gp r) -> r gp", r=32),
                               writes=["dvec"], allow_slow_non_contiguous=True)
                    ph.dma("sp", h0t[:], stt_in[l], writes=["h0t"])
                    TR(ph, ps[0][:, 0:32], rowt[:], ident[0:32, 0:32], ["rowt", "ident"], ["ps0"])
                    CP(ph, LRE[:], ps[0][:, 0:32], ["ps0"], ["LRE"])
                    TR(ph, ps[0][:, 32:64], rowt2[:], ident[0:32, 0:32], ["rowt2", "ident"], ["ps0b"])
                    CP(ph, LIM[:], ps[0][:, 32:64], ["ps0b"], ["LIM"])
                    CP(ph, rowt3[:].rearrange("r (g p) -> r g p", g=2), bc(ls2[:].unsqueeze(2), [32, 2, 64]),
                       ["ls2"], ["rowt3"])
                    TR(ph, ps[0][:, 64:96], rowt3[:], ident[0:32, 0:32], ["rowt3", "ident"], ["ps0c"])
                    ACT(ph, DEL[:], ps[0][:, 64:96], AF.Exp, ["ps0c"], ["DEL"])
                    CP(ph, h0r[:], h0t[:].rearrange("r (gp ri) -> r ri gp", ri=2), ["h0t"], ["h0r"])
                    for ri in range(2):
                        TR(ph, ps[1][:, 32 * ri:32 * ri + 32], h0r[:, ri, :], ident[0:32, 0:32], ["h0r", "ident"], ["ps1"])
                    CP(ph, H0[:], ps[1][:, 0:64].rearrange("p (r d) -> p r d", r=2), ["ps1"], ["H0"])
                    TT(ph, ta[:], LRE[:], DEL[:], ALU.mult, ["LRE", "DEL"], ["ta"])
                    TT(ph, tb[:], LIM[:], DEL[:], ALU.mult, ["LIM", "DEL"], ["tb"])
                    ACT(ph, ea[:], ta[:], AF.Exp, ["ta"], ["ea"])
                    def range_reduce(add):
                        TS(ph, tcc[:], tb[:], add, None, ALU.add, None, ["tb", "sinb"], ["tcc"])
                        for _ in range(4):
                            TS(ph, cr[:], tcc[:], 2 * math.pi, None, ALU.is_ge, None, ["tcc"], ["cr"])
                            STT(ph, tcc[:], cr[:], -2 * math.pi, tcc[:], ALU.mult, ALU.add, ["cr", "tcc"], ["tcc"])
                    range_reduce(math.pi)
                    ACT(ph, sinb[:], tcc[:], AF.Sin, ["tcc"], ["sinb"], bias=negpi[:])
                    range_reduce(1.5 * math.pi)
                    ACT(ph, cosb[:], tcc[:], AF.Sin, ["tcc"], ["cosb"], bias=negpi[:])
                    MS(ph, LP[:, 0, 0, :], 1.0, ["LP"]); MS(ph, LP[:, 0, 1, :], 0.0, ["LP"])
                    TT(ph, LP[:, 1, 0, :], ea[:], cosb[:], ALU.mult, ["ea", "cosb"], ["LP"])
                    TT(ph, LP[:, 1, 1, :], ea[:], sinb[:], ALU.mult, ["ea", "sinb"], ["LP"])
                    for k in range(2, 5):
                        TT(ph, ta[:], LP[:, k - 1, 0, :], LP[:, 1, 0, :], ALU.mult, ["LP"], ["ta"])
                        TT(ph, tb[:], LP[:, k - 1, 1, :], LP[:, 1, 1, :], ALU.mult, ["LP"], ["tb"])
                        TT(ph, LP[:, k, 0, :], ta[:], tb[:], ALU.subtract, ["ta", "tb"], ["LP"])
                        TT(ph, ta[:], LP[:, k - 1, 0, :], LP[:, 1, 1, :], ALU.mult, ["LP"], ["ta"])
                        TT(ph, tb[:], LP[:, k - 1, 1, :], LP[:, 1, 0, :], ALU.mult, ["LP"], ["tb"])
                        TT(ph, LP[:, k, 1, :], ta[:], tb[:], ALU.add, ["ta", "tb"], ["LP"])
                    TT(ph, ta[:], LRE[:], LRE[:], ALU.mult, ["LRE"], ["ta"])
                    TT(ph, tb[:], LIM[:], LIM[:], ALU.mult, ["LIM"], ["tb"])
                    TT(ph, ta[:], ta[:], tb[:], ALU.add, ["ta", "tb"], ["ta"])
                    ph.op("dve", lambda e: e.reciprocal(out=tcc[:], in_=ta[:]), ["ta"], ["tcc"])
                    TS(ph, ea[:], LP[:, 1, 0, :], -1.0, None, ALU.add, None, ["LP"], ["ea"])
                    TT(ph, ta[:], ea[:], LRE[:], ALU.mult, ["ea", "LRE"], ["ta"])
                    TT(ph, tb[:], LP[:, 1, 1, :], LIM[:], ALU.mult, ["LP", "LIM"], ["tb"])
                    TT(ph, ta[:], ta[:], tb[:], ALU.add, ["ta", "tb"], ["ta"])
                    TT(ph, cr[:], ta[:], tcc[:], ALU.mult, ["ta", "tcc"], ["cr"])
                    TT(ph, ta[:], LP[:, 1, 1, :], LRE[:], ALU.mult, ["LP", "LRE"], ["ta"])
                    TT(ph, tb[:], ea[:], LIM[:], ALU.mult, ["ea", "LIM"], ["tb"])
                    TT(ph, ta[:], ta[:], tb[:], ALU.subtract, ["ta", "tb"], ["ta"])
                    TT(ph, ci[:], ta[:], tcc[:], ALU.mult, ["ta", "tcc"], ["ci"])
                    crb = bc(cr[:].unsqueeze(2), [128, 32, 16]); cib = bc(ci[:].unsqueeze(2), [128, 32, 16])
                    TT(ph, w1[:], BRE[:], crb, ALU.mult, ["BRE", "cr"], ["w1"])
                    TT(ph, w2[:], BIM[:], cib, ALU.mult, ["BIM", "ci"], ["w2"])
                    TT(ph, BB[:, 0], w1[:], w2[:], ALU.subtract, ["w1", "w2"], ["BB"])
                    TT(ph, w1[:], BIM[:], crb, ALU.mult, ["BIM", "cr"], ["w1"])
                    TT(ph, w2[:], BRE[:], cib, ALU.mult, ["BRE", "ci"], ["w2"])
                    TT(ph, BB[:, 1], w1[:], w2[:], ALU.add, ["w1", "w2"], ["BB"])
                    for ti_, (src_d, dstC, nm) in enumerate(((cre_d, CRE, "CRE"), (cim_d, CIM, "CIM"))):
                        srcv = src_d[l].rearrange("d (gp g2) c p -> (d gp) c g2 p", g2=2)
                        for g2 in range(2):
                            ph.dma("sp" if g2 == 0 else "act",
                                   rowsC[:].rearrange("r (c g p) -> r c g p", c=16, g=2)[:, :, g2, :], srcv[:, :, g2, :],
                                   reads=["rowsCr"], writes=["rowsC"])
                        for c_ in range(16):
                            TR(ph, ps[2][:, 32 * c_:32 * c_ + 32], rowsC[:, 128 * c_:128 * c_ + 128], ident[0:32, 0:32],
                               ["rowsC", "ident"], ["ps2"])
                        CP(ph, dstC[:], ps[2][:, :].rearrange("p (c a) -> p a c", c=16), ["ps2"], [nm, "rowsCr"])

                    def padcopy(dst_of_half, src, rkeys, wkey):
                        for g2 in range(2):
                            CP(ph, dst_of_half(g2), src[64 * g2:64 * g2 + 64], rkeys, [wkey])

                    for k in range(4):
                        if k == 0:
                            res = (BB[:, 0], BB[:, 1])
                            rk = ["BB"]
                        else:
                            lr = bc(LP[:, k, 0, :].unsqueeze(2), [128, 32, 16])
                            li = bc(LP[:, k, 1, :].unsqueeze(2), [128, 32, 16])
                            TT(ph, w1[:], BB[:, 0], lr, ALU.mult, ["BB", "LP"], ["w1"])
                            TT(ph, w2[:], BB[:, 1], li, ALU.mult, ["BB", "LP"], ["w2"])
                            TT(ph, w1[:], w1[:], w2[:], ALU.subtract, ["w1", "w2"], ["w1"])
                            TT(ph, w2[:], BB[:, 1], lr, ALU.mult, ["BB", "LP"], ["w2"])
                            TT(ph, w3[:], BB[:, 0], li, ALU.mult, ["BB", "LP"], ["w3"])
                            TT(ph, w2[:], w2[:], w3[:], ALU.add, ["w2", "w3"], ["w2"])
                            res = (w1[:], w2[:])
                            rk = ["w1", "w2"]
                        for ri in range(2):
                            padcopy(lambda g2, k=k, ri=ri: BP[64 * g2:64 * g2 + 64, :, k, ri, 16 * g2:16 * g2 + 16],
                                    res[ri], rk, "BP")
                    padcopy(lambda g2: CPd[64 * g2:64 * g2 + 64, :, 0, 16 * g2:16 * g2 + 16], CRE[:], ["CRE"], "CPd")
                    TS(ph, w3[:], CIM[:], -1.0, None, ALU.mult, None, ["CIM"], ["w3"])
                    padcopy(lambda g2: CPd[64 * g2:64 * g2 + 64, :, 1, 16 * g2:16 * g2 + 16], w3[:], ["w3"], "CPd")
                    for d in range(2):
                        dgs = slice(16 * d, 16 * d + 16)
                        for t4 in range(4):
                            k = t4 + 1 if d == 0 else 4 - t4
                            lr = bc(LP[:, k, 0, dgs].unsqueeze(2), [128, 16, 16])
                            li = bc(LP[:, k, 1, dgs].unsqueeze(2), [128, 16, 16])
                            TT(ph, v1[:, 0:16], CRE[:, dgs], lr, ALU.mult, ["CRE", "LP"], ["v1"])
                            TT(ph, v2[:, 0:16], CIM[:, dgs], li, ALU.mult, ["CIM", "LP"], ["v2"])
                            TT(ph, v1[:, 0:16], v1[:, 0:16], v2[:, 0:16], ALU.subtract, ["v1", "v2"], ["v1"])
                            TT(ph, v2[:, 0:16], CRE[:, dgs], li, ALU.mult, ["CRE", "LP"], ["v2"])
                            TT(ph, v3[:, 0:16], CIM[:, dgs], lr, ALU.mult, ["CIM", "LP"], ["v3"])
                            TT(ph, v2[:, 0:16], v2[:, 0:16], v3[:, 0:16], ALU.add, ["v2", "v3"], ["v2"])
                            TS(ph, v2[:, 0:16], v2[:, 0:16], -1.0, None, ALU.mult, None, ["v2"], ["v2"])
                            for ri, srcw in enumerate((v1, v2)):
                                for g2 in range(2):
                                    CP(ph, C4[64 * g2:64 * g2 + 64, dgs, ri, 32 * t4 + 16 * g2:32 * t4 + 16 * g2 + 16],
                                       srcw[64 * g2:64 * g2 + 64, 0:16], ["v1", "v2"], ["C4"])
                    for dg in range(32):
                        d = dg // 16
                        gp = dg % 16
                        bank = 4 + dg % 2
                        for ri in range(2):
                            for s4 in range(4):
                                k = 3 - s4 if d == 0 else s4
                                MM(ph, ps[bank][32 * s4:32 * s4 + 32, 128 * ri:128 * ri + 128], BP[:, dg, k, ri, :],
                                   identb[:], True, True, ["BP", "identb"], [psk(bank)], tp=(0, 32 * s4))
                        MM(ph, ps[bank][:, 256:384], zerob[:], identb[:], True, True, ["zerob", "identb"], [psk(bank)])
                        for s4 in range(4):
                            for t4 in range(4):
                                tau = t4 - s4 if d == 0 else s4 - t4
                                if tau < 0:
                                    continue
                                for ri in range(2):
                                    ph.op("pe", lambda e, bank=bank, s4=s4, t4=t4, dg=dg, tau=tau, ri=ri: e.matmul(
                                        ps[bank][32 * s4:32 * s4 + 32, 256 + 32 * t4:256 + 32 * t4 + 32],
                                        BP[:, dg, tau, ri, :], CPd[:, dg, ri, :], start=False, stop=True,
                                        tile_position=(0, 32 * s4), skip_group_check=True),
                                        ["BP", "CPd"], [psk(bank)])
                        ACT(ph, B4[:, dg, :, :], ps[bank][:, 0:256].rearrange("p (r m) -> p r m", r=2), AF.Copy,
                            [psk(bank)], ["B4"])
                        if d == 0:
                            STT(ph, T4[:, dg, :], ident[:], dvec[:, gp:gp + 1], ps[bank][:, 256:384], ALU.mult, ALU.add,
                                [psk(bank), "dvec", "ident"], ["T4"])
                        else:
                            CP(ph, T4[:, dg, :], ps[bank][:, 256:384], [psk(bank)], ["T4"])
                    CP(ph, A1b[:, 0, :], LP[:, 4, 0, :], ["LP"], ["A1b"]); CP(ph, A1b[:, 1, :], LP[:, 4, 0, :], ["LP"], ["A1b"])
                    TS(ph, A2b[:, 0, :], LP[:, 4, 1, :], -1.0, None, ALU.mult, None, ["LP"], ["A2b"])
                    CP(ph, A2b[:, 1, :], LP[:, 4, 1, :], ["LP"], ["A2b"])
                    ph.emit()
                    checkpoint(f"C1{l}")

                with contextlib.ExitStack() as sC2:
                    def sc2(name, shape, dt=F32):
                        return sC2.enter_context(nc.sbuf_tensor(f"{name}_{l}", list(shape), dt))
                    VZ = sc2("VZ", [128, 2, 32, 384], BF16)
                    with contextlib.ExitStack() as sC2a:
                        hE = sC2a.enter_context(nc.sbuf_tensor(f"hE_{l}", [128, 2, 2, 32], F32))
                        pr = sC2a.enter_context(nc.sbuf_tensor(f"pr_{l}", [128, 16, 16], F32))
                        hst = sC2a.enter_context(nc.sbuf_tensor(f"hst_{l}", [32, 2, 128, 2], F32))
                        ph = Phase(nc, f"C2{l}")
                        for dg in range(32):
                            gp = dg % 16
                            for ri in range(2):
                                bank = (2 * dg + ri) % 4
                                MM(ph, ps[bank][:, 0:384], B4[:, dg, ri, :], U4[:, gp, :], True, True, [], [psk(bank)])
                                if ri == 0:
                                    CP(ph, VZ[:, ri, dg, :], ps[bank][:, 0:384], [psk(bank)], [])
                                else:
                                    ACT(ph, VZ[:, ri, dg, :], ps[bank][:, 0:384], AF.Copy, [psk(bank)], [])
                        for q in range(2):
                            for e_ in range(2):
                                for ri in range(2):
                                    for g2 in range(2):
                                        hs = slice(64 * g2, 64 * g2 + 64)
                                        uev = uedge[hs, 2 * q + e_, :].rearrange("p (gp g c) -> p gp g c", g=2, c=16)[:, :, g2, :]
                                        TT(ph, pr[hs], BB[hs, ri, 16 * e_:16 * e_ + 16, :], uev, ALU.mult, [], ["pr"])
                                        ph.op("dve", lambda e, hs=hs, q=q, ri=ri, e_=e_: e.tensor_reduce(
                                            out=hE[hs, q, ri, 16 * e_:16 * e_ + 16], in_=pr[hs], axis=AX.X, op=ALU.add),
                                            ["pr"], ["hE"])
                        for q in range(2):
                            for ri in range(2):
                                TR(ph, ps[4][0:32, 128 * ri:128 * ri + 128], hE[:, q, ri, :], ident[:], ["hE", "ident"], ["ps4"])
                            CP(ph, hst[:, q], ps[4][0:32, 0:256].rearrange("r (ri gp) -> r gp ri", ri=2), ["ps4"], ["hst"])
                            ph.dma("sp", nssm[q, l], hst[:, q].rearrange("r gp ri -> r (gp ri)"), reads=["hst"])
                        ph.emit()
                        checkpoint(f"C2{l}")
                    with contextlib.ExitStack() as sC3:
                        def sc3(name, shape, dt=F32):
                            return sC3.enter_context(nc.sbuf_tensor(f"{name}_{l}", list(shape), dt))
                        zS = [sc3(f"zS{i}", [128, 3, 32]) for i in range(3)]
                        tS1 = sc3("tS1", [128, 2, 32]); tS2 = sc3("tS2", [128, 2, 32])
                        zP = {(d, i): sc3(f"zP{d}{i}", [128, 3, 16, 2]) for d in range(2) for i in range(3)}
                        tP1 = [sc3(f"tP1{d}", [128, 2, 16, 2]) for d in range(2)]
                        tP2 = [sc3(f"tP2{d}", [128, 2, 16, 2]) for d in range(2)]
                        ph = Phase(nc, f"C3{l}")
                        CP(ph, zS[0][:, 0:2, :], H0[:], [], ["zS0"])
                        for d in range(2):
                            MS(ph, zP[(d, 0)][:], 0.0, [f"zP{d}0"], eng="pool")
                        vz0 = VZ[:, 0:1, 0:1, 0:1]
                        pstep = vz0.ap[0][0]

                        def vz_merged(jf, jb_):
                            return bass.AP(vz0.tensor, vz0.offset + jf,
                                           [[pstep, 128], [32 * 384, 2], [16 * 384 + (jb_ - jf), 2], [384, 16]])
                        NBS = 256
                        for i in range(NBS):
                            vj = vz_merged(128 + i, 128 + NBS - 1 - i)
                            zc = zS[i % 3]; zn = zS[(i + 1) % 3]
                            kc_ = f"zS{i % 3}"; kn_ = f"zS{(i + 1) % 3}"
                            TT(ph, tS1[:], zc[:, 0:2, :], A1b[:], ALU.mult, [kc_], ["tS1"])
                            zv = zc[:, 1:2, :]
                            zsw = bass.AP(zv.tensor, zv.offset, [[zv.ap[0][0], 128], [-32, 2], [1, 32]])
                            TT(ph, tS2[:], zsw, A2b[:], ALU.mult, [kc_], ["tS2"])
                            TT(ph, tS1[:], tS1[:], tS2[:], ALU.add, ["tS1", "tS2"], ["tS1"])
                            TT(ph, zn[:, 0:2, :].rearrange("p r (d g) -> p r d g", d=2),
                               tS1[:].rearrange("p r (d g) -> p r d g", d=2), vj, ALU.add,
                               ["tS1", ("vzS", i)], [kn_])
                            ACT(ph, vj, zc[:, 0:2, :].rearrange("p r (d g) -> p r d g", d=2), AF.Copy,
                                [kc_], [("vzS", i)])
                            if i < 64:
                                for d in range(2):
                                    dgs = slice(16 * d, 16 * d + 16)
                                    j = i if d == 0 else 63 - i
                                    pc = zP[(d, i % 3)]; pn = zP[(d, (i + 1) % 3)]
                                    pk = f"zP{d}{i % 3}"; pkn = f"zP{d}{(i + 1) % 3}"
                                    vp = VZ[:, :, dgs, j:j + 65:64]
                                    a1 = bc(A1b[:, :, dgs].unsqueeze(3), [128, 2, 16, 2])
                                    a2 = bc(A2b[:, :, dgs].unsqueeze(3), [128, 2, 16, 2])
                                    TT(ph, tP1[d][:], pc[:, 0:2], a1, ALU.mult, [pk], [f"tP1{d}"], eng="pool")
                                    TT(ph, tP2[d][:], pc[:, 1:3], a2, ALU.mult, [pk], [f"tP2{d}"], eng="pool")
                                    TT(ph, tP1[d][:], tP1[d][:], tP2[d][:], ALU.add, [f"tP1{d}", f"tP2{d}"],
                                       [f"tP1{d}"], eng="pool")
                                    TT(ph, pn[:, 0:2], tP1[d][:], vp, ALU.add, [f"tP1{d}", ("vzP", d, j)], [pkn],
                                       eng="pool")
                                    CP(ph, pn[:, 2], pn[:, 0], [pkn], [pkn], eng="pool")
                                    CP(ph, vp, pc[:, 0:2], [pk], [("vzP", d, j)], eng="pool")
                        ph.emit()
                        checkpoint(f"C3{l}")
                    with contextlib.ExitStack() as sC4:
                        def sc4(name, shape, dt=F32):
                            return sC4.enter_context(nc.sbuf_tensor(f"{name}_{l}", list(shape), dt))
                        zf = B4[:, 0:16].rearrange("p a r m -> p (a r m)").bitcast(F32).rearrange("p (c n) -> p c n", c=4)
                        zb = B4[:, 16:24].rearrange("p a r m -> p (a r m)").rearrange("p (c n) -> p c n", c=4)
                        wg = B4[:, 24:32].rearrange("p a r m -> p (a r m)").rearrange("p (c n) -> p c n", c=4)
                        sg = sc4("sg", [128, 512])
                        ph = Phase(nc, f"C4{l}")
                        for kc in range(4):
                            ph.dma("pool", wg[:, kc, :], w_glu[l, 128 * kc:128 * kc + 128, :], writes=["wg"])
                        for ct in range(3):
                            cols = slice(512 * ct, 512 * ct + 512)
                            for ch in range(4):
                                bank = ch
                                for g4 in range(4):
                                    gp = 4 * ch + g4
                                    if ct == 0:
                                        regions = [(t4, 128 * t4, 128, slice(0, 128)) for t4 in range(4)]
                                    else:
                                        regions = [(t4, 256 * (t4 % 2), 256, slice(128, 384)) for t4 in
                                                   (2 * (ct - 1), 2 * (ct - 1) + 1)]
                                    for (t4, off, N, ucols) in regions:
                                        outap = ps[bank][32 * g4:32 * g4 + 32, off:off + N]
                                        n = 0
                                        for d in range(2):
                                            dg = 16 * d + gp
                                            for (lw, rh) in ((T4[:, dg, 32 * t4:32 * t4 + 32], U4[:, gp, ucols]),
                                                             (C4[:, dg, 0, 32 * t4:32 * t4 + 32], VZ[:, 0, dg, ucols]),
                                                             (C4[:, dg, 1, 32 * t4:32 * t4 + 32], VZ[:, 1, dg, ucols])):
                                                MM(ph, outap, lw, rh, n == 0, n == 5, [], [psk(bank)], tp=(0, 32 * g4))
                                                n += 1
                                ACT(ph, zf[:, ch, :], ps[bank][:], AF.Gelu_apprx_tanh, [psk(bank)], [("zf", ch)])
                                CP(ph, zb[:, ch, :], zf[:, ch, :], [("zf", ch)], [("zb", ch)])
                            for m in range(4):
                                bank = 4 + m % 2
                                for kc in range(4):
                                    MM(ph, ps[bank][:], wg[:, kc, 128 * m:128 * m + 128], zb[:, kc, :], kc == 0, kc == 3,
                                       ["wg"] + [("zb", kc)], [psk(bank)])
                                ACT(ph, sg[:], ps[bank][:], AF.Sigmoid, [psk(bank)], ["sg"], bias=VT[:, l, 96 + m:97 + m])
                                TT(ph, hmod[:, 4 + m, cols], zf[:, m, :], sg[:], ALU.mult, [("zf", m), "sg"], [("mix2", m, ct)])
                        ph.emit()
                        checkpoint(f"C4{l}")
            sL.__exit__(None, None, None)

            with contextlib.ExitStack() as sD:
                wo = sD.enter_context(nc.sbuf_tensor(f"wo_{l}", [128, 8, 1024], BF16))
                ph = Phase(nc, f"D{l}")
                for kc in range(8):
                    ph.dma("pool", wo[:, kc, :], w_out[l, 128 * kc:128 * kc + 128, :], writes=["wo"])
                for ct in range(3):
                    cd = 0 if ct == 0 else 1
                    cols = slice(512 * ct, 512 * ct + 512)
                    for m in range(8):
                        bank = m % 4
                        for kc in range(8):
                            MM(ph, ps[bank][:], wo[:, kc, 128 * m:128 * m + 128], hmod[:, kc, cols], kc == 0, kc == 7,
                               ["wo"], [psk(bank)])
                        STT(ph, X[:, m, cols], ps[bank][:], modsb[:, l, 16 + m, cd:cd + 1], X[:, m, cols],
                            ALU.mult, ALU.add, [psk(bank)], [("X", m, ct)])
                ph.emit()
                checkpoint(f"D{l}")

            with contextlib.ExitStack() as sE:
                gact = sE.enter_context(nc.sbuf_tensor(f"gact_{l}", [128, 16, NCOL], BF16))
                with contextlib.ExitStack() as sE1:
                    def se1(name, shape, dt=F32):
                        return sE1.enter_context(nc.sbuf_tensor(f"{name}_{l}", list(shape), dt))
                    sq = [se1(f"sqE{i}", [128, 512], BF16) for i in range(2)]
                    tmp = [se1(f"tmpE{i}", [128, 512]) for i in range(2)]
                    lnv = se1("lnvE", [128, 512]); rstd = se1("rstdE", [128, 512])
                    wu = [se1(f"wu{i}", [128, 8, 1024], BF16) for i in range(2)]
                    zzs = [[se1(f"zz{a}{i}", [128, NCOL]) for i in range(2)] for a in range(2)]
                    ph = Phase(nc, f"E{l}")
                    norm_mod(ph, l, A2, 24, sq, lnv, rstd, tmp)
                    wupv = w_up[l].rearrange("(kc p) f -> p kc f", p=128)
                    for j in range(16):
                        qd, jj = j // 4, j % 4
                        wb = wu[qd % 2]
                        wk = f"wu{qd % 2}"
                        if jj == 0:
                            for qn in ([0, 1] if qd == 0 else ([qd + 1] if qd + 1 < 4 else [])):
                                wbn = wu[qn % 2]
                                wkn = f"wu{qn % 2}"
                                ph.dma("pool", wbn[:, :, 0:512], wupv[:, :, 512 * qn:512 * qn + 512],
                                       reads=[wkn + "r"], writes=[wkn])
                                ph.dma("pool", wbn[:, :, 512:1024], wupv[:, :, 2048 + 512 * qn:2048 + 512 * qn + 512],
                                       reads=[wkn + "r"], writes=[wkn])
                        zz = zzs[j % 2]
                        for hf in range(2):
                            f = j + 16 * hf
                            z = zz[hf]
                            zk = f"zz{j % 2}{hf}"
                            bks = [3 * hf + ct for ct in range(3)]
                            for ct in range(3):
                                for kc in range(8):
                                    MM(ph, ps[bks[ct]][:], wb[:, kc, 512 * hf + 128 * jj:512 * hf + 128 * jj + 128],
                                       hmod[:, kc, 512 * ct:512 * ct + 512], kc == 0, kc == 7,
                                       [wk, ("hmod", ct)],
                                       [psk(bks[ct])] + ([wk + "r"] if (jj == 3 and hf == 1 and ct == 2 and kc == 7) else []))
                            w0 = VTB[:, l, f:f + 1]; w1c = VTB[:, l, 32 + f:33 + f]; w2c = VTB[:, l, 64 + f:65 + f]
                            bcol = VT[:, l, 64 + f:65 + f]
                            RK = {n_: (zk, n_) for n_ in ("A", "B", "C", "D", "E1", "E2", "F")}
                            ctk = [[RK["A"], RK["B"], RK["C"]], [RK["D"], RK["E1"]], [RK["E2"], RK["F"]]]
                            for ct in range(3):
                                ACT(ph, z[:, 512 * ct:512 * ct + 512], ps[bks[ct]][:], AF.Identity, [psk(bks[ct])], ctk[ct],
                                    scale=w1c, bias=bcol)
                            pP, pS1, pS2 = ps[bks[0]], ps[bks[1]], ps[bks[2]]

                            def tap(dst, src, wcol, pbank, regs):
                                ks = [RK[r_] for r_ in regs]
                                STT(ph, dst, src, wcol, dst, ALU.mult, ALU.add, [psk(pbank)] + ks, ks)
                            tap(z[:, 128:512], pP[:, 0:384], w0, bks[0], ["B", "C"])
                            tap(z[:, 768:1280], pS1[:, 0:512], w0, bks[1], ["E1", "E2"])
                            tap(z[:, 0:128].rearrange("p (q j) -> p q j", q=2)[:, :, 1:64],
                                pP[:, 384:512].rearrange("p (q j) -> p q j", q=2)[:, :, 0:63], w0, bks[0], ["A"])
                            tap(z[:, 1280:1536], pS2[:, 0:256], w0, bks[2], ["F"])
                            tap(z[:, 513:768], pS2[:, 256:511], w0, bks[2], ["D"])
                            tap(z[:, 0:384], pP[:, 128:512], w2c, bks[0], ["A", "B"])
                            tap(z[:, 768:1280], pS2[:, 0:512], w2c, bks[2], ["E1", "E2"])
                            tap(z[:, 384:512].rearrange("p (q j) -> p q j", q=2)[:, :, 0:63],
                                pP[:, 0:128].rearrange("p (q j) -> p q j", q=2)[:, :, 1:64], w2c, bks[0], ["C"])
                            tap(z[:, 512:768], pS1[:, 256:512], w2c, bks[1], ["D"])
                            tap(z[:, 1280:1535], pS1[:, 1:256], w2c, bks[1], ["F"])
                        allk = lambda zk_: [(zk_, n_) for n_ in ("A", "B", "C", "D", "E1", "E2", "F")]
                        ACT(ph, zz[1][:], zz[1][:], AF.Silu, allk(f"zz{j % 2}1"), allk(f"zz{j % 2}1"))
                        TT(ph, gact[:, j, :], zz[0][:], zz[1][:], ALU.mult, allk(f"zz{j % 2}0") + allk(f"zz{j % 2}1"),
                           [("gact", j)], eng="pool")
                    for kc in range(16):
                        wbn = wu[kc // 8]
                        wkn = f"wu{kc // 8}"
                        ph.dma("pool", wbn[:, kc % 8, :], w_down[l, 128 * kc:128 * kc + 128, :],
                               reads=[wkn + "r"], writes=[("wd", kc)])
                    for ct in range(3):
                        cd = 0 if ct == 0 else 1
                        cols = slice(512 * ct, 512 * ct + 512)
                        for m in range(8):
                            bank = 6 + m % 2
                            for kc in range(16):
                                MM(ph, ps[bank][:], wu[kc // 8][:, kc % 8, 128 * m:128 * m + 128], gact[:, kc, cols],
                                   kc == 0, kc == 15, [("wd", kc), ("gact", kc)], [psk(bank)])
                            STT(ph, X[:, m, cols], ps[bank][:], modsb[:, l, 40 + m, cd:cd + 1], X[:, m, cols],
                                ALU.mult, ALU.add, [psk(bank)], [("X", m, ct)])
                    ph.emit()
                    checkpoint(f"E{l}")
                    checkpoint(f"F{l}")

        with contextlib.ExitStack() as sZ:
            ot = [sZ.enter_context(nc.sbuf_tensor(f"ot{i}", [128, 1024], F32)) for i in range(2)]
            ph = Phase(nc, "Z")
            for tt in range(12):
                o = ot[tt % 2]
                ok_ = f"ot{tt % 2}"
                for half in range(2):
                    bank = (2 * tt + half) % 4
                    for c4 in range(4):
                        c = 4 * half + c4
                        TR(ph, ps[bank][:, 128 * c4:128 * c4 + 128], X[:, c, 128 * tt:128 * tt + 128], ident[:], [],
                           [psk(bank)])
                    if half == 0:
                        CP(ph, o[:, 0:512], ps[bank][:], [psk(bank)], [ok_])
                    else:
                        ACT(ph, o[:, 512:1024], ps[bank][:], AF.Copy, [psk(bank)], [ok_])
                if tt < 4:
                    for q in range(2):
                        ph.dma("sp", yp[q].rearrange("(j s) d -> s j d", s=4)[tt], o[64 * q:64 * q + 64, :], reads=[ok_])
                else:
                    s4, hf = (tt - 4) // 2, (tt - 4) % 2
                    ph.dma("sp", ys.rearrange("(j s) d -> s j d", s=4)[s4, 128 * hf:128 * hf + 128, :], o[:, :], reads=[ok_])
            ph.emit()


_CACHE = {}


def _consts():
    ident = np.eye(128, dtype=np.float32)
    n = np.arange(1024)
    tok = 4 * (n % 256) + n // 256
    row = (tok // 64).astype(np.float64)
    colp = (tok % 64).astype(np.float64)
    p = np.arange(128)
    dd = p % 64
    axis = dd // 32
    half = (dd % 32) // 16
    f = dd % 16
    inv = 1.0 / (10000.0 ** (np.arange(8, dtype=np.float32) / np.float32(8)))
    inv16 = np.concatenate([inv, inv])
    nf = 16 // 2
    invf = (1.0 / (np.float32(10000.0) ** (np.arange(16, dtype=np.float32) / np.float32(16)))).astype(np.float32)
    pos = np.where(axis[:, None] == 0, row[None, :], colp[None, :]).astype(np.float32)
    ang = pos * invf[f][:, None]
    cosT = np.cos(ang).astype(np.float32)
    sinT = np.sin(ang).astype(np.float32)
    rsign = np.zeros((128, 128), np.float32)
    for m in range(128):
        if half[m] == 0:
            rsign[m + 16, m] = -1.0
        else:
            rsign[m - 16, m] = 1.0
    return ident, cosT, sinT, rsign


def make_inmaps(inp):
    f32 = np.float32
    g = {k: np.ascontiguousarray(np.asarray(v, dtype=f32)) for k, v in inp.items()}
    ident, cosT, sinT, rsign = _consts()
    va = np.concatenate([g["g_norm1"].reshape(2, 8, 128), g["g_norm2"].reshape(2, 8, 128),
                         g["b_mod"].reshape(2, 48, 128), g["conv_b"].reshape(2, 32, 128),
                         g["b_glu"].reshape(2, 4, 128)], axis=1)
    vb = g["conv_w"].reshape(2, 96, 128)
    qkn = np.stack([g["q_norm"], g["k_norm"]], axis=1)
    lamv = np.stack([g["lambda_q1"], g["lambda_k1"], g["lambda_q2"], g["lambda_k2"]], axis=1)
    shared = dict(w_mod=g["w_mod"], w_in=g["w_in"], w_out=g["w_out"], w_glu=g["w_glu"], w_up=g["w_up"],
                  w_down=g["w_down"], va=np.ascontiguousarray(va), vb=np.ascontiguousarray(vb),
                  qkn=np.ascontiguousarray(qkn), lamv=np.ascontiguousarray(lamv), subg=g["subln_g"],
                  lre=g["ssm_lambda_re"].reshape(2, 32, 128), lim=g["ssm_lambda_im"].reshape(2, 32, 128),
                  lstep=g["ssm_log_step"].reshape(2, 32, 2), bre=g["ssm_b_re"], bim=g["ssm_b_im"],
                  cre=g["ssm_c_re"], cim=g["ssm_c_im"], ssmd=g["ssm_d"],
                  ident=ident, cosT=cosT, sinT=sinT, rsign=rsign)
    in_maps = []
    for c in range(8):
        m = dict(shared)
        m["xp"] = np.ascontiguousarray(g["x_prompt"][2 * c:2 * c + 2])
        m["xs"] = np.ascontiguousarray(g["x_sample"][c])
        m["ck"] = np.ascontiguousarray(g["cache_k"][c].reshape(2, 512, 512))
        m["cv"] = np.ascontiguousarray(g["cache_v"][c].reshape(2, 512, 512))
        m["st"] = np.ascontiguousarray(g["state_ssm"][c].reshape(2, 32, 256))
        m["cvec"] = np.ascontiguousarray(np.stack([g["c_ctx"], g["c"][c]], axis=0).reshape(16, 128))
        in_maps.append(m)
    return in_maps


def kernel(**inp):
    f32 = np.float32
    if "nc" not in _CACHE:
        _CACHE["nc"] = build_program()
    nc = _CACHE["nc"]
    in_maps = make_inmaps(inp)
    res = run_bass_kernel_spmd(nc, in_maps, core_ids=list(range(8)))
    R = res.results
    y_prompt = np.concatenate([R[c]["yp"] for c in range(8)], axis=0).astype(f32)
    y_sample = np.stack([R[c]["ys"] for c in range(8)], axis=0).astype(f32)
    new_k = np.concatenate([R[c]["nk"] for c in range(8)], axis=0).reshape(16, 2, 256, 4, 2, 64).astype(f32)
    new_v = np.concatenate([R[c]["nv"] for c in range(8)], axis=0).reshape(16, 2, 256, 4, 128).astype(f32)
    new_ssm = np.concatenate([R[c]["nssm"] for c in range(8)], axis=0).reshape(16, 2, 2, 32, 64, 2).astype(f32)
    return (y_prompt, y_sample, new_k, new_v, new_ssm)
```

```python
import contextlib
import math
import numpy as np
import concourse.bass as bass
import concourse.mybir as mybir
from concourse.bass_utils import run_bass_kernel_spmd

F32 = mybir.dt.float32
BF16 = mybir.dt.bfloat16
AF = mybir.ActivationFunctionType
ALU = mybir.AluOpType
AX = mybir.AxisListType

ENGS = ("pe", "act", "dve", "pool", "sp")
STRICT_SAME = True
EPS = 1e-6
NCOL = 1536
DEPTH = 2


class Phase:
    def __init__(self, nc, name):
        self.nc = nc
        self.name = name
        self.ops = []
        self.last_w = {}
        self.readers = {}

    def _add(self, eng, fn, reads, writes, is_dma):
        def canon(k):
            return k[:3] if isinstance(k, str) and len(k) >= 3 and k.startswith("ps") and k[2].isdigit() else k
        reads = tuple(canon(k) for k in reads)
        writes = tuple(canon(k) for k in writes)
        writes = writes + tuple(k for k in reads if isinstance(k, str) and len(k) == 3 and k.startswith("ps")
                                and k[2].isdigit() and k not in writes)
        idx = len(self.ops)
        deps = set()
        raw = set()
        for k in reads:
            if k in self.last_w:
                deps.add(self.last_w[k])
                if not (isinstance(k, str) and len(k) == 3 and k.startswith("ps") and k[2].isdigit()
                        and self.ops[self.last_w[k]]["eng"] != "pe"):
                    raw.add(self.last_w[k])
        for k in writes:
            if k in self.last_w:
                deps.add(self.last_w[k])
            for r in self.readers.get(k, ()):
                deps.add(r)
        fd = set()
        for d in deps:
            o = self.ops[d]
            if o["is_dma"] or o["eng"] != eng or is_dma:
                fd.add(d)
            elif STRICT_SAME and eng != "pe" and d in raw:
                fd.add(d)
        for d in fd:
            self.ops[d]["signal"] = True
        import sys as _s
        self.ops.append(dict(eng=eng, fn=fn, deps=sorted(fd), is_dma=is_dma, signal=False,
                             tag=_s._getframe(3).f_lineno if _DBG.get("trunc") else None))
        for k in writes:
            self.last_w[k] = idx
            self.readers[k] = []
        for k in reads:
            if k not in writes:
                self.readers.setdefault(k, []).append(idx)
        return idx

    def op(self, eng, fn, reads=(), writes=()):
        return self._add(eng, fn, tuple(reads), tuple(writes), False)

    def dma(self, eng, out, in_, reads=(), writes=(), **kw):
        return self._add(eng, lambda e: e.dma_start(out=out, in_=in_, **kw), tuple(reads), tuple(writes), True)

    def emit(self):
        nc = self.nc
        ops = self.ops
        tr = _DBG.get("trunc")
        if tr is not None and tr[0] == self.name:
            print("TRUNC", self.name, "total ops", len(ops), "->", tr[1])
            for i_, o_ in enumerate(ops[:tr[1]][-3:]):
                print("   last ops:", o_["eng"], o_.get("tag"))
            ops = ops[:tr[1]]
            self.ops = ops
        ndma = sum(1 for o in ops if o["is_dma"])
        with contextlib.ExitStack() as st:
            G = getattr(nc, "_phase_sems", None)
            if G is None:
                G = dict(esem={e: nc.alloc_semaphore(name=f"g_{e}") for e in ENGS},
                         dsem=[nc.alloc_semaphore(name=f"g_d{i}") for i in range(88)],
                         ecount={e: 0 for e in ENGS}, dcount=[0] * 88, nxt=0)
                nc._phase_sems = G
            esem = G["esem"]; dsem = G["dsem"]; dcount = G["dcount"]; ecount = G["ecount"]
            NP = len(dsem)
            assert ndma <= NP, (self.name, ndma)
            used = set()
            for o in ops:
                if o["is_dma"]:
                    k = G["nxt"] % NP
                    G["nxt"] += 1
                    o["sem"] = k
                    dcount[k] += 16
                    o["val"] = dcount[k]
                    used.add(k)
                elif o["signal"]:
                    ecount[o["eng"]] += 1
                    o["val"] = ecount[o["eng"]]
            block = st.enter_context(nc.Block())

            def body(ename):
                def f(e):
                    waited = {}
                    for o in ops:
                        if o["eng"] != ename:
                            continue
                        for d in o["deps"]:
                            p = ops[d]
                            if p["is_dma"]:
                                key = ("d", p["sem"])
                                sem = dsem[p["sem"]]
                            else:
                                key = ("e", p["eng"])
                                sem = esem[p["eng"]]
                            if waited.get(key, 0) >= p["val"]:
                                continue
                            waited[key] = p["val"]
                            e.wait_ge(sem, p["val"])
                        ins = o["fn"](e)
                        if o["is_dma"]:
                            ins.then_inc(dsem[o["sem"]], 16)
                        elif o["signal"]:
                            ins.then_inc(esem[ename], 1)
                    if ename == "sp":
                        for i in sorted(used):
                            if waited.get(("d", i), 0) < dcount[i]:
                                e.wait_ge(dsem[i], dcount[i])
                return f

            block.tensor(body("pe"))
            block.scalar(body("act"))
            block.vector(body("dve"))
            block.gpsimd(body("pool"))
            block.sync(body("sp"))


def ACT(ph, out, in_, func, r, w, **kw):
    ph.op("act", lambda e: e.activation(out=out, in_=in_, func=func, **kw), r, w)


def TT(ph, out, in0, in1, op, r, w, eng="dve"):
    ph.op(eng, lambda e: e.tensor_tensor(out=out, in0=in0, in1=in1, op=op), r, w)


def STT(ph, out, in0, scalar, in1, op0, op1, r, w, eng="dve"):
    ph.op(eng, lambda e: e.scalar_tensor_tensor(out=out, in0=in0, scalar=scalar, in1=in1, op0=op0, op1=op1), r, w)


def TS(ph, out, in0, s1, s2, op0, op1, r, w, eng="dve"):
    if s2 is None:
        ph.op(eng, lambda e: e.tensor_scalar(out=out, in0=in0, scalar1=s1, scalar2=None, op0=op0), r, w)
    else:
        ph.op(eng, lambda e: e.tensor_scalar(out=out, in0=in0, scalar1=s1, scalar2=s2, op0=op0, op1=op1), r, w)


def CP(ph, out, in_, r, w, eng="dve"):
    ph.op(eng, lambda e: e.tensor_copy(out=out, in_=in_), r, w)


def MM(ph, out, lhsT, rhs, start, stop, r, w, tp=None):
    if tp is None:
        ph.op("pe", lambda e: e.matmul(out, lhsT, rhs, start=start, stop=stop), r, w)
    else:
        ph.op("pe", lambda e: e.matmul(out, lhsT, rhs, start=start, stop=stop, tile_position=tp), r, w)


def TR(ph, out, in_, ident, r, w):
    ph.op("pe", lambda e: e.transpose(out, in_, ident), r, w)


def MS(ph, ap, val, w, eng="dve"):
    ph.op(eng, lambda e: e.memset(ap, val), (), w)


def bc(ap, shape):
    return ap.broadcast_to(shape)


class _Stop(Exception):
    pass


_DBG = {"stop": None, "trunc": None}


def build_program():
    nc = bass.Bass("TRN2", target_bir_lowering=False)
    try:
        _build(nc)
    except _Stop:
        pass
    return nc


def _build(nc):

    def din(name, shape):
        return nc.dram_tensor(name, list(shape), F32, kind="ExternalInput").ap()

    def dout(name, shape):
        return nc.dram_tensor(name, list(shape), F32, kind="ExternalOutput").ap()

    xp = din("xp", [2, 256, 1024]); xs = din("xs", [1024, 1024])
    ck = din("ck", [2, 512, 512]); cv = din("cv", [2, 512, 512])
    stt_in = din("st", [2, 32, 256]); cvec = din("cvec", [16, 128])
    w_mod = din("w_mod", [2, 1024, 6144]); w_in = din("w_in", [2, 1024, 2048])
    w_out = din("w_out", [2, 1024, 1024]); w_glu = din("w_glu", [2, 512, 512])
    w_up = din("w_up", [2, 1024, 4096]); w_down = din("w_down", [2, 2048, 1024])
    va = din("va", [2, 100, 128]); vb = din("vb", [2, 96, 128])
    qkn = din("qkn", [2, 2, 64]); lamv = din("lamv", [2, 4, 64]); subg = din("subg", [2, 128])
    lre_d = din("lre", [2, 32, 128]); lim_d = din("lim", [2, 32, 128]); lst_d = din("lstep", [2, 32, 2])
    bre_d = din("bre", [2, 2, 32, 64, 16]); bim_d = din("bim", [2, 2, 32, 64, 16])
    cre_d = din("cre", [2, 2, 32, 16, 64]); cim_d = din("cim", [2, 2, 32, 16, 64])
    ssmd_d = din("ssmd", [2, 512])
    ident_d = din("ident", [128, 128]); cos_d = din("cosT", [128, 1024]); sin_d = din("sinT", [128, 1024])
    rsign_d = din("rsign", [128, 128])
    yp = dout("yp", [2, 256, 1024]); ys = dout("ys", [1024, 1024])
    nk = dout("nk", [2, 2, 256, 512]); nv = dout("nv", [2, 2, 256, 512]); nssm = dout("nssm", [2, 2, 32, 256])
    if _DBG["stop"] is not None:
        dbgX = dout("dbgX", [128, 8 * NCOL]); dbgH = dout("dbgH", [128, 8 * NCOL])
        dbgM = dout("dbgM", [128, 2 * 48 * 2])

    es = contextlib.ExitStack()
    with es:
        def sb(name, shape, dt=F32):
            return es.enter_context(nc.sbuf_tensor("s_" + name, list(shape), dt))

        ps = [es.enter_context(nc.psum_tensor(f"ps{i}", [128, 512], F32)) for i in range(8)]
        X = sb("X", [128, 8, NCOL])
        ident = sb("ident", [128, 128]); identb = sb("identb", [128, 128], BF16)
        onesD = sb("onesD", [128, 128], BF16); blk64 = sb("blk64", [128, 128], BF16)
        ones1 = sb("ones1", [128, 128], BF16); onesV = sb("onesV", [128, 128], BF16)
        ones32 = sb("ones32", [128, 128]); zerob = sb("zerob", [128, 128], BF16)
        ones32q = sb("ones32q", [128, 2, 128])
        epsc = sb("epsc", [128, 1]); negpi = sb("negpi", [128, 1])
        cosT = sb("cosT", [128, 1024]); sinT = sb("sinT", [128, 1024]); rsign = sb("rsign", [128, 128])
        VT = sb("VT", [128, 2, 100]); VTB = sb("VTB", [128, 2, 96])
        scond = sb("scond", [128, 8, 2]); scondb = sb("scondb", [128, 8, 2], BF16)
        modsb = sb("modsb", [128, 2, 48, 2])
        A1 = sb("A1", [128, 2, 8, 2]); A2 = sb("A2", [128, 2, 8, 2])
        gq = sb("gq", [128, 2, 2])
        gsub = sb("gsub", [128, 2])
        neglam = sb("neglam", [128, 2])
        Rq = sb("Rq", [128, 2, 2, 128], BF16)
        hmod = sb("hmod", [128, 8, NCOL], BF16)

        def psk(i):
            return f"ps{i}"

        def checkpoint(name):
            if _DBG["stop"] != name:
                return
            ph = Phase(nc, "dbg")
            ph.dma("sp", dbgX[:, :], X[:].rearrange("p c n -> p (c n)"))
            ph.dma("pool", dbgH[:, :], hmod[:].rearrange("p c n -> p (c n)"))
            ph.dma("sp", dbgM[:, :], modsb[:].rearrange("p l j c -> p (l j c)"))
            ph.emit()
            raise _Stop()

        def mod_half(ph, l, hb, wbuf, wkey, pbank):
            wmv = w_mod[l].rearrange("(kc p) f -> p kc f", p=128)
            ph.dma("pool", wbuf[:], wmv[:, :, 512 * hb:512 * hb + 512], reads=[wkey + "r"], writes=[wkey])

            def compute():
                for j4 in range(4):
                    for kc in range(8):
                        last = (j4 == 3 and kc == 7)
                        MM(ph, ps[pbank][:, 2 * j4:2 * j4 + 2], wbuf[:, kc, 128 * j4:128 * j4 + 128], scondb[:, kc, :],
                           kc == 0, kc == 7, [wkey], [psk(pbank), wkey + "r"] if last else [psk(pbank)])
                j0 = 4 * hb
                TT(ph, modsb[:, l, j0:j0 + 4, :], ps[pbank][:, 0:8].rearrange("p (j c) -> p j c", c=2),
                   bc(VT[:, l, 16 + j0:16 + j0 + 4].unsqueeze(2), [128, 4, 2]), ALU.add, [psk(pbank)], [("modsb", l, hb)])
            return compute

        def mod_A(ph, l, Aout, goff, scoff):
            rk = [("modsb", l, hb) for hb in range(12)]
            TS(ph, Aout[:, l], modsb[:, l, scoff:scoff + 8, :], 1.0, None, ALU.add, None, rk, [("A", l, goff)])
            TT(ph, Aout[:, l], Aout[:, l], bc(VT[:, l, goff:goff + 8].unsqueeze(2), [128, 8, 2]), ALU.mult,
               [("A", l, goff)], [("A", l, goff)])

        with contextlib.ExitStack() as s0:
            xtile = [s0.enter_context(nc.sbuf_tensor(f"xtile{i}", [128, 1024], F32)) for i in range(4)]
            wm = [s0.enter_context(nc.sbuf_tensor(f"wm{i}", [128, 8, 512], BF16)) for i in range(2)]
            vtmp = s0.enter_context(nc.sbuf_tensor("vtmp", [128, 128], F32))
            ctmp = s0.enter_context(nc.sbuf_tensor("ctmp", [128, 16], F32))
            lt = s0.enter_context(nc.sbuf_tensor("lt", [64, 2, 4], F32))
            lp = s0.enter_context(nc.sbuf_tensor("lp", [64, 2, 2], F32))
            le = s0.enter_context(nc.sbuf_tensor("le", [128, 2, 2], F32))
            ph = Phase(nc, "p0")
            ph.dma("sp", ident[:], ident_d[:, :], writes=["ident"])
            ph.dma("sp", cosT[:], cos_d[:, :], writes=["cos"])
            ph.dma("sp", sinT[:], sin_d[:, :], writes=["sin"])
            ph.dma("sp", rsign[:], rsign_d[:, :], writes=["rsign"])
            CP(ph, identb[:], ident[:], ["ident"], ["identb"])
            MS(ph, onesD[:], 1.0 / 1024.0, ["onesD"])
            MS(ph, ones1[:], 1.0, ["ones1"])
            MS(ph, onesV[:], 1.0 / 128.0, ["onesV"])
            MS(ph, ones32[:], 1.0, ["ones32"])
            MS(ph, ones32q[:], 0.0, ["ones32q"])
            MS(ph, ones32q[0:64, 0, :], 1.0, ["ones32q"])
            MS(ph, ones32q[64:128, 1, :], 1.0, ["ones32q"])
            MS(ph, zerob[:], 0.0, ["zerob"])
            MS(ph, blk64[:], 0.0, ["blk64"])
            MS(ph, blk64[0:64, 0:64], 1.0 / 64.0, ["blk64"])
            MS(ph, blk64[64:128, 64:128], 1.0 / 64.0, ["blk64"])
            MS(ph, epsc[:], EPS, ["epsc"])
            MS(ph, negpi[:], -math.pi, ["negpi"])
            for l in range(2):
                ph.dma("sp", vtmp[0:100, :], va[l], reads=["vtmp_r"], writes=["vtmp"])
                TR(ph, ps[0][:, 0:100], vtmp[0:100, :], ident[0:100, 0:100], ["vtmp", "ident"], ["ps0"])
                CP(ph, VT[:, l, :], ps[0][:, 0:100], ["ps0"], ["VT", "vtmp_r"])
                ph.dma("sp", vtmp[0:96, :], vb[l], reads=["vtmp_r"], writes=["vtmp"])
                TR(ph, ps[0][:, 0:96], vtmp[0:96, :], ident[0:96, 0:96], ["vtmp", "ident"], ["ps0"])
                CP(ph, VTB[:, l, :], ps[0][:, 0:96], ["ps0"], ["VTB", "vtmp_r"])
            ph.dma("sp", vtmp[0:16, :], cvec[:, :], reads=["vtmp_r"], writes=["vtmp"])
            TR(ph, ps[0][:, 0:16], vtmp[0:16, :], ident[0:16, 0:16], ["vtmp", "ident"], ["ps0"])
            ACT(ph, ctmp[:], ps[0][:, 0:16], AF.Silu, ["ps0"], ["ctmp", "vtmp_r"])
            CP(ph, scond[:], ctmp[:].rearrange("p (c k) -> p k c", c=2), ["ctmp"], ["scond"])
            CP(ph, scondb[:], scond[:], ["scond"], ["scondb"])
            for l in range(2):
                for j in range(2):
                    for half in range(2):
                        ph.dma("sp", gq[64 * half:64 * half + 64, l, j:j + 1],
                               qkn[l, j].rearrange("(p o) -> p o", o=1), writes=["gq"])
                ph.dma("sp", gsub[:, l:l + 1], subg[l].rearrange("(p o) -> p o", o=1), writes=["gsub"])
                ph.dma("sp", lt[:, l, :], lamv[l].rearrange("k p -> p k"), writes=["lt"],
                       allow_slow_non_contiguous=True)
            for l in range(2):
                lam_init = 0.8 - 0.6 * math.exp(-0.3 * l)
                TS(ph, gsub[:, l:l + 1], gsub[:, l:l + 1], 1.0 - lam_init, None, ALU.mult, None, ["gsub"], ["gsub"])
                TT(ph, lp[:, l, 0:1], lt[:, l, 0:1], lt[:, l, 1:2], ALU.mult, ["lt"], ["lp"])
                TT(ph, lp[:, l, 1:2], lt[:, l, 2:3], lt[:, l, 3:4], ALU.mult, ["lt"], ["lp"])
                MM(ph, ps[1][:, 0:2], ones32[0:64, :], lp[:, l, :], True, True, ["ones32", "lp"], ["ps1"])
                ACT(ph, le[:, l, :], ps[1][:, 0:2], AF.Exp, ["ps1"], ["le"])
                TT(ph, neglam[:, l:l + 1], le[:, l, 1:2], le[:, l, 0:1], ALU.subtract, ["le"], ["neglam"])
                TS(ph, neglam[:, l:l + 1], neglam[:, l:l + 1], -lam_init, None, ALU.add, None, ["neglam"], ["neglam"])
                for j in range(2):
                    TS(ph, Rq[:, l, j, :], rsign[:], gq[:, l, j:j + 1], None, ALU.mult, None, ["rsign", "gq"], ["Rq"])
            mod_pend0 = None
            for tt in range(12):
                if tt % 2 == 0:
                    hb_ = tt // 2
                    c_ = mod_half(ph, 0, hb_, wm[hb_ % 2], f"wm{hb_ % 2}", 6 + hb_ % 2)
                    if mod_pend0 is not None:
                        mod_pend0()
                    mod_pend0 = c_
                xt = xtile[tt % 4]
                xk = f"xt{tt % 4}"
                xq = "sp" if tt % 2 == 0 else "act"
                if tt < 4:
                    for q in range(2):
                        ph.dma(xq, xt[64 * q:64 * q + 64, :],
                               xp[q].rearrange("(j s) d -> s j d", s=4)[tt], reads=[xk + "r"], writes=[xk])
                else:
                    s4, hf = (tt - 4) // 2, (tt - 4) % 2
                    ph.dma(xq, xt[:, :], xs.rearrange("(j s) d -> s j d", s=4)[s4, 128 * hf:128 * hf + 128, :],
                           reads=[xk + "r"], writes=[xk])
                for half in range(2):
                    bank = 2 + (2 * tt + half) % 4
                    for c4 in range(4):
                        c = 4 * half + c4
                        TR(ph, ps[bank][:, 128 * c4:128 * c4 + 128], xt[:, 128 * c:128 * c + 128], ident[:],
                           [xk, "ident"], [psk(bank)])
                    dst = X[:, 4 * half:4 * half + 4, 128 * tt:128 * tt + 128]
                    src = ps[bank][:, :].rearrange("p (c t) -> p c t", c=4)
                    if half == 0:
                        CP(ph, dst, src, [psk(bank)], [("X", tt), xk + "r"])
                    else:
                        ACT(ph, dst, src, AF.Copy, [psk(bank)], [("X", tt), xk + "r"])
            mod_pend0()
            mod_A(ph, 0, A1, 0, 8)
            ph.emit()
            checkpoint("p0")

        MOD_REST = [(0, hb) for hb in range(6, 12)] + [(1, hb) for hb in range(12)]

        def norm_mod(ph, l, Atab, shoff, sq, lnv, rstd, tmp):
            for ct in range(3):
                cd = 0 if ct == 0 else 1
                cols = slice(512 * ct, 512 * ct + 512)
                for c in range(8):
                    ACT(ph, sq[c % 2][:], X[:, c, cols], AF.Square, [("X", ct)], [f"sq{c % 2}"])
                    MM(ph, ps[0][:], onesD[:], sq[c % 2][:], c == 0, c == 7, [f"sq{c % 2}"], ["ps0"])
                ACT(ph, lnv[:], ps[0][:], AF.Ln, ["ps0"], ["lnv"], bias=epsc[:])
                ACT(ph, rstd[:], lnv[:], AF.Exp, ["lnv"], ["rstd"], scale=-0.5)
                for c in range(8):
                    STT(ph, tmp[c % 2][:], X[:, c, cols], Atab[:, l, c, cd:cd + 1], rstd[:], ALU.mult, ALU.mult,
                        [("X", ct), "rstd"], [f"tmp{c % 2}"])
                    ACT(ph, hmod[:, c, cols], tmp[c % 2][:], AF.Identity, [f"tmp{c % 2}"], [("hmod", ct)],
                        bias=modsb[:, l, shoff + c, cd:cd + 1])

        for l in range(DEPTH):
            sL = contextlib.ExitStack()
            sL.__enter__()
            U4 = sL.enter_context(nc.sbuf_tensor(f"U4_{l}", [128, 16, 384], BF16))
            uedge = sL.enter_context(nc.sbuf_tensor(f"uedge_{l}", [128, 4, 512], F32))
            sA = contextlib.ExitStack()
            with sA:
                def sba(name, shape, dt=F32):
                    return sA.enter_context(nc.sbuf_tensor(f"{name}_{l}", list(shape), dt))

                qT = sba("qT", [128, 4, NCOL], BF16)
                kT = sba("kT", [128, 4, 2048], BF16)
                Vtok = sba("Vtok", [128, 16, 512], BF16)
                with contextlib.ExitStack() as sA1:
                    def sb1(name, shape, dt=F32):
                        return sA1.enter_context(nc.sbuf_tensor(f"{name}_{l}", list(shape), dt))
                    win = sb1("win", [128, 8, 2048], BF16)
                    sq = [sb1(f"sq{i}", [128, 512], BF16) for i in range(2)]
                    tmp = [sb1(f"tmp{i}", [128, 512]) for i in range(2)]
                    tmpB = [sb1(f"tmpB{i}", [128, 512]) for i in range(2)]
                    lnv = sb1("lnv", [128, 512]); rstd = sb1("rstd", [128, 512])
                    lrs = [lnv, rstd]; lrk = ["lnv", "rstd"]
                    qf = sb1("qf", [128, 512])
                    qbs = [sb1(f"qb{i}", [128, 512], BF16) for i in range(2)]
                    kst = sb1("kst", [128, 4, 128]); vst = qf
                    ckt = sb1("ckt", [128, 512])
                    hrep = tmpB[0][:].bitcast(BF16).rearrange("p (k n) -> p k n", k=8)
                    ph = Phase(nc, f"A{l}")
                    for kc in range(8):
                        ph.dma("pool", win[:, kc, :], w_in[l, 128 * kc:128 * kc + 128, :], writes=["win"])
                    for i in range(4):
                        ph.dma("pool", Vtok[:, 12 + i, :], cv[l, 128 * i:128 * i + 128, :], writes=[("V", 12 + i)])
                    norm_mod(ph, l, A1, 0, sq, lnv, rstd, tmp)
                    for i in range(4):
                        ph.dma("sp", ckt[:], ck[l, 128 * i:128 * i + 128, :], reads=["cktr"], writes=["ckt"])
                        for h in range(4):
                            TR(ph, ps[1][:, 128 * h:128 * h + 128], ckt[:, 128 * h:128 * h + 128], ident[:],
                               ["ckt", "ident"], ["ps1"])
                        CP(ph, kT[:, :, 1536 + 128 * i:1536 + 128 * i + 128],
                           ps[1][:, :].rearrange("p (h t) -> p h t", h=4), ["ps1"], [("kT", 3), "cktr"])
                    def qk_post(ct, mo, bank, sqt, sqk, par):
                        cols = slice(512 * ct, 512 * ct + 512)
                        isk = mo // 4
                        h = mo % 4

                        def run():
                            lr = lrs[par]; lk = lrk[par]
                            tA = tmp[par]; tAk = f"tmp{par}"
                            tB = tmpB[par]; tBk = f"tmpB{par}"
                            qb = qbs[par]; qbk = f"qb{par}"
                            MM(ph, ps[4][:], blk64[:], sqt[:], True, True, [sqk, "blk64"], ["ps4"])
                            ACT(ph, lr[:], ps[4][:], AF.Ln, ["ps4"], [lk], bias=epsc[:])
                            ACT(ph, lr[:], lr[:], AF.Exp, [lk], [lk], scale=-0.5)
                            gcol = gq[:, l, isk:isk + 1]
                            if ct == 0:
                                if isk == 0:
                                    for q in range(2):
                                        dst = qT[:, h, 0:512].rearrange("p (q s j) -> p q s j", q=2, s=4)[:, q]
                                        src = ps[bank][:, :].rearrange("p (s q j) -> p q s j", q=2, s=4)[:, q]
                                        rs = lr[:].rearrange("p (s q j) -> p q s j", q=2, s=4)[:, q]
                                        STT(ph, dst, src, gcol, rs, ALU.mult, ALU.mult, [psk(bank), lk], [("qT", 0)])
                                else:
                                    STT(ph, qf[:], ps[bank][:], gcol, lr[:], ALU.mult, ALU.mult,
                                        [psk(bank), lk], ["qf"])
                                    CP(ph, kT[:, h, 0:512], qf[:], ["qf"], [("kT", 0)])
                                    for s4 in range(4):
                                        TR(ph, ps[5][:, 128 * s4:128 * s4 + 128], qf[:, 128 * s4:128 * s4 + 128],
                                           ident[:], ["qf", "ident"], ["ps5"])
                                    ACT(ph, kst[:, :, :],
                                        ps[5][:, :].rearrange("p (s f) -> p s f", s=4), AF.Copy, ["ps5"], ["kst"])
                                    for s4 in range(4):
                                        for q in range(2):
                                            ph.dma("sp", nk[q, l].rearrange("(j s) f -> s j f", s=4)[s4, :, 128 * h:128 * h + 128],
                                                   kst[64 * q:64 * q + 64, s4, :], reads=["kst"])
                            else:
                                sc = slice(512 * (ct - 1), 512 * (ct - 1) + 512)
                                CP(ph, qb[:], ps[bank][:], [psk(bank)], [qbk])
                                MM(ph, ps[5][:], Rq[:, l, isk, :], qb[:], True, True, [qbk, "Rq"], ["ps5"])
                                STT(ph, tA[:], ps[bank][:], gcol, cosT[:, sc], ALU.mult, ALU.mult, [psk(bank), "cos"], [tAk])
                                TT(ph, tB[:], ps[5][:], sinT[:, sc], ALU.mult, ["ps5", "sin"], [tBk])
                                TT(ph, tA[:], tA[:], tB[:], ALU.add, [tAk, tBk], [tAk])
                                dst = (kT if isk else qT)[:, h, cols]
                                TT(ph, dst, tA[:], lr[:], ALU.mult, [tAk, lk], [("kT" if isk else "qT", ct)])
                        return run

                    pending = None
                    n_ = 0
                    for ct in range(3):
                        cols = slice(512 * ct, 512 * ct + 512)
                        for mo in range(8):
                            bank = 2 + n_ % 2
                            sqt = sq[n_ % 2]
                            sqk = f"sq{n_ % 2}"
                            for kc in range(8):
                                MM(ph, ps[bank][:], win[:, kc, 128 * mo:128 * mo + 128], hmod[:, kc, cols],
                                   kc == 0, kc == 7, ["win", ("hmod", ct)], [psk(bank)])
                            ACT(ph, sqt[:], ps[bank][:], AF.Square, [psk(bank)], [sqk])
                            if pending is not None:
                                pending()
                            pending = qk_post(ct, mo, bank, sqt, sqk, n_ % 2)
                            n_ += 1
                    pending()
                    for tt in range(12):
                        bank = 2 + tt % 2
                        for kc in range(8):
                            MM(ph, ps[bank][:], hmod[:, kc, 128 * tt:128 * tt + 128], win[:, kc, 1024:1536],
                               kc == 0, kc == 7, ["win", ("hmod", tt // 4)], [psk(bank)])
                        CP(ph, Vtok[:, tt, :], ps[bank][:], [psk(bank)], [("V", tt)])
                        if tt < 4:
                            ACT(ph, vst[:], ps[bank][:], AF.Copy, [psk(bank)], ["qf"])
                            for q in range(2):
                                ph.dma("sp", nv[q, l].rearrange("(j s) f -> s j f", s=4)[tt],
                                       vst[64 * q:64 * q + 64, :], reads=["qf"])
                    for gp in range(16):
                        bank = 6 + gp % 2
                        for kc in range(8):
                            lw = win[:, kc, 1536 + 32 * gp:1536 + 32 * gp + 32]
                            for s4 in range(4):
                                MM(ph, ps[bank][32 * s4:32 * s4 + 32, 0:128], lw, hmod[:, kc, 128 * s4:128 * s4 + 128],
                                   kc == 0, kc == 7, ["win", ("hmod", 0)], [psk(bank)], tp=(0, 32 * s4))
                        for kc in range(8):
                            lw = win[:, kc, 1536 + 32 * gp:1536 + 32 * gp + 32]
                            for s4 in range(4):
                                MM(ph, ps[bank][32 * s4:32 * s4 + 32, 128:384], lw,
                                   hmod[:, kc, 512 + 256 * s4:512 + 256 * s4 + 256],
                                   kc == 0, kc == 7, ["win", ("hmod", 1), ("hmod", 2)], [psk(bank)], tp=(0, 32 * s4))
                        if gp % 2 == 0:
                            CP(ph, U4[:, gp, :], ps[bank][:, 0:384], [psk(bank)], ["U4"])
                        else:
                            ACT(ph, U4[:, gp, :], ps[bank][:, 0:384], AF.Copy, [psk(bank)], ["U4"])
                    for q in range(2):
                        for e_ in range(2):
                            col = 64 * q if e_ == 0 else 384 + 64 * q + 63
                            CP(ph, hrep, bc(hmod[:, :, col:col + 1], [128, 8, 128]), [("hmod", 0)], ["tmpB0"])
                            for kc in range(8):
                                MM(ph, ps[4][:], hrep[:, kc, :], win[:, kc, 1536:2048], kc == 0, kc == 7,
                                   ["tmpB0", "win"], ["ps4"])
                            CP(ph, uedge[:, 2 * q + e_, :], ps[4][:], ["ps4"], ["uedge"])
                    ph.emit()
                    checkpoint(f"A{l}")

                with contextlib.ExitStack() as sB:
                    def sbb(name, shape, dt=F32):
                        return sB.enter_context(nc.sbuf_tensor(f"{name}_{l}", list(shape), dt))
                    Et = [sbb(f"E{i}", [128, 512], BF16) for i in range(8)]
                    Esum = [sbb(f"Esum{i}", [128, 512]) for i in range(2)]
                    r0 = sbb("r0", [128, 512]); r1 = sbb("r1", [128, 512])
                    Osb = [sbb(f"Osb{i}", [128, 512]) for i in range(4)]
                    if l == 0:
                        wmB = sbb("wmB", [128, 8, 512], BF16)
                    oa = sbb("oa", [128, 512]); ob = sbb("ob", [128, 512])
                    sqb = sbb("sqb", [128, 512], BF16)
                    lnv = sbb("lnvB", [128, 512]); rstd = sbb("rstdB", [128, 512])
                    ph = Phase(nc, f"B{l}")
                    ecnt = [0]
                    BRE_v = hmod[:, 4, 0:1024].bitcast(F32).rearrange("p (a c) -> p a c", c=16)
                    BIM_v = hmod[:, 5, 0:1024].bitcast(F32).rearrange("p (a c) -> p a c", c=16)
                    ph.dma("sp", BRE_v, bre_d[l].rearrange("d (gp g2) p c -> (g2 p) (d gp) c", g2=2), writes=["BREv"])
                    ph.dma("act", BIM_v, bim_d[l].rearrange("d (gp g2) p c -> (g2 p) (d gp) c", g2=2), writes=["BIMv"])

                    def attn_unit(h, qcols, N, keychunks, outdst, outkey):
                        nkc = len(keychunks)
                        tiles = [(ci, s) for ci in range(nkc) for s in range(2)]
                        slots = {}

                        def issue_score(t):
                            ci, s = tiles[t]
                            kcol = keychunks[ci][0]
                            sbank = (0, 1, 2, 5, 7)[ecnt[0] % 5]
                            ei = ecnt[0] % 8
                            ecnt[0] += 1
                            slots[t] = ei
                            MM(ph, ps[sbank][:, 0:N], kT[64 * s:64 * s + 64, h, kcol:kcol + 128],
                               qT[64 * s:64 * s + 64, h, qcols:qcols + N], True, True, [], [psk(sbank)])
                            ACT(ph, Et[ei][:, 0:N], ps[sbank][:, 0:N], AF.Exp, [psk(sbank)], [f"E{ei}"], scale=0.125)

                        def issue_av(t):
                            ci, s = tiles[t]
                            (kcol, vt, plo, phi) = keychunks[ci]
                            ei = slots[t]
                            MM(ph, ps[4 + 2 * s][:, 0:N], Vtok[plo:phi, vt, 128 * h:128 * h + 128],
                               Et[ei][plo:phi, 0:N], ci == 0, ci == nkc - 1, [f"E{ei}"], [psk(4 + 2 * s)])
                            if ci == 0:
                                CP(ph, Esum[s][:, 0:N], Et[ei][:, 0:N], [f"E{ei}"], [f"Esum{s}"])
                            else:
                                TT(ph, Esum[s][:, 0:N], Esum[s][:, 0:N], Et[ei][:, 0:N], ALU.add,
                                   [f"E{ei}", f"Esum{s}"], [f"Esum{s}"])
                        LOOK = 4
                        for t in range(len(tiles) + LOOK):
                            if t < len(tiles):
                                issue_score(t)
                            if t >= LOOK:
                                issue_av(t - LOOK)
                        def part1():
                          ACT(ph, Osb[0][:, 0:N], ps[4][:, 0:N], AF.Copy, ["ps4"], ["Os0"])
                          ACT(ph, Osb[2][:, 0:N], ps[6][:, 0:N], AF.Copy, ["ps6"], ["Os2"])
                          plo_, phi_ = keychunks[0][2], keychunks[0][3]
                          lw = ones32[:] if phi_ - plo_ == 128 else ones32q[:, plo_ // 64, :]
                          for s_, rr in ((0, r0), (1, r1)):
                              MM(ph, ps[3][:, 0:N], lw, Esum[s_][:, 0:N], True, True, [f"Esum{s_}"], ["ps3"])
                              ACT(ph, rr[:, 0:N], ps[3][:, 0:N], AF.Ln, ["ps3"], [f"r{s_}"])
                              ACT(ph, rr[:, 0:N], rr[:, 0:N], AF.Exp, [f"r{s_}"], [f"r{s_}"], scale=-1.0)
                          TT(ph, oa[:, 0:N], Osb[0][:, 0:N], r0[:, 0:N], ALU.mult, ["Os0", "r0"], ["oa"])
                          TT(ph, ob[:, 0:N], Osb[2][:, 0:N], r1[:, 0:N], ALU.mult, ["Os2", "r1"], ["ob"])
                          STT(ph, oa[:, 0:N], ob[:, 0:N], neglam[:, l:l + 1], oa[:, 0:N], ALU.mult, ALU.add,
                              ["oa", "ob"], ["oa"])

                        def tail():
                            ACT(ph, sqb[:, 0:N], oa[:, 0:N], AF.Square, ["oa"], ["sqb"])
                            MM(ph, ps[3][:, 0:N], onesV[:], sqb[:, 0:N], True, True, ["sqb"], ["ps3"])
                            ACT(ph, lnv[:, 0:N], ps[3][:, 0:N], AF.Ln, ["ps3"], ["lnv"], bias=epsc[:])
                            ACT(ph, rstd[:, 0:N], lnv[:, 0:N], AF.Exp, ["lnv"], ["rstd"], scale=-0.5)
                            src = oa[:, 0:N]
                            rs = rstd[:, 0:N]
                            if N == 256:
                                src = src.rearrange("p (s j) -> p s j", s=4)
                                rs = rs.rearrange("p (s j) -> p s j", s=4)
                            STT(ph, outdst, src, gsub[:, l:l + 1], rs, ALU.mult, ALU.mult, ["oa", "rstd"], [outkey])
                        return part1, tail

                    units = []
                    for h in range(4):
                        for q in range(2):
                            kch = [(128 * s4, s4, 64 * q, 64 * q + 64) for s4 in range(4)]
                            dst = hmod[:, h, 0:512].rearrange("p (s q j) -> p q s j", s=4, q=2)[:, q]
                            units.append((h, 256 * q, 256, kch, dst, ("mix", h, q)))
                        for qt in range(2):
                            kch = [(512 + 128 * i, 4 + i, 0, 128) for i in range(8)] + \
                                  [(1536 + 128 * i, 12 + i, 0, 128) for i in range(4)]
                            dst = hmod[:, h, 512 + 512 * qt:512 + 512 * qt + 512]
                            units.append((h, 512 + 512 * qt, 512, kch, dst, ("mix", h, 2 + qt)))
                    pending = None
                    mod_todo = list(MOD_REST) if l == 0 else []
                    mod_pend = None
                    for ui, u_ in enumerate(units):
                        p1_, t_ = attn_unit(*u_)
                        if pending is not None:
                            pending()
                        p1_()
                        pending = t_
                        for _ in range(2 if ui < 2 else 1):
                            if mod_pend is not None:
                                mod_pend()
                                mod_pend = None
                            if mod_todo:
                                ml_, mhb_ = mod_todo.pop(0)
                                mod_pend = mod_half(ph, ml_, mhb_, wmB, "wmB", 3)
                    pending()
                    if l == 0:
                        if mod_pend is not None:
                            mod_pend()
                        while mod_todo:
                            ml_, mhb_ = mod_todo.pop(0)
                            mod_half(ph, ml_, mhb_, wmB, "wmB", 3)()
                        mod_A(ph, 0, A2, 8, 32)
                        mod_A(ph, 1, A1, 0, 8)
                        mod_A(ph, 1, A2, 8, 32)
                    ph.emit()
                    checkpoint(f"B{l}")
            sC = contextlib.ExitStack()
            with sC:
                def sbc(name, shape, dt=F32):
                    return sC.enter_context(nc.sbuf_tensor(f"{name}_{l}", list(shape), dt))
                T4 = sbc("T4", [128, 32, 128], BF16)
                B4 = sbc("B4", [128, 32, 2, 128], BF16)
                C4 = sbc("C4", [128, 32, 2, 128], BF16)
                BB = sbc("BB", [128, 2, 32, 16])
                LP = sbc("LP", [128, 5, 2, 32])
                A1b = sbc("A1b", [128, 2, 32]); A2b = sbc("A2b", [128, 2, 32])
                H0 = sbc("H0", [128, 2, 32])
                with contextlib.ExitStack() as sC1:
                    def sc1(name, shape, dt=F32):
                        return sC1.enter_context(nc.sbuf_tensor(f"{name}_{l}", list(shape), dt))
                    BP = sc1("BP", [128, 32, 4, 2, 32], BF16)
                    CPd = sc1("CPd", [128, 32, 2, 32], BF16)
                    rowt = sc1("rowt", [32, 128]); rowt2 = sc1("rowt2", [32, 128]); rowt3 = sc1("rowt3", [32, 128])
                    ls2 = sc1("ls2", [32, 2]); h0t = sc1("h0t", [32, 256]); h0r = sc1("h0r", [32, 2, 128])
                    LRE = sc1("LRE", [128, 32]); LIM = sc1("LIM", [128, 32]); DEL = sc1("DEL", [128, 32])
                    ta = sc1("ta", [128, 32]); tb = sc1("tb", [128, 32]); tcc = sc1("tcc", [128, 32])
                    ea = sc1("ea", [128, 32]); sinb = sc1("sinb", [128, 32]); cosb = sc1("cosb", [128, 32])
                    cr = sc1("cr", [128, 32]); ci = sc1("ci", [128, 32])
                    BRE = hmod[:, 4, 0:1024].bitcast(F32).rearrange("p (a c) -> p a c", c=16)
                    BIM = hmod[:, 5, 0:1024].bitcast(F32).rearrange("p (a c) -> p a c", c=16)
                    CRE = sc1("CRE", [128, 32, 16]); CIM = sc1("CIM", [128, 32, 16])
                    rowsC = sc1("rowsC", [32, 2048])
                    w1 = sc1("w1", [128, 32, 16]); w2 = sc1("w2", [128, 32, 16]); w3 = sc1("w3", [128, 32, 16])
                    dvec = sc1("dvec", [128, 16])
                    v1 = sc1("v1", [128, 16, 16]); v2 = sc1("v2", [128, 16, 16]); v3 = sc1("v3", [128, 16, 16])
                    ph = Phase(nc, f"C1{l}")
                    MS(ph, BP[:], 0.0, ["BP"]); MS(ph, CPd[:], 0.0, ["CPd"]); MS(ph, C4[:], 0.0, ["C4"])
                    ph.dma("sp", rowt[:], lre_d[l], writes=["rowt"])
                    ph.dma("sp", rowt2[:], lim_d[l], writes=["rowt2"])
                    ph.dma("sp", ls2[:], lst_d[l], writes=["ls2"])
                    for j in range(4):
                        ph.dma("sp", dvec[32 * j:32 * j + 32, :], ssmd_d[l].rearrange("(gp r) -> r gp", r=32),
                               writes=["dvec"], allow_slow_non_contiguous=True)
                    ph.dma("sp", h0t[:], stt_in[l], writes=["h0t"])
                    TR(ph, ps[0][:, 0:32], rowt[:], ident[0:32, 0:32], ["rowt", "ident"], ["ps0"])
                    CP(ph, LRE[:], ps[0][:, 0:32], ["ps0"], ["LRE"])
                    TR(ph, ps[0][:, 32:64], rowt2[:], ident[0:32, 0:32], ["rowt2", "ident"], ["ps0b"])
                    CP(ph, LIM[:], ps[0][:, 32:64], ["ps0b"], ["LIM"])
                    CP(ph, rowt3[:].rearrange("r (g p) -> r g p", g=2), bc(ls2[:].unsqueeze(2), [32, 2, 64]),
                       ["ls2"], ["rowt3"])
                    TR(ph, ps[0][:, 64:96], rowt3[:], ident[0:32, 0:32], ["rowt3", "ident"], ["ps0c"])
                    ACT(ph, DEL[:], ps[0][:, 64:96], AF.Exp, ["ps0c"], ["DEL"])
                    CP(ph, h0r[:], h0t[:].rearrange("r (gp ri) -> r ri gp", ri=2), ["h0t"], ["h0r"])
                    for ri in range(2):
                        TR(ph, ps[1][:, 32 * ri:32 * ri + 32], h0r[:, ri, :], ident[0:32, 0:32], ["h0r", "ident"], ["ps1"])
                    CP(ph, H0[:], ps[1][:, 0:64].rearrange("p (r d) -> p r d", r=2), ["ps1"], ["H0"])
                    TT(ph, ta[:], LRE[:], DEL[:], ALU.mult, ["LRE", "DEL"], ["ta"])
                    TT(ph, tb[:], LIM[:], DEL[:], ALU.mult, ["LIM", "DEL"], ["tb"])
                    ACT(ph, ea[:], ta[:], AF.Exp, ["ta"], ["ea"])
                    def range_reduce(add):
                        TS(ph, tcc[:], tb[:], add, None, ALU.add, None, ["tb", "sinb"], ["tcc"])
                        for _ in range(4):
                            TS(ph, cr[:], tcc[:], 2 * math.pi, None, ALU.is_ge, None, ["tcc"], ["cr"])
                            STT(ph, tcc[:], cr[:], -2 * math.pi, tcc[:], ALU.mult, ALU.add, ["cr", "tcc"], ["tcc"])
                    range_reduce(math.pi)
                    ACT(ph, sinb[:], tcc[:], AF.Sin, ["tcc"], ["sinb"], bias=negpi[:])
                    range_reduce(1.5 * math.pi)
                    ACT(ph, cosb[:], tcc[:], AF.Sin, ["tcc"], ["cosb"], bias=negpi[:])
                    MS(ph, LP[:, 0, 0, :], 1.0, ["LP"]); MS(ph, LP[:, 0, 1, :], 0.0, ["LP"])
                    TT(ph, LP[:, 1, 0, :], ea[:], cosb[:], ALU.mult, ["ea", "cosb"], ["LP"])
                    TT(ph, LP[:, 1, 1, :], ea[:], sinb[:], ALU.mult, ["ea", "sinb"], ["LP"])
                    for k in range(2, 5):
                        TT(ph, ta[:], LP[:, k - 1, 0, :], LP[:, 1, 0, :], ALU.mult, ["LP"], ["ta"])
                        TT(ph, tb[:], LP[:, k - 1, 1, :], LP[:, 1, 1, :], ALU.mult, ["LP"], ["tb"])
                        TT(ph, LP[:, k, 0, :], ta[:], tb[:], ALU.subtract, ["ta", "tb"], ["LP"])
                        TT(ph, ta[:], LP[:, k - 1, 0, :], LP[:, 1, 1, :], ALU.mult, ["LP"], ["ta"])
                        TT(ph, tb[:], LP[:, k - 1, 1, :], LP[:, 1, 0, :], ALU.mult, ["LP"], ["tb"])
                        TT(ph, LP[:, k, 1, :], ta[:], tb[:], ALU.add, ["ta", "tb"], ["LP"])
                    TT(ph, ta[:], LRE[:], LRE[:], ALU.mult, ["LRE"], ["ta"])
                    TT(ph, tb[:], LIM[:], LIM[:], ALU.mult, ["LIM"], ["tb"])
                    TT(ph, ta[:], ta[:], tb[:], ALU.add, ["ta", "tb"], ["ta"])
                    ph.op("dve", lambda e: e.reciprocal(out=tcc[:], in_=ta[:]), ["ta"], ["tcc"])
                    TS(ph, ea[:], LP[:, 1, 0, :], -1.0, None, ALU.add, None, ["LP"], ["ea"])
                    TT(ph, ta[:], ea[:], LRE[:], ALU.mult, ["ea", "LRE"], ["ta"])
                    TT(ph, tb[:], LP[:, 1, 1, :], LIM[:], ALU.mult, ["LP", "LIM"], ["tb"])
                    TT(ph, ta[:], ta[:], tb[:], ALU.add, ["ta", "tb"], ["ta"])
                    TT(ph, cr[:], ta[:], tcc[:], ALU.mult, ["ta", "tcc"], ["cr"])
                    TT(ph, ta[:], LP[:, 1, 1, :], LRE[:], ALU.mult, ["LP", "LRE"], ["ta"])
                    TT(ph, tb[:], ea[:], LIM[:], ALU.mult, ["ea", "LIM"], ["tb"])
                    TT(ph, ta[:], ta[:], tb[:], ALU.subtract, ["ta", "tb"], ["ta"])
                    TT(ph, ci[:], ta[:], tcc[:], ALU.mult, ["ta", "tcc"], ["ci"])
                    crb = bc(cr[:].unsqueeze(2), [128, 32, 16]); cib = bc(ci[:].unsqueeze(2), [128, 32, 16])
                    TT(ph, w1[:], BRE[:], crb, ALU.mult, ["BRE", "cr"], ["w1"])
                    TT(ph, w2[:], BIM[:], cib, ALU.mult, ["BIM", "ci"], ["w2"])
                    TT(ph, BB[:, 0], w1[:], w2[:], ALU.subtract, ["w1", "w2"], ["BB"])
                    TT(ph, w1[:], BIM[:], crb, ALU.mult, ["BIM", "cr"], ["w1"])
                    TT(ph, w2[:], BRE[:], cib, ALU.mult, ["BRE", "ci"], ["w2"])
                    TT(ph, BB[:, 1], w1[:], w2[:], ALU.add, ["w1", "w2"], ["BB"])
                    for ti_, (src_d, dstC, nm) in enumerate(((cre_d, CRE, "CRE"), (cim_d, CIM, "CIM"))):
                        srcv = src_d[l].rearrange("d (gp g2) c p -> (d gp) c g2 p", g2=2)
                        for g2 in range(2):
                            ph.dma("sp" if g2 == 0 else "act",
                                   rowsC[:].rearrange("r (c g p) -> r c g p", c=16, g=2)[:, :, g2, :], srcv[:, :, g2, :],
                                   reads=["rowsCr"], writes=["rowsC"])
                        for c_ in range(16):
                            TR(ph, ps[2][:, 32 * c_:32 * c_ + 32], rowsC[:, 128 * c_:128 * c_ + 128], ident[0:32, 0:32],
                               ["rowsC", "ident"], ["ps2"])
                        CP(ph, dstC[:], ps[2][:, :].rearrange("p (c a) -> p a c", c=16), ["ps2"], [nm, "rowsCr"])

                    def padcopy(dst_of_half, src, rkeys, wkey):
                        for g2 in range(2):
                            CP(ph, dst_of_half(g2), src[64 * g2:64 * g2 + 64], rkeys, [wkey])

                    for k in range(4):
                        if k == 0:
                            res = (BB[:, 0], BB[:, 1])
                            rk = ["BB"]
                        else:
                            lr = bc(LP[:, k, 0, :].unsqueeze(2), [128, 32, 16])
                            li = bc(LP[:, k, 1, :].unsqueeze(2), [128, 32, 16])
                            TT(ph, w1[:], BB[:, 0], lr, ALU.mult, ["BB", "LP"], ["w1"])
                            TT(ph, w2[:], BB[:, 1], li, ALU.mult, ["BB", "LP"], ["w2"])
                            TT(ph, w1[:], w1[:], w2[:], ALU.subtract, ["w1", "w2"], ["w1"])
                            TT(ph, w2[:], BB[:, 1], lr, ALU.mult, ["BB", "LP"], ["w2"])
                            TT(ph, w3[:], BB[:, 0], li, ALU.mult, ["BB", "LP"], ["w3"])
                            TT(ph, w2[:], w2[:], w3[:], ALU.add, ["w2", "w3"], ["w2"])
                            res = (w1[:], w2[:])
                            rk = ["w1", "w2"]
                        for ri in range(2):
                            padcopy(lambda g2, k=k, ri=ri: BP[64 * g2:64 * g2 + 64, :, k, ri, 16 * g2:16 * g2 + 16],
                                    res[ri], rk, "BP")
                    padcopy(lambda g2: CPd[64 * g2:64 * g2 + 64, :, 0, 16 * g2:16 * g2 + 16], CRE[:], ["CRE"], "CPd")
                    TS(ph, w3[:], CIM[:], -1.0, None, ALU.mult, None, ["CIM"], ["w3"])
                    padcopy(lambda g2: CPd[64 * g2:64 * g2 + 64, :, 1, 16 * g2:16 * g2 + 16], w3[:], ["w3"], "CPd")
                    for d in range(2):
                        dgs = slice(16 * d, 16 * d + 16)
                        for t4 in range(4):
                            k = t4 + 1 if d == 0 else 4 - t4
                            lr = bc(LP[:, k, 0, dgs].unsqueeze(2), [128, 16, 16])
                            li = bc(LP[:, k, 1, dgs].unsqueeze(2), [128, 16, 16])
                            TT(ph, v1[:, 0:16], CRE[:, dgs], lr, ALU.mult, ["CRE", "LP"], ["v1"])
                            TT(ph, v2[:, 0:16], CIM[:, dgs], li, ALU.mult, ["CIM", "LP"], ["v2"])
                            TT(ph, v1[:, 0:16], v1[:, 0:16], v2[:, 0:16], ALU.subtract, ["v1", "v2"], ["v1"])
                            TT(ph, v2[:, 0:16], CRE[:, dgs], li, ALU.mult, ["CRE", "LP"], ["v2"])
                            TT(ph, v3[:, 0:16], CIM[:, dgs], lr, ALU.mult, ["CIM", "LP"], ["v3"])
                            TT(ph, v2[:, 0:16], v2[:, 0:16], v3[:, 0:16], ALU.add, ["v2", "v3"], ["v2"])
                            TS(ph, v2[:, 0:16], v2[:, 0:16], -1.0, None, ALU.mult, None, ["v2"], ["v2"])
                            for ri, srcw in enumerate((v1, v2)):
                                for g2 in range(2):
                                    CP(ph, C4[64 * g2:64 * g2 + 64, dgs, ri, 32 * t4 + 16 * g2:32 * t4 + 16 * g2 + 16],
                                       srcw[64 * g2:64 * g2 + 64, 0:16], ["v1", "v2"], ["C4"])
                    for dg in range(32):
                        d = dg // 16
                        gp = dg % 16
                        bank = 4 + dg % 2
                        for ri in range(2):
                            for s4 in range(4):
                                k = 3 - s4 if d == 0 else s4
                                MM(ph, ps[bank][32 * s4:32 * s4 + 32, 128 * ri:128 * ri + 128], BP[:, dg, k, ri, :],
                                   identb[:], True, True, ["BP", "identb"], [psk(bank)], tp=(0, 32 * s4))
                        MM(ph, ps[bank][:, 256:384], zerob[:], identb[:], True, True, ["zerob", "identb"], [psk(bank)])
                        for s4 in range(4):
                            for t4 in range(4):
                                tau = t4 - s4 if d == 0 else s4 - t4
                                if tau < 0:
                                    continue
                                for ri in range(2):
                                    ph.op("pe", lambda e, bank=bank, s4=s4, t4=t4, dg=dg, tau=tau, ri=ri: e.matmul(
                                        ps[bank][32 * s4:32 * s4 + 32, 256 + 32 * t4:256 + 32 * t4 + 32],
                                        BP[:, dg, tau, ri, :], CPd[:, dg, ri, :], start=False, stop=True,
                                        tile_position=(0, 32 * s4), skip_group_check=True),
                                        ["BP", "CPd"], [psk(bank)])
                        ACT(ph, B4[:, dg, :, :], ps[bank][:, 0:256].rearrange("p (r m) -> p r m", r=2), AF.Copy,
                            [psk(bank)], ["B4"])
                        if d == 0:
                            STT(ph, T4[:, dg, :], ident[:], dvec[:, gp:gp + 1], ps[bank][:, 256:384], ALU.mult, ALU.add,
                                [psk(bank), "dvec", "ident"], ["T4"])
                        else:
                            CP(ph, T4[:, dg, :], ps[bank][:, 256:384], [psk(bank)], ["T4"])
                    CP(ph, A1b[:, 0, :], LP[:, 4, 0, :], ["LP"], ["A1b"]); CP(ph, A1b[:, 1, :], LP[:, 4, 0, :], ["LP"], ["A1b"])
                    TS(ph, A2b[:, 0, :], LP[:, 4, 1, :], -1.0, None, ALU.mult, None, ["LP"], ["A2b"])
                    CP(ph, A2b[:, 1, :], LP[:, 4, 1, :], ["LP"], ["A2b"])
                    ph.emit()
                    checkpoint(f"C1{l}")

                with contextlib.ExitStack() as sC2:
                    def sc2(name, shape, dt=F32):
                        return sC2.enter_context(nc.sbuf_tensor(f"{name}_{l}", list(shape), dt))
                    VZ = sc2("VZ", [128, 2, 32, 384], BF16)
                    with contextlib.ExitStack() as sC2a:
                        hE = sC2a.enter_context(nc.sbuf_tensor(f"hE_{l}", [128, 2, 2, 32], F32))
                        pr = sC2a.enter_context(nc.sbuf_tensor(f"pr_{l}", [128, 16, 16], F32))
                        hst = sC2a.enter_context(nc.sbuf_tensor(f"hst_{l}", [32, 2, 128, 2], F32))
                        ph = Phase(nc, f"C2{l}")
                        for dg in range(32):
                            gp = dg % 16
                            for ri in range(2):
                                bank = (2 * dg + ri) % 4
                                MM(ph, ps[bank][:, 0:384], B4[:, dg, ri, :], U4[:, gp, :], True, True, [], [psk(bank)])
                                if ri == 0:
                                    CP(ph, VZ[:, ri, dg, :], ps[bank][:, 0:384], [psk(bank)], [])
                                else:
                                    ACT(ph, VZ[:, ri, dg, :], ps[bank][:, 0:384], AF.Copy, [psk(bank)], [])
                        for q in range(2):
                            for e_ in range(2):
                                for ri in range(2):
                                    for g2 in range(2):
                                        hs = slice(64 * g2, 64 * g2 + 64)
                                        uev = uedge[hs, 2 * q + e_, :].rearrange("p (gp g c) -> p gp g c", g=2, c=16)[:, :, g2, :]
                                        TT(ph, pr[hs], BB[hs, ri, 16 * e_:16 * e_ + 16, :], uev, ALU.mult, [], ["pr"])
                                        ph.op("dve", lambda e, hs=hs, q=q, ri=ri, e_=e_: e.tensor_reduce(
                                            out=hE[hs, q, ri, 16 * e_:16 * e_ + 16], in_=pr[hs], axis=AX.X, op=ALU.add),
                                            ["pr"], ["hE"])
                        for q in range(2):
                            for ri in range(2):
                                TR(ph, ps[4][0:32, 128 * ri:128 * ri + 128], hE[:, q, ri, :], ident[:], ["hE", "ident"], ["ps4"])
                            CP(ph, hst[:, q], ps[4][0:32, 0:256].rearrange("r (ri gp) -> r gp ri", ri=2), ["ps4"], ["hst"])
                            ph.dma("sp", nssm[q, l], hst[:, q].rearrange("r gp ri -> r (gp ri)"), reads=["hst"])
                        ph.emit()
                        checkpoint(f"C2{l}")
                    with contextlib.ExitStack() as sC3:
                        def sc3(name, shape, dt=F32):
                            return sC3.enter_context(nc.sbuf_tensor(f"{name}_{l}", list(shape), dt))
                        zS = [sc3(f"zS{i}", [128, 3, 32]) for i in range(3)]
                        tS1 = sc3("tS1", [128, 2, 32]); tS2 = sc3("tS2", [128, 2, 32])
                        zP = {(d, i): sc3(f"zP{d}{i}", [128, 3, 16, 2]) for d in range(2) for i in range(3)}
                        tP1 = [sc3(f"tP1{d}", [128, 2, 16, 2]) for d in range(2)]
                        tP2 = [sc3(f"tP2{d}", [128, 2, 16, 2]) for d in range(2)]
                        ph = Phase(nc, f"C3{l}")
                        CP(ph, zS[0][:, 0:2, :], H0[:], [], ["zS0"])
                        for d in range(2):
                            MS(ph, zP[(d, 0)][:], 0.0, [f"zP{d}0"], eng="pool")
                        vz0 = VZ[:, 0:1, 0:1, 0:1]
                        pstep = vz0.ap[0][0]

                        def vz_merged(jf, jb_):
                            return bass.AP(vz0.tensor, vz0.offset + jf,
                                           [[pstep, 128], [32 * 384, 2], [16 * 384 + (jb_ - jf), 2], [384, 16]])
                        NBS = 256
                        for i in range(NBS):
                            vj = vz_merged(128 + i, 128 + NBS - 1 - i)
                            zc = zS[i % 3]; zn = zS[(i + 1) % 3]
                            kc_ = f"zS{i % 3}"; kn_ = f"zS{(i + 1) % 3}"
                            TT(ph, tS1[:], zc[:, 0:2, :], A1b[:], ALU.mult, [kc_], ["tS1"])
                            zv = zc[:, 1:2, :]
                            zsw = bass.AP(zv.tensor, zv.offset, [[zv.ap[0][0], 128], [-32, 2], [1, 32]])
                            TT(ph, tS2[:], zsw, A2b[:], ALU.mult, [kc_], ["tS2"])
                            TT(ph, tS1[:], tS1[:], tS2[:], ALU.add, ["tS1", "tS2"], ["tS1"])
                            TT(ph, zn[:, 0:2, :].rearrange("p r (d g) -> p r d g", d=2),
                               tS1[:].rearrange("p r (d g) -> p r d g", d=2), vj, ALU.add,
                               ["tS1", ("vzS", i)], [kn_])
                            ACT(ph, vj, zc[:, 0:2, :].rearrange("p r (d g) -> p r d g", d=2), AF.Copy,
                                [kc_], [("vzS", i)])
                            if i < 64:
                                for d in range(2):
                                    dgs = slice(16 * d, 16 * d + 16)
                                    j = i if d == 0 else 63 - i
                                    pc = zP[(d, i % 3)]; pn = zP[(d, (i + 1) % 3)]
                                    pk = f"zP{d}{i % 3}"; pkn = f"zP{d}{(i + 1) % 3}"
                                    vp = VZ[:, :, dgs, j:j + 65:64]
                                    a1 = bc(A1b[:, :, dgs].unsqueeze(3), [128, 2, 16, 2])
                                    a2 = bc(A2b[:, :, dgs].unsqueeze(3), [128, 2, 16, 2])
                                    TT(ph, tP1[d][:], pc[:, 0:2], a1, ALU.mult, [pk], [f"tP1{d}"], eng="pool")
                                    TT(ph, tP2[d][:], pc[:, 1:3], a2, ALU.mult, [pk], [f"tP2{d}"], eng="pool")
                                    TT(ph, tP1[d][:], tP1[d][:], tP2[d][:], ALU.add, [f"tP1{d}", f"tP2{d}"],
                                       [f"tP1{d}"], eng="pool")
                                    TT(ph, pn[:, 0:2], tP1[d][:], vp, ALU.add, [f"tP1{d}", ("vzP", d, j)], [pkn],
                                       eng="pool")
                                    CP(ph, pn[:, 2], pn[:, 0], [pkn], [pkn], eng="pool")
                                    CP(ph, vp, pc[:, 0:2], [pk], [("vzP", d, j)], eng="pool")
                        ph.emit()
                        checkpoint(f"C3{l}")
                    with contextlib.ExitStack() as sC4:
                        def sc4(name, shape, dt=F32):
                            return sC4.enter_context(nc.sbuf_tensor(f"{name}_{l}", list(shape), dt))
                        zf = B4[:, 0:16].rearrange("p a r m -> p (a r m)").bitcast(F32).rearrange("p (c n) -> p c n", c=4)
                        zb = B4[:, 16:24].rearrange("p a r m -> p (a r m)").rearrange("p (c n) -> p c n", c=4)
                        wg = B4[:, 24:32].rearrange("p a r m -> p (a r m)").rearrange("p (c n) -> p c n", c=4)
                        sg = sc4("sg", [128, 512])
                        ph = Phase(nc, f"C4{l}")
                        for kc in range(4):
                            ph.dma("pool", wg[:, kc, :], w_glu[l, 128 * kc:128 * kc + 128, :], writes=["wg"])
                        for ct in range(3):
                            cols = slice(512 * ct, 512 * ct + 512)
                            for ch in range(4):
                                bank = ch
                                if ct == 0:
                                    regions = [(t4, 128 * t4, 128, slice(0, 128)) for t4 in range(4)]
                                else:
                                    regions = [(t4, 256 * (t4 % 2), 256, slice(128, 384)) for t4 in
                                               (2 * (ct - 1), 2 * (ct - 1) + 1)]
                                for (t4, off, N, ucols) in regions:
                                    for n in range(6):
                                        d, w_ = n // 3, n % 3
                                        for g4 in range(4):
                                            gp = 4 * ch + g4
                                            dg = 16 * d + gp
                                            outap = ps[bank][32 * g4:32 * g4 + 32, off:off + N]
                                            if w_ == 0:
                                                lw, rh = T4[:, dg, 32 * t4:32 * t4 + 32], U4[:, gp, ucols]
                                            else:
                                                lw = C4[:, dg, w_ - 1, 32 * t4:32 * t4 + 32]
                                                rh = VZ[:, w_ - 1, dg, ucols]
                                            MM(ph, outap, lw, rh, n == 0, n == 5, [], [psk(bank)], tp=(0, 32 * g4))
                                ACT(ph, zf[:, ch, :], ps[bank][:], AF.Gelu_apprx_tanh, [psk(bank)], [("zf", ch)])
                                CP(ph, zb[:, ch, :], zf[:, ch, :], [("zf", ch)], [("zb", ch)])
                            for m in range(4):
                                bank = 4 + m % 2
                                for kc in range(4):
                                    MM(ph, ps[bank][:], wg[:, kc, 128 * m:128 * m + 128], zb[:, kc, :], kc == 0, kc == 3,
                                       ["wg"] + [("zb", kc)], [psk(bank)])
                                ACT(ph, sg[:], ps[bank][:], AF.Sigmoid, [psk(bank)], ["sg"], bias=VT[:, l, 96 + m:97 + m])
                                TT(ph, hmod[:, 4 + m, cols], zf[:, m, :], sg[:], ALU.mult, [("zf", m), "sg"], [("mix2", m, ct)])
                        ph.emit()
                        checkpoint(f"C4{l}")
            sL.__exit__(None, None, None)

            with contextlib.ExitStack() as sD:
                wo = sD.enter_context(nc.sbuf_tensor(f"wo_{l}", [128, 8, 1024], BF16))
                ph = Phase(nc, f"D{l}")
                for kc in range(8):
                    ph.dma("pool", wo[:, kc, :], w_out[l, 128 * kc:128 * kc + 128, :], writes=["wo"])
                for ct in range(3):
                    cd = 0 if ct == 0 else 1
                    cols = slice(512 * ct, 512 * ct + 512)
                    for m in range(8):
                        bank = m % 4
                        for kc in range(8):
                            MM(ph, ps[bank][:], wo[:, kc, 128 * m:128 * m + 128], hmod[:, kc, cols], kc == 0, kc == 7,
                               ["wo"], [psk(bank)])
                        STT(ph, X[:, m, cols], ps[bank][:], modsb[:, l, 16 + m, cd:cd + 1], X[:, m, cols],
                            ALU.mult, ALU.add, [psk(bank)], [("X", m, ct)])
                ph.emit()
                checkpoint(f"D{l}")

            with contextlib.ExitStack() as sE:
                gact = sE.enter_context(nc.sbuf_tensor(f"gact_{l}", [128, 16, NCOL], BF16))
                with contextlib.ExitStack() as sE1:
                    def se1(name, shape, dt=F32):
                        return sE1.enter_context(nc.sbuf_tensor(f"{name}_{l}", list(shape), dt))
                    sq = [se1(f"sqE{i}", [128, 512], BF16) for i in range(2)]
                    tmp = [se1(f"tmpE{i}", [128, 512]) for i in range(2)]
                    lnv = se1("lnvE", [128, 512]); rstd = se1("rstdE", [128, 512])
                    wu = [se1(f"wu{i}", [128, 8, 1024], BF16) for i in range(2)]
                    zzs = [[se1(f"zz{a}{i}", [128, NCOL]) for i in range(2)] for a in range(2)]
                    ph = Phase(nc, f"E{l}")
                    norm_mod(ph, l, A2, 24, sq, lnv, rstd, tmp)
                    wupv = w_up[l].rearrange("(kc p) f -> p kc f", p=128)
                    for j in range(16):
                        qd, jj = j // 4, j % 4
                        wb = wu[qd % 2]
                        wk = f"wu{qd % 2}"
                        if jj == 0:
                            for qn in ([0, 1] if qd == 0 else ([qd + 1] if qd + 1 < 4 else [])):
                                wbn = wu[qn % 2]
                                wkn = f"wu{qn % 2}"
                                ph.dma("pool", wbn[:, :, 0:512], wupv[:, :, 512 * qn:512 * qn + 512],
                                       reads=[wkn + "r"], writes=[wkn])
                                ph.dma("pool", wbn[:, :, 512:1024], wupv[:, :, 2048 + 512 * qn:2048 + 512 * qn + 512],
                                       reads=[wkn + "r"], writes=[wkn])
                        zz = zzs[j % 2]
                        for hf in range(2):
                            f = j + 16 * hf
                            z = zz[hf]
                            zk = f"zz{j % 2}{hf}"
                            bks = [3 * hf + ct for ct in range(3)]
                            for ct in range(3):
                                for kc in range(8):
                                    MM(ph, ps[bks[ct]][:], wb[:, kc, 512 * hf + 128 * jj:512 * hf + 128 * jj + 128],
                                       hmod[:, kc, 512 * ct:512 * ct + 512], kc == 0, kc == 7,
                                       [wk, ("hmod", ct)],
                                       [psk(bks[ct])] + ([wk + "r"] if (jj == 3 and hf == 1 and ct == 2 and kc == 7) else []))
                            w0 = VTB[:, l, f:f + 1]; w1c = VTB[:, l, 32 + f:33 + f]; w2c = VTB[:, l, 64 + f:65 + f]
                            bcol = VT[:, l, 64 + f:65 + f]
                            RK = {n_: (zk, n_) for n_ in ("A", "B", "C", "D", "E1", "E2", "F")}
                            ctk = [[RK["A"], RK["B"], RK["C"]], [RK["D"], RK["E1"]], [RK["E2"], RK["F"]]]
                            for ct in range(3):
                                ACT(ph, z[:, 512 * ct:512 * ct + 512], ps[bks[ct]][:], AF.Identity, [psk(bks[ct])], ctk[ct],
                                    scale=w1c, bias=bcol)
                            pP, pS1, pS2 = ps[bks[0]], ps[bks[1]], ps[bks[2]]

                            def tap(dst, src, wcol, pbank, regs):
                                ks = [RK[r_] for r_ in regs]
                                STT(ph, dst, src, wcol, dst, ALU.mult, ALU.add, [psk(pbank)] + ks, ks)
                            tap(z[:, 128:512], pP[:, 0:384], w0, bks[0], ["B", "C"])
                            tap(z[:, 768:1280], pS1[:, 0:512], w0, bks[1], ["E1", "E2"])
                            tap(z[:, 0:128].rearrange("p (q j) -> p q j", q=2)[:, :, 1:64],
                                pP[:, 384:512].rearrange("p (q j) -> p q j", q=2)[:, :, 0:63], w0, bks[0], ["A"])
                            tap(z[:, 1280:1536], pS2[:, 0:256], w0, bks[2], ["F"])
                            tap(z[:, 513:768], pS2[:, 256:511], w0, bks[2], ["D"])
                            tap(z[:, 0:384], pP[:, 128:512], w2c, bks[0], ["A", "B"])
                            tap(z[:, 768:1280], pS2[:, 0:512], w2c, bks[2], ["E1", "E2"])
                            tap(z[:, 384:512].rearrange("p (q j) -> p q j", q=2)[:, :, 0:63],
                                pP[:, 0:128].rearrange("p (q j) -> p q j", q=2)[:, :, 1:64], w2c, bks[0], ["C"])
                            tap(z[:, 512:768], pS1[:, 256:512], w2c, bks[1], ["D"])
                            tap(z[:, 1280:1535], pS1[:, 1:256], w2c, bks[1], ["F"])
                        allk = lambda zk_: [(zk_, n_) for n_ in ("A", "B", "C", "D", "E1", "E2", "F")]
                        ACT(ph, zz[1][:], zz[1][:], AF.Silu, allk(f"zz{j % 2}1"), allk(f"zz{j % 2}1"))
                        TT(ph, gact[:, j, :], zz[0][:], zz[1][:], ALU.mult, allk(f"zz{j % 2}0") + allk(f"zz{j % 2}1"),
                           [("gact", j)], eng="pool")
                    for kc in range(16):
                        wbn = wu[kc // 8]
                        wkn = f"wu{kc // 8}"
                        ph.dma("pool", wbn[:, kc % 8, :], w_down[l, 128 * kc:128 * kc + 128, :],
                               reads=[wkn + "r"], writes=[("wd", kc)])
                    for ct in range(3):
                        cd = 0 if ct == 0 else 1
                        cols = slice(512 * ct, 512 * ct + 512)
                        for m in range(8):
                            bank = 6 + m % 2
                            for kc in range(16):
                                MM(ph, ps[bank][:], wu[kc // 8][:, kc % 8, 128 * m:128 * m + 128], gact[:, kc, cols],
                                   kc == 0, kc == 15, [("wd", kc), ("gact", kc)], [psk(bank)])
                            STT(ph, X[:, m, cols], ps[bank][:], modsb[:, l, 40 + m, cd:cd + 1], X[:, m, cols],
                                ALU.mult, ALU.add, [psk(bank)], [("X", m, ct)])
                    ph.emit()
                    checkpoint(f"E{l}")
                    checkpoint(f"F{l}")

        with contextlib.ExitStack() as sZ:
            ot = [sZ.enter_context(nc.sbuf_tensor(f"ot{i}", [128, 1024], F32)) for i in range(2)]
            ph = Phase(nc, "Z")
            for tt in range(12):
                o = ot[tt % 2]
                ok_ = f"ot{tt % 2}"
                for half in range(2):
                    bank = (2 * tt + half) % 4
                    for c4 in range(4):
                        c = 4 * half + c4
                        TR(ph, ps[bank][:, 128 * c4:128 * c4 + 128], X[:, c, 128 * tt:128 * tt + 128], ident[:], [],
                           [psk(bank)])
                    if half == 0:
                        CP(ph, o[:, 0:512], ps[bank][:], [psk(bank)], [ok_])
                    else:
                        ACT(ph, o[:, 512:1024], ps[bank][:], AF.Copy, [psk(bank)], [ok_])
                if tt < 4:
                    for q in range(2):
                        ph.dma("sp", yp[q].rearrange("(j s) d -> s j d", s=4)[tt], o[64 * q:64 * q + 64, :], reads=[ok_])
                else:
                    s4, hf = (tt - 4) // 2, (tt - 4) % 2
                    ph.dma("sp", ys.rearrange("(j s) d -> s j d", s=4)[s4, 128 * hf:128 * hf + 128, :], o[:, :], reads=[ok_])
            ph.emit()


_CACHE = {}


def _consts():
    ident = np.eye(128, dtype=np.float32)
    n = np.arange(1024)
    tok = 4 * (n % 256) + n // 256
    row = (tok // 64).astype(np.float64)
    colp = (tok % 64).astype(np.float64)
    p = np.arange(128)
    dd = p % 64
    axis = dd // 32
    half = (dd % 32) // 16
    f = dd % 16
    inv = 1.0 / (10000.0 ** (np.arange(8, dtype=np.float32) / np.float32(8)))
    inv16 = np.concatenate([inv, inv])
    nf = 16 // 2
    invf = (1.0 / (np.float32(10000.0) ** (np.arange(16, dtype=np.float32) / np.float32(16)))).astype(np.float32)
    pos = np.where(axis[:, None] == 0, row[None, :], colp[None, :]).astype(np.float32)
    ang = pos * invf[f][:, None]
    cosT = np.cos(ang).astype(np.float32)
    sinT = np.sin(ang).astype(np.float32)
    rsign = np.zeros((128, 128), np.float32)
    for m in range(128):
        if half[m] == 0:
            rsign[m + 16, m] = -1.0
        else:
            rsign[m - 16, m] = 1.0
    return ident, cosT, sinT, rsign


def make_inmaps(inp):
    f32 = np.float32
    g = {k: np.ascontiguousarray(np.asarray(v, dtype=f32)) for k, v in inp.items()}
    ident, cosT, sinT, rsign = _consts()
    va = np.concatenate([g["g_norm1"].reshape(2, 8, 128), g["g_norm2"].reshape(2, 8, 128),
                         g["b_mod"].reshape(2, 48, 128), g["conv_b"].reshape(2, 32, 128),
                         g["b_glu"].reshape(2, 4, 128)], axis=1)
    vb = g["conv_w"].reshape(2, 96, 128)
    qkn = np.stack([g["q_norm"], g["k_norm"]], axis=1)
    lamv = np.stack([g["lambda_q1"], g["lambda_k1"], g["lambda_q2"], g["lambda_k2"]], axis=1)
    shared = dict(w_mod=g["w_mod"], w_in=g["w_in"], w_out=g["w_out"], w_glu=g["w_glu"], w_up=g["w_up"],
                  w_down=g["w_down"], va=np.ascontiguousarray(va), vb=np.ascontiguousarray(vb),
                  qkn=np.ascontiguousarray(qkn), lamv=np.ascontiguousarray(lamv), subg=g["subln_g"],
                  lre=g["ssm_lambda_re"].reshape(2, 32, 128), lim=g["ssm_lambda_im"].reshape(2, 32, 128),
                  lstep=g["ssm_log_step"].reshape(2, 32, 2), bre=g["ssm_b_re"], bim=g["ssm_b_im"],
                  cre=g["ssm_c_re"], cim=g["ssm_c_im"], ssmd=g["ssm_d"],
                  ident=ident, cosT=cosT, sinT=sinT, rsign=rsign)
    in_maps = []
    for c in range(8):
        m = dict(shared)
        m["xp"] = np.ascontiguousarray(g["x_prompt"][2 * c:2 * c + 2])
        m["xs"] = np.ascontiguousarray(g["x_sample"][c])
        m["ck"] = np.ascontiguousarray(g["cache_k"][c].reshape(2, 512, 512))
        m["cv"] = np.ascontiguousarray(g["cache_v"][c].reshape(2, 512, 512))
        m["st"] = np.ascontiguousarray(g["state_ssm"][c].reshape(2, 32, 256))
        m["cvec"] = np.ascontiguousarray(np.stack([g["c_ctx"], g["c"][c]], axis=0).reshape(16, 128))
        in_maps.append(m)
    return in_maps


def kernel(**inp):
    f32 = np.float32
    if "nc" not in _CACHE:
        _CACHE["nc"] = build_program()
    nc = _CACHE["nc"]
    in_maps = make_inmaps(inp)
    res = run_bass_kernel_spmd(nc, in_maps, core_ids=list(range(8)))
    R = res.results
    y_prompt = np.concatenate([R[c]["yp"] for c in range(8)], axis=0).astype(f32)
    y_sample = np.stack([R[c]["ys"] for c in range(8)], axis=0).astype(f32)
    new_k = np.concatenate([R[c]["nk"] for c in range(8)], axis=0).reshape(16, 2, 256, 4, 2, 64).astype(f32)
    new_v = np.concatenate([R[c]["nv"] for c in range(8)], axis=0).reshape(16, 2, 256, 4, 128).astype(f32)
    new_ssm = np.concatenate([R[c]["nssm"] for c in range(8)], axis=0).reshape(16, 2, 2, 32, 64, 2).astype(f32)
    return (y_prompt, y_sample, new_k, new_v, new_ssm)
```

```python
import contextlib
import math
import numpy as np
import concourse.bass as bass
import concourse.mybir as mybir
from concourse.bass_utils import run_bass_kernel_spmd

F32 = mybir.dt.float32
BF16 = mybir.dt.bfloat16
AF = mybir.ActivationFunctionType
ALU = mybir.AluOpType
AX = mybir.AxisListType

ENGS = ("pe", "act", "dve", "pool", "sp")
STRICT_SAME = True
EPS = 1e-6
NCOL = 1536
DEPTH = 2


class Phase:
    def __init__(self, nc, name):
        self.nc = nc
        self.name = name
        self.ops = []
        self.last_w = {}
        self.readers = {}

    def _add(self, eng, fn, reads, writes, is_dma):
        def canon(k):
            return k[:3] if isinstance(k, str) and len(k) >= 3 and k.startswith("ps") and k[2].isdigit() else k
        reads = tuple(canon(k) for k in reads)
        writes = tuple(canon(k) for k in writes)
        writes = writes + tuple(k for k in reads if isinstance(k, str) and len(k) == 3 and k.startswith("ps")
                                and k[2].isdigit() and k not in writes)
        idx = len(self.ops)
        deps = set()
        raw = set()
        for k in reads:
            if k in self.last_w:
                deps.add(self.last_w[k])
                if not (isinstance(k, str) and len(k) == 3 and k.startswith("ps") and k[2].isdigit()
                        and self.ops[self.last_w[k]]["eng"] != "pe"):
                    raw.add(self.last_w[k])
        for k in writes:
            if k in self.last_w:
                deps.add(self.last_w[k])
            for r in self.readers.get(k, ()):
                deps.add(r)
        fd = set()
        for d in deps:
            o = self.ops[d]
            if o["is_dma"] or o["eng"] != eng or is_dma:
                fd.add(d)
            elif STRICT_SAME and eng != "pe" and d in raw:
                fd.add(d)
        for d in fd:
            self.ops[d]["signal"] = True
        import sys as _s
        self.ops.append(dict(eng=eng, fn=fn, deps=sorted(fd), is_dma=is_dma, signal=False,
                             tag=_s._getframe(3).f_lineno if _DBG.get("trunc") else None))
        for k in writes:
            self.last_w[k] = idx
            self.readers[k] = []
        for k in reads:
            if k not in writes:
                self.readers.setdefault(k, []).append(idx)
        return idx

    def op(self, eng, fn, reads=(), writes=()):
        return self._add(eng, fn, tuple(reads), tuple(writes), False)

    def dma(self, eng, out, in_, reads=(), writes=(), **kw):
        return self._add(eng, lambda e: e.dma_start(out=out, in_=in_, **kw), tuple(reads), tuple(writes), True)

    def emit(self):
        nc = self.nc
        ops = self.ops
        tr = _DBG.get("trunc")
        if tr is not None and tr[0] == self.name:
            print("TRUNC", self.name, "total ops", len(ops), "->", tr[1])
            for i_, o_ in enumerate(ops[:tr[1]][-3:]):
                print("   last ops:", o_["eng"], o_.get("tag"))
            ops = ops[:tr[1]]
            self.ops = ops
        ndma = sum(1 for o in ops if o["is_dma"])
        with contextlib.ExitStack() as st:
            G = getattr(nc, "_phase_sems", None)
            if G is None:
                G = dict(esem={e: nc.alloc_semaphore(name=f"g_{e}") for e in ENGS},
                         dsem=[nc.alloc_semaphore(name=f"g_d{i}") for i in range(88)],
                         ecount={e: 0 for e in ENGS}, dcount=[0] * 88, nxt=0)
                nc._phase_sems = G
            esem = G["esem"]; dsem = G["dsem"]; dcount = G["dcount"]; ecount = G["ecount"]
            NP = len(dsem)
            assert ndma <= NP, (self.name, ndma)
            used = set()
            for o in ops:
                if o["is_dma"]:
                    k = G["nxt"] % NP
                    G["nxt"] += 1
                    o["sem"] = k
                    dcount[k] += 16
                    o["val"] = dcount[k]
                    used.add(k)
                elif o["signal"]:
                    ecount[o["eng"]] += 1
                    o["val"] = ecount[o["eng"]]
            block = st.enter_context(nc.Block())

            def body(ename):
                def f(e):
                    waited = {}
                    for o in ops:
                        if o["eng"] != ename:
                            continue
                        for d in o["deps"]:
                            p = ops[d]
                            if p["is_dma"]:
                                key = ("d", p["sem"])
                                sem = dsem[p["sem"]]
                            else:
                                key = ("e", p["eng"])
                                sem = esem[p["eng"]]
                            if waited.get(key, 0) >= p["val"]:
                                continue
                            waited[key] = p["val"]
                            e.wait_ge(sem, p["val"])
                        ins = o["fn"](e)
                        if o["is_dma"]:
                            ins.then_inc(dsem[o["sem"]], 16)
                        elif o["signal"]:
                            ins.then_inc(esem[ename], 1)
                    if ename == "sp":
                        for i in sorted(used):
                            if waited.get(("d", i), 0) < dcount[i]:
                                e.wait_ge(dsem[i], dcount[i])
                return f

            block.tensor(body("pe"))
            block.scalar(body("act"))
            block.vector(body("dve"))
            block.gpsimd(body("pool"))
            block.sync(body("sp"))


def ACT(ph, out, in_, func, r, w, **kw):
    ph.op("act", lambda e: e.activation(out=out, in_=in_, func=func, **kw), r, w)


def TT(ph, out, in0, in1, op, r, w, eng="dve"):
    ph.op(eng, lambda e: e.tensor_tensor(out=out, in0=in0, in1=in1, op=op), r, w)


def STT(ph, out, in0, scalar, in1, op0, op1, r, w, eng="dve"):
    ph.op(eng, lambda e: e.scalar_tensor_tensor(out=out, in0=in0, scalar=scalar, in1=in1, op0=op0, op1=op1), r, w)


def TS(ph, out, in0, s1, s2, op0, op1, r, w, eng="dve"):
    if s2 is None:
        ph.op(eng, lambda e: e.tensor_scalar(out=out, in0=in0, scalar1=s1, scalar2=None, op0=op0), r, w)
    else:
        ph.op(eng, lambda e: e.tensor_scalar(out=out, in0=in0, scalar1=s1, scalar2=s2, op0=op0, op1=op1), r, w)


def CP(ph, out, in_, r, w, eng="dve"):
    ph.op(eng, lambda e: e.tensor_copy(out=out, in_=in_), r, w)


def MM(ph, out, lhsT, rhs, start, stop, r, w, tp=None):
    if tp is None:
        ph.op("pe", lambda e: e.matmul(out, lhsT, rhs, start=start, stop=stop), r, w)
    else:
        ph.op("pe", lambda e: e.matmul(out, lhsT, rhs, start=start, stop=stop, tile_position=tp), r, w)


def TR(ph, out, in_, ident, r, w):
    ph.op("pe", lambda e: e.transpose(out, in_, ident), r, w)


def MS(ph, ap, val, w, eng="dve"):
    ph.op(eng, lambda e: e.memset(ap, val), (), w)


def bc(ap, shape):
    return ap.broadcast_to(shape)


class _Stop(Exception):
    pass


_DBG = {"stop": None, "trunc": None}


def build_program():
    nc = bass.Bass("TRN2", target_bir_lowering=False)
    try:
        _build(nc)
    except _Stop:
        pass
    return nc


def _build(nc):

    def din(name, shape):
        return nc.dram_tensor(name, list(shape), F32, kind="ExternalInput").ap()

    def dout(name, shape):
        return nc.dram_tensor(name, list(shape), F32, kind="ExternalOutput").ap()

    xp = din("xp", [2, 256, 1024]); xs = din("xs", [1024, 1024])
    ck = din("ck", [2, 512, 512]); cv = din("cv", [2, 512, 512])
    stt_in = din("st", [2, 32, 256]); cvec = din("cvec", [16, 128])
    w_mod = din("w_mod", [2, 1024, 6144]); w_in = din("w_in", [2, 1024, 2048])
    w_out = din("w_out", [2, 1024, 1024]); w_glu = din("w_glu", [2, 512, 512])
    w_up = din("w_up", [2, 1024, 4096]); w_down = din("w_down", [2, 2048, 1024])
    va = din("va", [2, 100, 128]); vb = din("vb", [2, 96, 128])
    qkn = din("qkn", [2, 2, 64]); lamv = din("lamv", [2, 4, 64]); subg = din("subg", [2, 128])
    lre_d = din("lre", [2, 32, 128]); lim_d = din("lim", [2, 32, 128]); lst_d = din("lstep", [2, 32, 2])
    bre_d = din("bre", [2, 2, 32, 64, 16]); bim_d = din("bim", [2, 2, 32, 64, 16])
    cre_d = din("cre", [2, 2, 32, 16, 64]); cim_d = din("cim", [2, 2, 32, 16, 64])
    ssmd_d = din("ssmd", [2, 512])
    ident_d = din("ident", [128, 128]); cos_d = din("cosT", [128, 1024]); sin_d = din("sinT", [128, 1024])
    rsign_d = din("rsign", [128, 128])
    yp = dout("yp", [2, 256, 1024]); ys = dout("ys", [1024, 1024])
    nk = dout("nk", [2, 2, 256, 512]); nv = dout("nv", [2, 2, 256, 512]); nssm = dout("nssm", [2, 2, 32, 256])
    if _DBG["stop"] is not None:
        dbgX = dout("dbgX", [128, 8 * NCOL]); dbgH = dout("dbgH", [128, 8 * NCOL])
        dbgM = dout("dbgM", [128, 2 * 48 * 2])

    es = contextlib.ExitStack()
    with es:
        def sb(name, shape, dt=F32):
            return es.enter_context(nc.sbuf_tensor("s_" + name, list(shape), dt))

        ps = [es.enter_context(nc.psum_tensor(f"ps{i}", [128, 512], F32)) for i in range(8)]
        X = sb("X", [128, 8, NCOL])
        ident = sb("ident", [128, 128]); identb = sb("identb", [128, 128], BF16)
        onesD = sb("onesD", [128, 128], BF16); blk64 = sb("blk64", [128, 128], BF16)
        ones1 = sb("ones1", [128, 128], BF16); onesV = sb("onesV", [128, 128], BF16)
        ones32 = sb("ones32", [128, 128]); zerob = sb("zerob", [128, 128], BF16)
        ones32q = sb("ones32q", [128, 2, 128])
        epsc = sb("epsc", [128, 1]); negpi = sb("negpi", [128, 1])
        cosT = sb("cosT", [128, 1024]); sinT = sb("sinT", [128, 1024]); rsign = sb("rsign", [128, 128])
        VT = sb("VT", [128, 2, 100]); VTB = sb("VTB", [128, 2, 96])
        scond = sb("scond", [128, 8, 2]); scondb = sb("scondb", [128, 8, 2], BF16)
        modsb = sb("modsb", [128, 2, 48, 2])
        A1 = sb("A1", [128, 2, 8, 2]); A2 = sb("A2", [128, 2, 8, 2])
        gq = sb("gq", [128, 2, 2])
        gsub = sb("gsub", [128, 2])
        neglam = sb("neglam", [128, 2])
        Rq = sb("Rq", [128, 2, 2, 128], BF16)
        hmod = sb("hmod", [128, 8, NCOL], BF16)

        def psk(i):
            return f"ps{i}"

        def checkpoint(name):
            if _DBG["stop"] != name:
                return
            ph = Phase(nc, "dbg")
            ph.dma("sp", dbgX[:, :], X[:].rearrange("p c n -> p (c n)"))
            ph.dma("pool", dbgH[:, :], hmod[:].rearrange("p c n -> p (c n)"))
            ph.dma("sp", dbgM[:, :], modsb[:].rearrange("p l j c -> p (l j c)"))
            ph.emit()
            raise _Stop()

        def mod_half(ph, l, hb, wbuf, wkey, pbank):
            wmv = w_mod[l].rearrange("(kc p) f -> p kc f", p=128)
            ph.dma("pool", wbuf[:], wmv[:, :, 512 * hb:512 * hb + 512], reads=[wkey + "r"], writes=[wkey])

            def compute():
                for j4 in range(4):
                    for kc in range(8):
                        last = (j4 == 3 and kc == 7)
                        MM(ph, ps[pbank][:, 2 * j4:2 * j4 + 2], wbuf[:, kc, 128 * j4:128 * j4 + 128], scondb[:, kc, :],
                           kc == 0, kc == 7, [wkey], [psk(pbank), wkey + "r"] if last else [psk(pbank)])
                j0 = 4 * hb
                TT(ph, modsb[:, l, j0:j0 + 4, :], ps[pbank][:, 0:8].rearrange("p (j c) -> p j c", c=2),
                   bc(VT[:, l, 16 + j0:16 + j0 + 4].unsqueeze(2), [128, 4, 2]), ALU.add, [psk(pbank)], [("modsb", l, hb)])
            return compute

        def mod_A(ph, l, Aout, goff, scoff):
            rk = [("modsb", l, hb) for hb in range(12)]
            TS(ph, Aout[:, l], modsb[:, l, scoff:scoff + 8, :], 1.0, None, ALU.add, None, rk, [("A", l, goff)])
            TT(ph, Aout[:, l], Aout[:, l], bc(VT[:, l, goff:goff + 8].unsqueeze(2), [128, 8, 2]), ALU.mult,
               [("A", l, goff)], [("A", l, goff)])

        with contextlib.ExitStack() as s0:
            xtile = [s0.enter_context(nc.sbuf_tensor(f"xtile{i}", [128, 1024], F32)) for i in range(4)]
            wm = [s0.enter_context(nc.sbuf_tensor(f"wm{i}", [128, 8, 512], BF16)) for i in range(2)]
            vtmp = s0.enter_context(nc.sbuf_tensor("vtmp", [128, 128], F32))
            ctmp = s0.enter_context(nc.sbuf_tensor("ctmp", [128, 16], F32))
            lt = s0.enter_context(nc.sbuf_tensor("lt", [64, 2, 4], F32))
            lp = s0.enter_context(nc.sbuf_tensor("lp", [64, 2, 2], F32))
            le = s0.enter_context(nc.sbuf_tensor("le", [128, 2, 2], F32))
            ph = Phase(nc, "p0")
            ph.dma("sp", ident[:], ident_d[:, :], writes=["ident"])
            ph.dma("sp", cosT[:], cos_d[:, :], writes=["cos"])
            ph.dma("sp", sinT[:], sin_d[:, :], writes=["sin"])
            ph.dma("sp", rsign[:], rsign_d[:, :], writes=["rsign"])
            CP(ph, identb[:], ident[:], ["ident"], ["identb"])
            MS(ph, onesD[:], 1.0 / 1024.0, ["onesD"])
            MS(ph, ones1[:], 1.0, ["ones1"])
            MS(ph, onesV[:], 1.0 / 128.0, ["onesV"])
            MS(ph, ones32[:], 1.0, ["ones32"])
            MS(ph, ones32q[:], 0.0, ["ones32q"])
            MS(ph, ones32q[0:64, 0, :], 1.0, ["ones32q"])
            MS(ph, ones32q[64:128, 1, :], 1.0, ["ones32q"])
            MS(ph, zerob[:], 0.0, ["zerob"])
            MS(ph, blk64[:], 0.0, ["blk64"])
            MS(ph, blk64[0:64, 0:64], 1.0 / 64.0, ["blk64"])
            MS(ph, blk64[64:128, 64:128], 1.0 / 64.0, ["blk64"])
            MS(ph, epsc[:], EPS, ["epsc"])
            MS(ph, negpi[:], -math.pi, ["negpi"])
            for l in range(2):
                ph.dma("sp", vtmp[0:100, :], va[l], reads=["vtmp_r"], writes=["vtmp"])
                TR(ph, ps[0][:, 0:100], vtmp[0:100, :], ident[0:100, 0:100], ["vtmp", "ident"], ["ps0"])
                CP(ph, VT[:, l, :], ps[0][:, 0:100], ["ps0"], ["VT", "vtmp_r"])
                ph.dma("sp", vtmp[0:96, :], vb[l], reads=["vtmp_r"], writes=["vtmp"])
                TR(ph, ps[0][:, 0:96], vtmp[0:96, :], ident[0:96, 0:96], ["vtmp", "ident"], ["ps0"])
                CP(ph, VTB[:, l, :], ps[0][:, 0:96], ["ps0"], ["VTB", "vtmp_r"])
            ph.dma("sp", vtmp[0:16, :], cvec[:, :], reads=["vtmp_r"], writes=["vtmp"])
            TR(ph, ps[0][:, 0:16], vtmp[0:16, :], ident[0:16, 0:16], ["vtmp", "ident"], ["ps0"])
            ACT(ph, ctmp[:], ps[0][:, 0:16], AF.Silu, ["ps0"], ["ctmp", "vtmp_r"])
            CP(ph, scond[:], ctmp[:].rearrange("p (c k) -> p k c", c=2), ["ctmp"], ["scond"])
            CP(ph, scondb[:], scond[:], ["scond"], ["scondb"])
            for l in range(2):
                for j in range(2):
                    for half in range(2):
                        ph.dma("sp", gq[64 * half:64 * half + 64, l, j:j + 1],
                               qkn[l, j].rearrange("(p o) -> p o", o=1), writes=["gq"])
                ph.dma("sp", gsub[:, l:l + 1], subg[l].rearrange("(p o) -> p o", o=1), writes=["gsub"])
                ph.dma("sp", lt[:, l, :], lamv[l].rearrange("k p -> p k"), writes=["lt"],
                       allow_slow_non_contiguous=True)
            for l in range(2):
                lam_init = 0.8 - 0.6 * math.exp(-0.3 * l)
                TS(ph, gsub[:, l:l + 1], gsub[:, l:l + 1], 1.0 - lam_init, None, ALU.mult, None, ["gsub"], ["gsub"])
                TT(ph, lp[:, l, 0:1], lt[:, l, 0:1], lt[:, l, 1:2], ALU.mult, ["lt"], ["lp"])
                TT(ph, lp[:, l, 1:2], lt[:, l, 2:3], lt[:, l, 3:4], ALU.mult, ["lt"], ["lp"])
                MM(ph, ps[1][:, 0:2], ones32[0:64, :], lp[:, l, :], True, True, ["ones32", "lp"], ["ps1"])
                ACT(ph, le[:, l, :], ps[1][:, 0:2], AF.Exp, ["ps1"], ["le"])
                TT(ph, neglam[:, l:l + 1], le[:, l, 1:2], le[:, l, 0:1], ALU.subtract, ["le"], ["neglam"])
                TS(ph, neglam[:, l:l + 1], neglam[:, l:l + 1], -lam_init, None, ALU.add, None, ["neglam"], ["neglam"])
                for j in range(2):
                    TS(ph, Rq[:, l, j, :], rsign[:], gq[:, l, j:j + 1], None, ALU.mult, None, ["rsign", "gq"], ["Rq"])
            mod_pend0 = None
            for tt in range(12):
                if tt % 2 == 0:
                    hb_ = tt // 2
                    c_ = mod_half(ph, 0, hb_, wm[hb_ % 2], f"wm{hb_ % 2}", 6 + hb_ % 2)
                    if mod_pend0 is not None:
                        mod_pend0()
                    mod_pend0 = c_
                xt = xtile[tt % 4]
                xk = f"xt{tt % 4}"
                xq = "sp" if tt % 2 == 0 else "act"
                if tt < 4:
                    for q in range(2):
                        ph.dma(xq, xt[64 * q:64 * q + 64, :],
                               xp[q].rearrange("(j s) d -> s j d", s=4)[tt], reads=[xk + "r"], writes=[xk])
                else:
                    s4, hf = (tt - 4) // 2, (tt - 4) % 2
                    ph.dma(xq, xt[:, :], xs.rearrange("(j s) d -> s j d", s=4)[s4, 128 * hf:128 * hf + 128, :],
                           reads=[xk + "r"], writes=[xk])
                for half in range(2):
                    bank = 2 + (2 * tt + half) % 4
                    for c4 in range(4):
                        c = 4 * half + c4
                        TR(ph, ps[bank][:, 128 * c4:128 * c4 + 128], xt[:, 128 * c:128 * c + 128], ident[:],
                           [xk, "ident"], [psk(bank)])
                    dst = X[:, 4 * half:4 * half + 4, 128 * tt:128 * tt + 128]
                    src = ps[bank][:, :].rearrange("p (c t) -> p c t", c=4)
                    if half == 0:
                        CP(ph, dst, src, [psk(bank)], [("X", tt), xk + "r"])
                    else:
                        ACT(ph, dst, src, AF.Copy, [psk(bank)], [("X", tt), xk + "r"])
            mod_pend0()
            mod_A(ph, 0, A1, 0, 8)
            ph.emit()
            checkpoint("p0")

        MOD_REST = [(0, hb) for hb in range(6, 12)] + [(1, hb) for hb in range(12)]

        def norm_mod(ph, l, Atab, shoff, sq, lnv, rstd, tmp):
            for ct in range(3):
                cd = 0 if ct == 0 else 1
                cols = slice(512 * ct, 512 * ct + 512)
                for c in range(8):
                    ACT(ph, sq[c % 2][:], X[:, c, cols], AF.Square, [("X", ct)], [f"sq{c % 2}"])
                    MM(ph, ps[0][:], onesD[:], sq[c % 2][:], c == 0, c == 7, [f"sq{c % 2}"], ["ps0"])
                ACT(ph, lnv[:], ps[0][:], AF.Ln, ["ps0"], ["lnv"], bias=epsc[:])
                ACT(ph, rstd[:], lnv[:], AF.Exp, ["lnv"], ["rstd"], scale=-0.5)
                for c in range(8):
                    STT(ph, tmp[c % 2][:], X[:, c, cols], Atab[:, l, c, cd:cd + 1], rstd[:], ALU.mult, ALU.mult,
                        [("X", ct), "rstd"], [f"tmp{c % 2}"])
                    ACT(ph, hmod[:, c, cols], tmp[c % 2][:], AF.Identity, [f"tmp{c % 2}"], [("hmod", ct)],
                        bias=modsb[:, l, shoff + c, cd:cd + 1])

        for l in range(DEPTH):
            sL = contextlib.ExitStack()
            sL.__enter__()
            U4 = sL.enter_context(nc.sbuf_tensor(f"U4_{l}", [128, 16, 384], BF16))
            uedge = sL.enter_context(nc.sbuf_tensor(f"uedge_{l}", [128, 4, 512], F32))
            sA = contextlib.ExitStack()
            with sA:
                def sba(name, shape, dt=F32):
                    return sA.enter_context(nc.sbuf_tensor(f"{name}_{l}", list(shape), dt))

                qT = sba("qT", [128, 4, NCOL], BF16)
                kT = sba("kT", [128, 4, 2048], BF16)
                Vtok = sba("Vtok", [128, 16, 512], BF16)
                with contextlib.ExitStack() as sA1:
                    def sb1(name, shape, dt=F32):
                        return sA1.enter_context(nc.sbuf_tensor(f"{name}_{l}", list(shape), dt))
                    win = sb1("win", [128, 8, 2048], BF16)
                    sq = [sb1(f"sq{i}", [128, 512], BF16) for i in range(2)]
                    tmp = [sb1(f"tmp{i}", [128, 512]) for i in range(2)]
                    tmpB = [sb1(f"tmpB{i}", [128, 512]) for i in range(2)]
                    lnv = sb1("lnv", [128, 512]); rstd = sb1("rstd", [128, 512])
                    lrs = [lnv, rstd]; lrk = ["lnv", "rstd"]
                    qf = sb1("qf", [128, 512])
                    qbs = [sb1(f"qb{i}", [128, 512], BF16) for i in range(2)]
                    kst = sb1("kst", [128, 4, 128]); vst = qf
                    ckt = sb1("ckt", [128, 512])
                    hrep = tmpB[0][:].bitcast(BF16).rearrange("p (k n) -> p k n", k=8)
                    ph = Phase(nc, f"A{l}")
                    for kc in range(8):
                        ph.dma("pool", win[:, kc, :], w_in[l, 128 * kc:128 * kc + 128, :], writes=["win"])
                    for i in range(4):
                        ph.dma("pool", Vtok[:, 12 + i, :], cv[l, 128 * i:128 * i + 128, :], writes=[("V", 12 + i)])
                    norm_mod(ph, l, A1, 0, sq, lnv, rstd, tmp)
                    for i in range(4):
                        ph.dma("sp", ckt[:], ck[l, 128 * i:128 * i + 128, :], reads=["cktr"], writes=["ckt"])
                        for h in range(4):
                            TR(ph, ps[1][:, 128 * h:128 * h + 128], ckt[:, 128 * h:128 * h + 128], ident[:],
                               ["ckt", "ident"], ["ps1"])
                        CP(ph, kT[:, :, 1536 + 128 * i:1536 + 128 * i + 128],
                           ps[1][:, :].rearrange("p (h t) -> p h t", h=4), ["ps1"], [("kT", 3), "cktr"])
                    def qk_post(ct, mo, bank, sqt, sqk, par):
                        cols = slice(512 * ct, 512 * ct + 512)
                        isk = mo // 4
                        h = mo % 4

                        def run():
                            lr = lrs[par]; lk = lrk[par]
                            tA = tmp[par]; tAk = f"tmp{par}"
                            tB = tmpB[par]; tBk = f"tmpB{par}"
                            qb = qbs[par]; qbk = f"qb{par}"
                            MM(ph, ps[4][:], blk64[:], sqt[:], True, True, [sqk, "blk64"], ["ps4"])
                            ACT(ph, lr[:], ps[4][:], AF.Ln, ["ps4"], [lk], bias=epsc[:])
                            ACT(ph, lr[:], lr[:], AF.Exp, [lk], [lk], scale=-0.5)
                            gcol = gq[:, l, isk:isk + 1]
                            if ct == 0:
                                if isk == 0:
                                    for q in range(2):
                                        dst = qT[:, h, 0:512].rearrange("p (q s j) -> p q s j", q=2, s=4)[:, q]
                                        src = ps[bank][:, :].rearrange("p (s q j) -> p q s j", q=2, s=4)[:, q]
                                        rs = lr[:].rearrange("p (s q j) -> p q s j", q=2, s=4)[:, q]
                                        STT(ph, dst, src, gcol, rs, ALU.mult, ALU.mult, [psk(bank), lk], [("qT", 0)])
                                else:
                                    STT(ph, qf[:], ps[bank][:], gcol, lr[:], ALU.mult, ALU.mult,
                                        [psk(bank), lk], ["qf"])
                                    CP(ph, kT[:, h, 0:512], qf[:], ["qf"], [("kT", 0)])
                                    for s4 in range(4):
                                        TR(ph, ps[5][:, 128 * s4:128 * s4 + 128], qf[:, 128 * s4:128 * s4 + 128],
                                           ident[:], ["qf", "ident"], ["ps5"])
                                    ACT(ph, kst[:, :, :],
                                        ps[5][:, :].rearrange("p (s f) -> p s f", s=4), AF.Copy, ["ps5"], ["kst"])
                                    for s4 in range(4):
                                        for q in range(2):
                                            ph.dma("sp", nk[q, l].rearrange("(j s) f -> s j f", s=4)[s4, :, 128 * h:128 * h + 128],
                                                   kst[64 * q:64 * q + 64, s4, :], reads=["kst"])
                            else:
                                sc = slice(512 * (ct - 1), 512 * (ct - 1) + 512)
                                CP(ph, qb[:], ps[bank][:], [psk(bank)], [qbk])
                                MM(ph, ps[5][:], Rq[:, l, isk, :], qb[:], True, True, [qbk, "Rq"], ["ps5"])
                                STT(ph, tA[:], ps[bank][:], gcol, cosT[:, sc], ALU.mult, ALU.mult, [psk(bank), "cos"], [tAk])
                                TT(ph, tB[:], ps[5][:], sinT[:, sc], ALU.mult, ["ps5", "sin"], [tBk])
                                TT(ph, tA[:], tA[:], tB[:], ALU.add, [tAk, tBk], [tAk])
                                dst = (kT if isk else qT)[:, h, cols]
                                TT(ph, dst, tA[:], lr[:], ALU.mult, [tAk, lk], [("kT" if isk else "qT", ct)])
                        return run

                    pending = None
                    n_ = 0
                    for ct in range(3):
                        cols = slice(512 * ct, 512 * ct + 512)
                        for mo in range(8):
                            bank = 2 + n_ % 2
                            sqt = sq[n_ % 2]
                            sqk = f"sq{n_ % 2}"
                            for kc in range(8):
                                MM(ph, ps[bank][:], win[:, kc, 128 * mo:128 * mo + 128], hmod[:, kc, cols],
                                   kc == 0, kc == 7, ["win", ("hmod", ct)], [psk(bank)])
                            ACT(ph, sqt[:], ps[bank][:], AF.Square, [psk(bank)], [sqk])
                            if pending is not None:
                                pending()
                            pending = qk_post(ct, mo, bank, sqt, sqk, n_ % 2)
                            n_ += 1
                    pending()
                    for tt in range(12):
                        bank = 2 + tt % 2
                        for kc in range(8):
                            MM(ph, ps[bank][:], hmod[:, kc, 128 * tt:128 * tt + 128], win[:, kc, 1024:1536],
                               kc == 0, kc == 7, ["win", ("hmod", tt // 4)], [psk(bank)])
                        CP(ph, Vtok[:, tt, :], ps[bank][:], [psk(bank)], [("V", tt)])
                        if tt < 4:
                            ACT(ph, vst[:], ps[bank][:], AF.Copy, [psk(bank)], ["qf"])
                            for q in range(2):
                                ph.dma("sp", nv[q, l].rearrange("(j s) f -> s j f", s=4)[tt],
                                       vst[64 * q:64 * q + 64, :], reads=["qf"])
                    for gp in range(16):
                        bank = 6 + gp % 2
                        for kc in range(8):
                            lw = win[:, kc, 1536 + 32 * gp:1536 + 32 * gp + 32]
                            for s4 in range(4):
                                MM(ph, ps[bank][32 * s4:32 * s4 + 32, 0:128], lw, hmod[:, kc, 128 * s4:128 * s4 + 128],
                                   kc == 0, kc == 7, ["win", ("hmod", 0)], [psk(bank)], tp=(0, 32 * s4))
                        for kc in range(8):
                            lw = win[:, kc, 1536 + 32 * gp:1536 + 32 * gp + 32]
                            for s4 in range(4):
                                MM(ph, ps[bank][32 * s4:32 * s4 + 32, 128:384], lw,
                                   hmod[:, kc, 512 + 256 * s4:512 + 256 * s4 + 256],
                                   kc == 0, kc == 7, ["win", ("hmod", 1), ("hmod", 2)], [psk(bank)], tp=(0, 32 * s4))
                        if gp % 2 == 0:
                            CP(ph, U4[:, gp, :], ps[bank][:, 0:384], [psk(bank)], ["U4"])
                        else:
                            ACT(ph, U4[:, gp, :], ps[bank][:, 0:384], AF.Copy, [psk(bank)], ["U4"])
                    for q in range(2):
                        for e_ in range(2):
                            col = 64 * q if e_ == 0 else 384 + 64 * q + 63
                            CP(ph, hrep, bc(hmod[:, :, col:col + 1], [128, 8, 128]), [("hmod", 0)], ["tmpB0"])
                            for kc in range(8):
                                MM(ph, ps[4][:], hrep[:, kc, :], win[:, kc, 1536:2048], kc == 0, kc == 7,
                                   ["tmpB0", "win"], ["ps4"])
                            CP(ph, uedge[:, 2 * q + e_, :], ps[4][:], ["ps4"], ["uedge"])
                    ph.emit()
                    checkpoint(f"A{l}")

                with contextlib.ExitStack() as sB:
                    def sbb(name, shape, dt=F32):
                        return sB.enter_context(nc.sbuf_tensor(f"{name}_{l}", list(shape), dt))
                    Et = [sbb(f"E{i}", [128, 512], BF16) for i in range(8)]
                    Esum = [sbb(f"Esum{i}", [128, 512]) for i in range(2)]
                    r0 = sbb("r0", [128, 512]); r1 = sbb("r1", [128, 512])
                    Osb = [sbb(f"Osb{i}", [128, 512]) for i in range(4)]
                    if l == 0:
                        wmB = sbb("wmB", [128, 8, 512], BF16)
                    oa = sbb("oa", [128, 512]); ob = sbb("ob", [128, 512])
                    sqb = sbb("sqb", [128, 512], BF16)
                    lnv = sbb("lnvB", [128, 512]); rstd = sbb("rstdB", [128, 512])
                    ph = Phase(nc, f"B{l}")
                    ecnt = [0]
                    BRE_v = hmod[:, 4, 0:1024].bitcast(F32).rearrange("p (a c) -> p a c", c=16)
                    BIM_v = hmod[:, 5, 0:1024].bitcast(F32).rearrange("p (a c) -> p a c", c=16)
                    ph.dma("sp", BRE_v, bre_d[l].rearrange("d (gp g2) p c -> (g2 p) (d gp) c", g2=2), writes=["BREv"])
                    ph.dma("act", BIM_v, bim_d[l].rearrange("d (gp g2) p c -> (g2 p) (d gp) c", g2=2), writes=["BIMv"])

                    def attn_unit(h, qcols, N, keychunks, outdst, outkey):
                        nkc = len(keychunks)
                        tiles = [(ci, s) for ci in range(nkc) for s in range(2)]
                        slots = {}

                        def issue_score(t):
                            ci, s = tiles[t]
                            kcol = keychunks[ci][0]
                            sbank = (0, 1, 2, 5, 7)[ecnt[0] % 5]
                            ei = ecnt[0] % 8
                            ecnt[0] += 1
                            slots[t] = ei
                            MM(ph, ps[sbank][:, 0:N], kT[64 * s:64 * s + 64, h, kcol:kcol + 128],
                               qT[64 * s:64 * s + 64, h, qcols:qcols + N], True, True, [], [psk(sbank)])
                            ACT(ph, Et[ei][:, 0:N], ps[sbank][:, 0:N], AF.Exp, [psk(sbank)], [f"E{ei}"], scale=0.125)

                        def issue_av(t):
                            ci, s = tiles[t]
                            (kcol, vt, plo, phi) = keychunks[ci]
                            ei = slots[t]
                            MM(ph, ps[4 + 2 * s][:, 0:N], Vtok[plo:phi, vt, 128 * h:128 * h + 128],
                               Et[ei][plo:phi, 0:N], ci == 0, ci == nkc - 1, [f"E{ei}"], [psk(4 + 2 * s)])
                            if ci == 0:
                                CP(ph, Esum[s][:, 0:N], Et[ei][:, 0:N], [f"E{ei}"], [f"Esum{s}"])
                            else:
                                TT(ph, Esum[s][:, 0:N], Esum[s][:, 0:N], Et[ei][:, 0:N], ALU.add,
                                   [f"E{ei}", f"Esum{s}"], [f"Esum{s}"])
                        LOOK = 4
                        for t in range(len(tiles) + LOOK):
                            if t < len(tiles):
                                issue_score(t)
                            if t >= LOOK:
                                issue_av(t - LOOK)
                        def part1():
                          ACT(ph, Osb[0][:, 0:N], ps[4][:, 0:N], AF.Copy, ["ps4"], ["Os0"])
                          ACT(ph, Osb[2][:, 0:N], ps[6][:, 0:N], AF.Copy, ["ps6"], ["Os2"])
                          plo_, phi_ = keychunks[0][2], keychunks[0][3]
                          lw = ones32[:] if phi_ - plo_ == 128 else ones32q[:, plo_ // 64, :]
                          for s_, rr in ((0, r0), (1, r1)):
                              MM(ph, ps[3][:, 0:N], lw, Esum[s_][:, 0:N], True, True, [f"Esum{s_}"], ["ps3"])
                              ACT(ph, rr[:, 0:N], ps[3][:, 0:N], AF.Ln, ["ps3"], [f"r{s_}"])
                              ACT(ph, rr[:, 0:N], rr[:, 0:N], AF.Exp, [f"r{s_}"], [f"r{s_}"], scale=-1.0)
                          TT(ph, oa[:, 0:N], Osb[0][:, 0:N], r0[:, 0:N], ALU.mult, ["Os0", "r0"], ["oa"])
                          TT(ph, ob[:, 0:N], Osb[2][:, 0:N], r1[:, 0:N], ALU.mult, ["Os2", "r1"], ["ob"])
                          STT(ph, oa[:, 0:N], ob[:, 0:N], neglam[:, l:l + 1], oa[:, 0:N], ALU.mult, ALU.add,
                              ["oa", "ob"], ["oa"])

                        def tail():
                            ACT(ph, sqb[:, 0:N], oa[:, 0:N], AF.Square, ["oa"], ["sqb"])
                            MM(ph, ps[3][:, 0:N], onesV[:], sqb[:, 0:N], True, True, ["sqb"], ["ps3"])
                            ACT(ph, lnv[:, 0:N], ps[3][:, 0:N], AF.Ln, ["ps3"], ["lnv"], bias=epsc[:])
                            ACT(ph, rstd[:, 0:N], lnv[:, 0:N], AF.Exp, ["lnv"], ["rstd"], scale=-0.5)
                            src = oa[:, 0:N]
                            rs = rstd[:, 0:N]
                            if N == 256:
                                src = src.rearrange("p (s j) -> p s j", s=4)
                                rs = rs.rearrange("p (s j) -> p s j", s=4)
                            STT(ph, outdst, src, gsub[:, l:l + 1], rs, ALU.mult, ALU.mult, ["oa", "rstd"], [outkey])
                        return part1, tail

                    units = []
                    for h in range(4):
                        for q in range(2):
                            kch = [(128 * s4, s4, 64 * q, 64 * q + 64) for s4 in range(4)]
                            dst = hmod[:, h, 0:512].rearrange("p (s q j) -> p q s j", s=4, q=2)[:, q]
                            units.append((h, 256 * q, 256, kch, dst, ("mix", h, q)))
                        for qt in range(2):
                            kch = [(512 + 128 * i, 4 + i, 0, 128) for i in range(8)] + \
                                  [(1536 + 128 * i, 12 + i, 0, 128) for i in range(4)]
                            dst = hmod[:, h, 512 + 512 * qt:512 + 512 * qt + 512]
                            units.append((h, 512 + 512 * qt, 512, kch, dst, ("mix", h, 2 + qt)))
                    pending = None
                    mod_todo = list(MOD_REST) if l == 0 else []
                    mod_pend = None
                    for ui, u_ in enumerate(units):
                        p1_, t_ = attn_unit(*u_)
                        if pending is not None:
                            pending()
                        p1_()
                        pending = t_
                        for _ in range(2 if ui < 2 else 1):
                            if mod_pend is not None:
                                mod_pend()
                                mod_pend = None
                            if mod_todo:
                                ml_, mhb_ = mod_todo.pop(0)
                                mod_pend = mod_half(ph, ml_, mhb_, wmB, "wmB", 3)
                    pending()
                    if l == 0:
                        if mod_pend is not None:
                            mod_pend()
                        while mod_todo:
                            ml_, mhb_ = mod_todo.pop(0)
                            mod_half(ph, ml_, mhb_, wmB, "wmB", 3)()
                        mod_A(ph, 0, A2, 8, 32)
                        mod_A(ph, 1, A1, 0, 8)
                        mod_A(ph, 1, A2, 8, 32)
                    ph.emit()
                    checkpoint(f"B{l}")
            sC = contextlib.ExitStack()
            with sC:
                def sbc(name, shape, dt=F32):
                    return sC.enter_context(nc.sbuf_tensor(f"{name}_{l}", list(shape), dt))
                T4 = sbc("T4", [128, 32, 128], BF16)
                B4 = sbc("B4", [128, 32, 2, 128], BF16)
                C4 = sbc("C4", [128, 32, 2, 128], BF16)
                BB = sbc("BB", [128, 2, 32, 16])
                LP = sbc("LP", [128, 5, 2, 32])
                A1b = sbc("A1b", [128, 2, 32]); A2b = sbc("A2b", [128, 2, 32])
                H0 = sbc("H0", [128, 2, 32])
                with contextlib.ExitStack() as sC1:
                    def sc1(name, shape, dt=F32):
                        return sC1.enter_context(nc.sbuf_tensor(f"{name}_{l}", list(shape), dt))
                    BP = sc1("BP", [128, 32, 4, 2, 32], BF16)
                    CPd = sc1("CPd", [128, 32, 2, 32], BF16)
                    rowt = sc1("rowt", [32, 128]); rowt2 = sc1("rowt2", [32, 128]); rowt3 = sc1("rowt3", [32, 128])
                    ls2 = sc1("ls2", [32, 2]); h0t = sc1("h0t", [32, 256]); h0r = sc1("h0r", [32, 2, 128])
                    LRE = sc1("LRE", [128, 32]); LIM = sc1("LIM", [128, 32]); DEL = sc1("DEL", [128, 32])
                    ta = sc1("ta", [128, 32]); tb = sc1("tb", [128, 32]); tcc = sc1("tcc", [128, 32])
                    ea = sc1("ea", [128, 32]); sinb = sc1("sinb", [128, 32]); cosb = sc1("cosb", [128, 32])
                    cr = sc1("cr", [128, 32]); ci = sc1("ci", [128, 32])
                    BRE = hmod[:, 4, 0:1024].bitcast(F32).rearrange("p (a c) -> p a c", c=16)
                    BIM = hmod[:, 5, 0:1024].bitcast(F32).rearrange("p (a c) -> p a c", c=16)
                    CRE = sc1("CRE", [128, 32, 16]); CIM = sc1("CIM", [128, 32, 16])
                    rowsC = sc1("rowsC", [32, 2048])
                    w1 = sc1("w1", [128, 32, 16]); w2 = sc1("w2", [128, 32, 16]); w3 = sc1("w3", [128, 32, 16])
                    dvec = sc1("dvec", [128, 16])
                    v1 = sc1("v1", [128, 16, 16]); v2 = sc1("v2", [128, 16, 16]); v3 = sc1("v3", [128, 16, 16])
                    ph = Phase(nc, f"C1{l}")
                    MS(ph, BP[:], 0.0, ["BP"]); MS(ph, CPd[:], 0.0, ["CPd"]); MS(ph, C4[:], 0.0, ["C4"])
                    ph.dma("sp", rowt[:], lre_d[l], writes=["rowt"])
                    ph.dma("sp", rowt2[:], lim_d[l], writes=["rowt2"])
                    ph.dma("sp", ls2[:], lst_d[l], writes=["ls2"])
                    for j in range(4):
                        ph.dma("sp", dvec[32 * j:32 * j + 32, :], ssmd_d[l].rearrange("(gp r) -> r gp", r=32),
                               writes=["dvec"], allow_slow_non_contiguous=True)
                    ph.dma("sp", h0t[:], stt_in[l], writes=["h0t"])
                    TR(ph, ps[0][:, 0:32], rowt[:], ident[0:32, 0:32], ["rowt", "ident"], ["ps0"])
                    CP(ph, LRE[:], ps[0][:, 0:32], ["ps0"], ["LRE"])
                    TR(ph, ps[0][:, 32:64], rowt2[:], ident[0:32, 0:32], ["rowt2", "ident"], ["ps0b"])
                    CP(ph, LIM[:], ps[0][:, 32:64], ["ps0b"], ["LIM"])
                    CP(ph, rowt3[:].rearrange("r (g p) -> r g p", g=2), bc(ls2[:].unsqueeze(2), [32, 2, 64]),
                       ["ls2"], ["rowt3"])
                    TR(ph, ps[0][:, 64:96], rowt3[:], ident[0:32, 0:32], ["rowt3", "ident"], ["ps0c"])
                    ACT(ph, DEL[:], ps[0][:, 64:96], AF.Exp, ["ps0c"], ["DEL"])
                    CP(ph, h0r[:], h0t[:].rearrange("r (gp ri) -> r ri gp", ri=2), ["h0t"], ["h0r"])
                    for ri in range(2):
                        TR(ph, ps[1][:, 32 * ri:32 * ri + 32], h0r[:, ri, :], ident[0:32, 0:32], ["h0r", "ident"], ["ps1"])
                    CP(ph, H0[:], ps[1][:, 0:64].rearrange("p (r d) -> p r d", r=2), ["ps1"], ["H0"])
                    TT(ph, ta[:], LRE[:], DEL[:], ALU.mult, ["LRE", "DEL"], ["ta"])
                    TT(ph, tb[:], LIM[:], DEL[:], ALU.mult, ["LIM", "DEL"], ["tb"])
                    ACT(ph, ea[:], ta[:], AF.Exp, ["ta"], ["ea"])
                    def range_reduce(add):
                        TS(ph, tcc[:], tb[:], add, None, ALU.add, None, ["tb", "sinb"], ["tcc"])
                        for _ in range(4):
                            TS(ph, cr[:], tcc[:], 2 * math.pi, None, ALU.is_ge, None, ["tcc"], ["cr"])
                            STT(ph, tcc[:], cr[:], -2 * math.pi, tcc[:], ALU.mult, ALU.add, ["cr", "tcc"], ["tcc"])
                    range_reduce(math.pi)
                    ACT(ph, sinb[:], tcc[:], AF.Sin, ["tcc"], ["sinb"], bias=negpi[:])
                    range_reduce(1.5 * math.pi)
                    ACT(ph, cosb[:], tcc[:], AF.Sin, ["tcc"], ["cosb"], bias=negpi[:])
                    MS(ph, LP[:, 0, 0, :], 1.0, ["LP"]); MS(ph, LP[:, 0, 1, :], 0.0, ["LP"])
                    TT(ph, LP[:, 1, 0, :], ea[:], cosb[:], ALU.mult, ["ea", "cosb"], ["LP"])
                    TT(ph, LP[:, 1, 1, :], ea[:], sinb[:], ALU.mult, ["ea", "sinb"], ["LP"])
                    for k in range(2, 5):
                        TT(ph, ta[:], LP[:, k - 1, 0, :], LP[:, 1, 0, :], ALU.mult, ["LP"], ["ta"])
                        TT(ph, tb[:], LP[:, k - 1, 1, :], LP[:, 1, 1, :], ALU.mult, ["LP"], ["tb"])
                        TT(ph, LP[:, k, 0, :], ta[:], tb[:], ALU.subtract, ["ta", "tb"], ["LP"])
                        TT(ph, ta[:], LP[:, k - 1, 0, :], LP[:, 1, 1, :], ALU.mult, ["LP"], ["ta"])
                        TT(ph, tb[:], LP[:, k - 1, 1, :], LP[:, 1, 0, :], ALU.mult, ["LP"], ["tb"])
                        TT(ph, LP[:, k, 1, :], ta[:], tb[:], ALU.add, ["ta", "tb"], ["LP"])
                    TT(ph, ta[:], LRE[:], LRE[:], ALU.mult, ["LRE"], ["ta"])
                    TT(ph, tb[:], LIM[:], LIM[:], ALU.mult, ["LIM"], ["tb"])
                    TT(ph, ta[:], ta[:], tb[:], ALU.add, ["ta", "tb"], ["ta"])
                    ph.op("dve", lambda e: e.reciprocal(out=tcc[:], in_=ta[:]), ["ta"], ["tcc"])
                    TS(ph, ea[:], LP[:, 1, 0, :], -1.0, None, ALU.add, None, ["LP"], ["ea"])
                    TT(ph, ta[:], ea[:], LRE[:], ALU.mult, ["ea", "LRE"], ["ta"])
                    TT(ph, tb[:], LP[:, 1, 1, :], LIM[:], ALU.mult, ["LP", "LIM"], ["tb"])
                    TT(ph, ta[:], ta[:], tb[:], ALU.add, ["ta", "tb"], ["ta"])
                    TT(ph, cr[:], ta[:], tcc[:], ALU.mult, ["ta", "tcc"], ["cr"])
                    TT(ph, ta[:], LP[:, 1, 1, :], LRE[:], ALU.mult, ["LP", "LRE"], ["ta"])
                    TT(ph, tb[:], ea[:], LIM[:], ALU.mult, ["ea", "LIM"], ["tb"])
                    TT(ph, ta[:], ta[:], tb[:], ALU.subtract, ["ta", "tb"], ["ta"])
                    TT(ph, ci[:], ta[:], tcc[:], ALU.mult, ["ta", "tcc"], ["ci"])
                    crb = bc(cr[:].unsqueeze(2), [128, 32, 16]); cib = bc(ci[:].unsqueeze(2), [128, 32, 16])
                    TT(ph, w1[:], BRE[:], crb, ALU.mult, ["BRE", "cr"], ["w1"])
                    TT(ph, w2[:], BIM[:], cib, ALU.mult, ["BIM", "ci"], ["w2"])
                    TT(ph, BB[:, 0], w1[:], w2[:], ALU.subtract, ["w1", "w2"], ["BB"])
                    TT(ph, w1[:], BIM[:], crb, ALU.mult, ["BIM", "cr"], ["w1"])
                    TT(ph, w2[:], BRE[:], cib, ALU.mult, ["BRE", "ci"], ["w2"])
                    TT(ph, BB[:, 1], w1[:], w2[:], ALU.add, ["w1", "w2"], ["BB"])
                    for ti_, (src_d, dstC, nm) in enumerate(((cre_d, CRE, "CRE"), (cim_d, CIM, "CIM"))):
                        srcv = src_d[l].rearrange("d (gp g2) c p -> (d gp) c g2 p", g2=2)
                        for g2 in range(2):
                            ph.dma("sp" if g2 == 0 else "act",
                                   rowsC[:].rearrange("r (c g p) -> r c g p", c=16, g=2)[:, :, g2, :], srcv[:, :, g2, :],
                                   reads=["rowsCr"], writes=["rowsC"])
                        for c_ in range(16):
                            TR(ph, ps[2][:, 32 * c_:32 * c_ + 32], rowsC[:, 128 * c_:128 * c_ + 128], ident[0:32, 0:32],
                               ["rowsC", "ident"], ["ps2"])
                        CP(ph, dstC[:], ps[2][:, :].rearrange("p (c a) -> p a c", c=16), ["ps2"], [nm, "rowsCr"])

                    def padcopy(dst_of_half, src, rkeys, wkey):
                        for g2 in range(2):
                            CP(ph, dst_of_half(g2), src[64 * g2:64 * g2 + 64], rkeys, [wkey])

                    for k in range(4):
                        if k == 0:
                            res = (BB[:, 0], BB[:, 1])
                            rk = ["BB"]
                        else:
                            lr = bc(LP[:, k, 0, :].unsqueeze(2), [128, 32, 16])
                            li = bc(LP[:, k, 1, :].unsqueeze(2), [128, 32, 16])
                            TT(ph, w1[:], BB[:, 0], lr, ALU.mult, ["BB", "LP"], ["w1"])
                            TT(ph, w2[:], BB[:, 1], li, ALU.mult, ["BB", "LP"], ["w2"])
                            TT(ph, w1[:], w1[:], w2[:], ALU.subtract, ["w1", "w2"], ["w1"])
                            TT(ph, w2[:], BB[:, 1], lr, ALU.mult, ["BB", "LP"], ["w2"])
                            TT(ph, w3[:], BB[:, 0], li, ALU.mult, ["BB", "LP"], ["w3"])
                            TT(ph, w2[:], w2[:], w3[:], ALU.add, ["w2", "w3"], ["w2"])
                            res = (w1[:], w2[:])
                            rk = ["w1", "w2"]
                        for ri in range(2):
                            padcopy(lambda g2, k=k, ri=ri: BP[64 * g2:64 * g2 + 64, :, k, ri, 16 * g2:16 * g2 + 16],
                                    res[ri], rk, "BP")
                    padcopy(lambda g2: CPd[64 * g2:64 * g2 + 64, :, 0, 16 * g2:16 * g2 + 16], CRE[:], ["CRE"], "CPd")
                    TS(ph, w3[:], CIM[:], -1.0, None, ALU.mult, None, ["CIM"], ["w3"])
                    padcopy(lambda g2: CPd[64 * g2:64 * g2 + 64, :, 1, 16 * g2:16 * g2 + 16], w3[:], ["w3"], "CPd")
                    for d in range(2):
                        dgs = slice(16 * d, 16 * d + 16)
                        for t4 in range(4):
                            k = t4 + 1 if d == 0 else 4 - t4
                            lr = bc(LP[:, k, 0, dgs].unsqueeze(2), [128, 16, 16])
                            li = bc(LP[:, k, 1, dgs].unsqueeze(2), [128, 16, 16])
                            TT(ph, v1[:, 0:16], CRE[:, dgs], lr, ALU.mult, ["CRE", "LP"], ["v1"])
                            TT(ph, v2[:, 0:16], CIM[:, dgs], li, ALU.mult, ["CIM", "LP"], ["v2"])
                            TT(ph, v1[:, 0:16], v1[:, 0:16], v2[:, 0:16], ALU.subtract, ["v1", "v2"], ["v1"])
                            TT(ph, v2[:, 0:16], CRE[:, dgs], li, ALU.mult, ["CRE", "LP"], ["v2"])
                            TT(ph, v3[:, 0:16], CIM[:, dgs], lr, ALU.mult, ["CIM", "LP"], ["v3"])
                            TT(ph, v2[:, 0:16], v2[:, 0:16], v3[:, 0:16], ALU.add, ["v2", "v3"], ["v2"])
                            TS(ph, v2[:, 0:16], v2[:, 0:16], -1.0, None, ALU.mult, None, ["v2"], ["v2"])
                            for ri, srcw in enumerate((v1, v2)):
                                for g2 in range(2):
                                    CP(ph, C4[64 * g2:64 * g2 + 64, dgs, ri, 32 * t4 + 16 * g2:32 * t4 + 16 * g2 + 16],
                                       srcw[64 * g2:64 * g2 + 64, 0:16], ["v1", "v2"], ["C4"])
                    for dg in range(32):
                        d = dg // 16
                        gp = dg % 16
                        bank = 4 + dg % 2
                        for ri in range(2):
                            for s4 in range(4):
                                k = 3 - s4 if d == 0 else s4
                                MM(ph, ps[bank][32 * s4:32 * s4 + 32, 128 * ri:128 * ri + 128], BP[:, dg, k, ri, :],
                                   identb[:], True, True, ["BP", "identb"], [psk(bank)], tp=(0, 32 * s4))
                        MM(ph, ps[bank][:, 256:384], zerob[:], identb[:], True, True, ["zerob", "identb"], [psk(bank)])
                        for s4 in range(4):
                            for t4 in range(4):
                                tau = t4 - s4 if d == 0 else s4 - t4
                                if tau < 0:
                                    continue
                                for ri in range(2):
                                    ph.op("pe", lambda e, bank=bank, s4=s4, t4=t4, dg=dg, tau=tau, ri=ri: e.matmul(
                                        ps[bank][32 * s4:32 * s4 + 32, 256 + 32 * t4:256 + 32 * t4 + 32],
                                        BP[:, dg, tau, ri, :], CPd[:, dg, ri, :], start=False, stop=True,
                                        tile_position=(0, 32 * s4), skip_group_check=True),
                                        ["BP", "CPd"], [psk(bank)])
                        ACT(ph, B4[:, dg, :, :], ps[bank][:, 0:256].rearrange("p (r m) -> p r m", r=2), AF.Copy,
                            [psk(bank)], ["B4"])
                        if d == 0:
                            STT(ph, T4[:, dg, :], ident[:], dvec[:, gp:gp + 1], ps[bank][:, 256:384], ALU.mult, ALU.add,
                                [psk(bank), "dvec", "ident"], ["T4"])
                        else:
                            CP(ph, T4[:, dg, :], ps[bank][:, 256:384], [psk(bank)], ["T4"])
                    CP(ph, A1b[:, 0, :], LP[:, 4, 0, :], ["LP"], ["A1b"]); CP(ph, A1b[:, 1, :], LP[:, 4, 0, :], ["LP"], ["A1b"])
                    TS(ph, A2b[:, 0, :], LP[:, 4, 1, :], -1.0, None, ALU.mult, None, ["LP"], ["A2b"])
                    CP(ph, A2b[:, 1, :], LP[:, 4, 1, :], ["LP"], ["A2b"])
                    ph.emit()
                    checkpoint(f"C1{l}")

                with contextlib.ExitStack() as sC2:
                    def sc2(name, shape, dt=F32):
                        return sC2.enter_context(nc.sbuf_tensor(f"{name}_{l}", list(shape), dt))
                    VZ = sc2("VZ", [128, 2, 32, 384], BF16)
                    with contextlib.ExitStack() as sC2a:
                        hE = sC2a.enter_context(nc.sbuf_tensor(f"hE_{l}", [128, 2, 2, 32], F32))
                        pr = sC2a.enter_context(nc.sbuf_tensor(f"pr_{l}", [128, 16, 16], F32))
                        hst = sC2a.enter_context(nc.sbuf_tensor(f"hst_{l}", [32, 2, 128, 2], F32))
                        ph = Phase(nc, f"C2{l}")
                        for dg in range(32):
                            gp = dg % 16
                            for ri in range(2):
                                bank = (2 * dg + ri) % 4
                                MM(ph, ps[bank][:, 0:384], B4[:, dg, ri, :], U4[:, gp, :], True, True, [], [psk(bank)])
                                if ri == 0:
                                    CP(ph, VZ[:, ri, dg, :], ps[bank][:, 0:384], [psk(bank)], [])
                                else:
                                    ACT(ph, VZ[:, ri, dg, :], ps[bank][:, 0:384], AF.Copy, [psk(bank)], [])
                        for q in range(2):
                            for e_ in range(2):
                                for ri in range(2):
                                    for g2 in range(2):
                                        hs = slice(64 * g2, 64 * g2 + 64)
                                        uev = uedge[hs, 2 * q + e_, :].rearrange("p (gp g c) -> p gp g c", g=2, c=16)[:, :, g2, :]
                                        TT(ph, pr[hs], BB[hs, ri, 16 * e_:16 * e_ + 16, :], uev, ALU.mult, [], ["pr"])
                                        ph.op("dve", lambda e, hs=hs, q=q, ri=ri, e_=e_: e.tensor_reduce(
                                            out=hE[hs, q, ri, 16 * e_:16 * e_ + 16], in_=pr[hs], axis=AX.X, op=ALU.add),
                                            ["pr"], ["hE"])
                        for q in range(2):
                            for ri in range(2):
                                TR(ph, ps[4][0:32, 128 * ri:128 * ri + 128], hE[:, q, ri, :], ident[:], ["hE", "ident"], ["ps4"])
                            CP(ph, hst[:, q], ps[4][0:32, 0:256].rearrange("r (ri gp) -> r gp ri", ri=2), ["ps4"], ["hst"])
                            ph.dma("sp", nssm[q, l], hst[:, q].rearrange("r gp ri -> r (gp ri)"), reads=["hst"])
                        ph.emit()
                        checkpoint(f"C2{l}")
                    with contextlib.ExitStack() as sC3:
                        def sc3(name, shape, dt=F32):
                            return sC3.enter_context(nc.sbuf_tensor(f"{name}_{l}", list(shape), dt))
                        zS = [sc3(f"zS{i}", [128, 3, 32]) for i in range(3)]
                        tS1 = sc3("tS1", [128, 2, 32]); tS2 = sc3("tS2", [128, 2, 32])
                        zP = {(d, i): sc3(f"zP{d}{i}", [128, 3, 16, 2]) for d in range(2) for i in range(3)}
                        tP1 = [sc3(f"tP1{d}", [128, 2, 16, 2]) for d in range(2)]
                        tP2 = [sc3(f"tP2{d}", [128, 2, 16, 2]) for d in range(2)]
                        ph = Phase(nc, f"C3{l}")
                        CP(ph, zS[0][:, 0:2, :], H0[:], [], ["zS0"])
                        for d in range(2):
                            MS(ph, zP[(d, 0)][:], 0.0, [f"zP{d}0"], eng="pool")
                        vz0 = VZ[:, 0:1, 0:1, 0:1]
                        pstep = vz0.ap[0][0]

                        def vz_merged(jf, jb_):
                            return bass.AP(vz0.tensor, vz0.offset + jf,
                                           [[pstep, 128], [32 * 384, 2], [16 * 384 + (jb_ - jf), 2], [384, 16]])
                        NBS = 256
                        for i in range(NBS):
                            vj = vz_merged(128 + i, 128 + NBS - 1 - i)
                            zc = zS[i % 3]; zn = zS[(i + 1) % 3]
                            kc_ = f"zS{i % 3}"; kn_ = f"zS{(i + 1) % 3}"
                            TT(ph, tS1[:], zc[:, 0:2, :], A1b[:], ALU.mult, [kc_], ["tS1"])
                            zv = zc[:, 1:2, :]
                            zsw = bass.AP(zv.tensor, zv.offset, [[zv.ap[0][0], 128], [-32, 2], [1, 32]])
                            TT(ph, tS2[:], zsw, A2b[:], ALU.mult, [kc_], ["tS2"])
                            TT(ph, tS1[:], tS1[:], tS2[:], ALU.add, ["tS1", "tS2"], ["tS1"])
                            TT(ph, zn[:, 0:2, :].rearrange("p r (d g) -> p r d g", d=2),
                               tS1[:].rearrange("p r (d g) -> p r d g", d=2), vj, ALU.add,
                               ["tS1", ("vzS", i)], [kn_])
                            ACT(ph, vj, zc[:, 0:2, :].rearrange("p r (d g) -> p r d g", d=2), AF.Copy,
                                [kc_], [("vzS", i)])
                            if i < 64:
                                for d in range(2):
                                    dgs = slice(16 * d, 16 * d + 16)
                                    j = i if d == 0 else 63 - i
                                    pc = zP[(d, i % 3)]; pn = zP[(d, (i + 1) % 3)]
                                    pk = f"zP{d}{i % 3}"; pkn = f"zP{d}{(i + 1) % 3}"
                                    vp = VZ[:, :, dgs, j:j + 65:64]
                                    a1 = bc(A1b[:, :, dgs].unsqueeze(3), [128, 2, 16, 2])
                                    a2 = bc(A2b[:, :, dgs].unsqueeze(3), [128, 2, 16, 2])
                                    TT(ph, tP1[d][:], pc[:, 0:2], a1, ALU.mult, [pk], [f"tP1{d}"], eng="pool")
                                    TT(ph, tP2[d][:], pc[:, 1:3], a2, ALU.mult, [pk], [f"tP2{d}"], eng="pool")
                                    TT(ph, tP1[d][:], tP1[d][:], tP2[d][:], ALU.add, [f"tP1{d}", f"tP2{d}"],
                                       [f"tP1{d}"], eng="pool")
                                    TT(ph, pn[:, 0:2], tP1[d][:], vp, ALU.add, [f"tP1{d}", ("vzP", d, j)], [pkn],
                                       eng="pool")
                                    CP(ph, pn[:, 2], pn[:, 0], [pkn], [pkn], eng="pool")
                                    CP(ph, vp, pc[:, 0:2], [pk], [("vzP", d, j)], eng="pool")
                        ph.emit()
                        checkpoint(f"C3{l}")
                    with contextlib.ExitStack() as sC4:
                        def sc4(name, shape, dt=F32):
                            return sC4.enter_context(nc.sbuf_tensor(f"{name}_{l}", list(shape), dt))
                        zf = B4[:, 0:16].rearrange("p a r m -> p (a r m)").bitcast(F32).rearrange("p (c n) -> p c n", c=4)
                        zb = B4[:, 16:24].rearrange("p a r m -> p (a r m)").rearrange("p (c n) -> p c n", c=4)
                        wg = B4[:, 24:32].rearrange("p a r m -> p (a r m)").rearrange("p (c n) -> p c n", c=4)
                        sg = sc4("sg", [128, 512])
                        ph = Phase(nc, f"C4{l}")
                        for kc in range(4):
                            ph.dma("pool", wg[:, kc, :], w_glu[l, 128 * kc:128 * kc + 128, :], writes=["wg"])
                        for ct in range(3):
                            cols = slice(512 * ct, 512 * ct + 512)
                            for ch in range(4):
                                bank = ch
                                if ct == 0:
                                    regions = [(t4, 128 * t4, 128, slice(0, 128)) for t4 in range(4)]
                                else:
                                    regions = [(t4, 256 * (t4 % 2), 256, slice(128, 384)) for t4 in
                                               (2 * (ct - 1), 2 * (ct - 1) + 1)]
                                for (t4, off, N, ucols) in regions:
                                    for n in range(6):
                                        d, w_ = n // 3, n % 3
                                        for g4 in range(4):
                                            gp = 4 * ch + g4
                                            dg = 16 * d + gp
                                            outap = ps[bank][32 * g4:32 * g4 + 32, off:off + N]
                                            if w_ == 0:
                                                lw, rh = T4[:, dg, 32 * t4:32 * t4 + 32], U4[:, gp, ucols]
                                            else:
                                                lw = C4[:, dg, w_ - 1, 32 * t4:32 * t4 + 32]
                                                rh = VZ[:, w_ - 1, dg, ucols]
                                            MM(ph, outap, lw, rh, n == 0, n == 5, [], [psk(bank)], tp=(0, 32 * g4))
                                ACT(ph, zf[:, ch, :], ps[bank][:], AF.Gelu_apprx_tanh, [psk(bank)], [("zf", ch)])
                                CP(ph, zb[:, ch, :], zf[:, ch, :], [("zf", ch)], [("zb", ch)])
                            for m in range(4):
                                bank = 4 + m % 2
                                for kc in range(4):
                                    MM(ph, ps[bank][:], wg[:, kc, 128 * m:128 * m + 128], zb[:, kc, :], kc == 0, kc == 3,
                                       ["wg"] + [("zb", kc)], [psk(bank)])
                                ACT(ph, sg[:], ps[bank][:], AF.Sigmoid, [psk(bank)], ["sg"], bias=VT[:, l, 96 + m:97 + m])
                                TT(ph, hmod[:, 4 + m, cols], zf[:, m, :], sg[:], ALU.mult, [("zf", m), "sg"], [("mix2", m, ct)])
                        ph.emit()
                        checkpoint(f"C4{l}")
            sL.__exit__(None, None, None)

            with contextlib.ExitStack() as sD:
                wo = sD.enter_context(nc.sbuf_tensor(f"wo_{l}", [128, 8, 1024], BF16))
                ph = Phase(nc, f"D{l}")
                wov = w_out[l].rearrange("(kc p) f -> p kc f", p=128)
                for hf_ in range(2):
                    ph.dma("pool", wo[:, :, 512 * hf_:512 * hf_ + 512], wov[:, :, 512 * hf_:512 * hf_ + 512],
                           writes=[("wo", hf_)])
                for ct in range(3):
                    cd = 0 if ct == 0 else 1
                    cols = slice(512 * ct, 512 * ct + 512)
                    for m in range(8):
                        bank = m % 4
                        for kc in range(8):
                            MM(ph, ps[bank][:], wo[:, kc, 128 * m:128 * m + 128], hmod[:, kc, cols], kc == 0, kc == 7,
                               [("wo", m // 4)], [psk(bank)])
                        STT(ph, X[:, m, cols], ps[bank][:], modsb[:, l, 16 + m, cd:cd + 1], X[:, m, cols],
                            ALU.mult, ALU.add, [psk(bank)], [("X", m, ct)])
                ph.emit()
                checkpoint(f"D{l}")

            with contextlib.ExitStack() as sE:
                gact = sE.enter_context(nc.sbuf_tensor(f"gact_{l}", [128, 16, NCOL], BF16))
                with contextlib.ExitStack() as sE1:
                    def se1(name, shape, dt=F32):
                        return sE1.enter_context(nc.sbuf_tensor(f"{name}_{l}", list(shape), dt))
                    sq = [se1(f"sqE{i}", [128, 512], BF16) for i in range(2)]
                    tmp = [se1(f"tmpE{i}", [128, 512]) for i in range(2)]
                    lnv = se1("lnvE", [128, 512]); rstd = se1("rstdE", [128, 512])
                    wu = [se1(f"wu{i}", [128, 8, 1024], BF16) for i in range(2)]
                    zzs = [[se1(f"zz{a}{i}", [128, NCOL]) for i in range(2)] for a in range(2)]
                    ph = Phase(nc, f"E{l}")
                    norm_mod(ph, l, A2, 24, sq, lnv, rstd, tmp)
                    wupv = w_up[l].rearrange("(kc p) f -> p kc f", p=128)
                    for j in range(16):
                        qd, jj = j // 4, j % 4
                        wb = wu[qd % 2]
                        wk = f"wu{qd % 2}"
                        if jj == 0:
                            for qn in ([0, 1] if qd == 0 else ([qd + 1] if qd + 1 < 4 else [])):
                                wbn = wu[qn % 2]
                                wkn = f"wu{qn % 2}"
                                ph.dma("pool", wbn[:, :, 0:512], wupv[:, :, 512 * qn:512 * qn + 512],
                                       reads=[wkn + "r"], writes=[wkn])
                                ph.dma("pool", wbn[:, :, 512:1024], wupv[:, :, 2048 + 512 * qn:2048 + 512 * qn + 512],
                                       reads=[wkn + "r"], writes=[wkn])
                        zz = zzs[j % 2]
                        for hf in range(2):
                            f = j + 16 * hf
                            z = zz[hf]
                            zk = f"zz{j % 2}{hf}"
                            bks = [3 * hf + ct for ct in range(3)]
                            for ct in range(3):
                                for kc in range(8):
                                    MM(ph, ps[bks[ct]][:], wb[:, kc, 512 * hf + 128 * jj:512 * hf + 128 * jj + 128],
                                       hmod[:, kc, 512 * ct:512 * ct + 512], kc == 0, kc == 7,
                                       [wk, ("hmod", ct)],
                                       [psk(bks[ct])] + ([wk + "r"] if (jj == 3 and hf == 1 and ct == 2 and kc == 7) else []))
                            w0 = VTB[:, l, f:f + 1]; w1c = VTB[:, l, 32 + f:33 + f]; w2c = VTB[:, l, 64 + f:65 + f]
                            bcol = VT[:, l, 64 + f:65 + f]
                            RK = {n_: (zk, n_) for n_ in ("A", "B", "C", "D", "E1", "E2", "F")}
                            ctk = [[RK["A"], RK["B"], RK["C"]], [RK["D"], RK["E1"]], [RK["E2"], RK["F"]]]
                            for ct in range(3):
                                ACT(ph, z[:, 512 * ct:512 * ct + 512], ps[bks[ct]][:], AF.Identity, [psk(bks[ct])], ctk[ct],
                                    scale=w1c, bias=bcol)
                            pP, pS1, pS2 = ps[bks[0]], ps[bks[1]], ps[bks[2]]

                            def tap(dst, src, wcol, pbank, regs):
                                ks = [RK[r_] for r_ in regs]
                                STT(ph, dst, src, wcol, dst, ALU.mult, ALU.add, [psk(pbank)] + ks, ks)
                            tap(z[:, 128:512], pP[:, 0:384], w0, bks[0], ["B", "C"])
                            tap(z[:, 768:1280], pS1[:, 0:512], w0, bks[1], ["E1", "E2"])
                            tap(z[:, 0:128].rearrange("p (q j) -> p q j", q=2)[:, :, 1:64],
                                pP[:, 384:512].rearrange("p (q j) -> p q j", q=2)[:, :, 0:63], w0, bks[0], ["A"])
                            tap(z[:, 1280:1536], pS2[:, 0:256], w0, bks[2], ["F"])
                            tap(z[:, 513:768], pS2[:, 256:511], w0, bks[2], ["D"])
                            tap(z[:, 0:384], pP[:, 128:512], w2c, bks[0], ["A", "B"])
                            tap(z[:, 768:1280], pS2[:, 0:512], w2c, bks[2], ["E1", "E2"])
                            tap(z[:, 384:512].rearrange("p (q j) -> p q j", q=2)[:, :, 0:63],
                                pP[:, 0:128].rearrange("p (q j) -> p q j", q=2)[:, :, 1:64], w2c, bks[0], ["C"])
                            tap(z[:, 512:768], pS1[:, 256:512], w2c, bks[1], ["D"])
                            tap(z[:, 1280:1535], pS1[:, 1:256], w2c, bks[1], ["F"])
                        allk = lambda zk_: [(zk_, n_) for n_ in ("A", "B", "C", "D", "E1", "E2", "F")]
                        ACT(ph, zz[1][:], zz[1][:], AF.Silu, allk(f"zz{j % 2}1"), allk(f"zz{j % 2}1"))
                        TT(ph, gact[:, j, :], zz[0][:], zz[1][:], ALU.mult, allk(f"zz{j % 2}0") + allk(f"zz{j % 2}1"),
                           [("gact", j)], eng="pool")
                    for kc in range(16):
                        wbn = wu[kc // 8]
                        wkn = f"wu{kc // 8}"
                        ph.dma("pool", wbn[:, kc % 8, :], w_down[l, 128 * kc:128 * kc + 128, :],
                               reads=[wkn + "r"], writes=[("wd", kc)])
                    for ct in range(3):
                        cd = 0 if ct == 0 else 1
                        cols = slice(512 * ct, 512 * ct + 512)
                        for m in range(8):
                            bank = 6 + m % 2
                            for kc in range(16):
                                MM(ph, ps[bank][:], wu[kc // 8][:, kc % 8, 128 * m:128 * m + 128], gact[:, kc, cols],
                                   kc == 0, kc == 15, [("wd", kc), ("gact", kc)], [psk(bank)])
                            STT(ph, X[:, m, cols], ps[bank][:], modsb[:, l, 40 + m, cd:cd + 1], X[:, m, cols],
                                ALU.mult, ALU.add, [psk(bank)], [("X", m, ct)])
                    ph.emit()
                    checkpoint(f"E{l}")
                    checkpoint(f"F{l}")

        with contextlib.ExitStack() as sZ:
            ot = [sZ.enter_context(nc.sbuf_tensor(f"ot{i}", [128, 1024], F32)) for i in range(2)]
            ph = Phase(nc, "Z")
            for tt in range(12):
                o = ot[tt % 2]
                ok_ = f"ot{tt % 2}"
                for half in range(2):
                    bank = (2 * tt + half) % 4
                    for c4 in range(4):
                        c = 4 * half + c4
                        TR(ph, ps[bank][:, 128 * c4:128 * c4 + 128], X[:, c, 128 * tt:128 * tt + 128], ident[:], [],
                           [psk(bank)])
                    if half == 0:
                        CP(ph, o[:, 0:512], ps[bank][:], [psk(bank)], [ok_])
                    else:
                        ACT(ph, o[:, 512:1024], ps[bank][:], AF.Copy, [psk(bank)], [ok_])
                if tt < 4:
                    for q in range(2):
                        ph.dma("sp", yp[q].rearrange("(j s) d -> s j d", s=4)[tt], o[64 * q:64 * q + 64, :], reads=[ok_])
                else:
                    s4, hf = (tt - 4) // 2, (tt - 4) % 2
                    ph.dma("sp", ys.rearrange("(j s) d -> s j d", s=4)[s4, 128 * hf:128 * hf + 128, :], o[:, :], reads=[ok_])
            ph.emit()


_CACHE = {}


def _consts():
    ident = np.eye(128, dtype=np.float32)
    n = np.arange(1024)
    tok = 4 * (n % 256) + n // 256
    row = (tok // 64).astype(np.float64)
    colp = (tok % 64).astype(np.float64)
    p = np.arange(128)
    dd = p % 64
    axis = dd // 32
    half = (dd % 32) // 16
    f = dd % 16
    inv = 1.0 / (10000.0 ** (np.arange(8, dtype=np.float32) / np.float32(8)))
    inv16 = np.concatenate([inv, inv])
    nf = 16 // 2
    invf = (1.0 / (np.float32(10000.0) ** (np.arange(16, dtype=np.float32) / np.float32(16)))).astype(np.float32)
    pos = np.where(axis[:, None] == 0, row[None, :], colp[None, :]).astype(np.float32)
    ang = pos * invf[f][:, None]
    cosT = np.cos(ang).astype(np.float32)
    sinT = np.sin(ang).astype(np.float32)
    rsign = np.zeros((128, 128), np.float32)
    for m in range(128):
        if half[m] == 0:
            rsign[m + 16, m] = -1.0
        else:
            rsign[m - 16, m] = 1.0
    return ident, cosT, sinT, rsign


def make_inmaps(inp):
    f32 = np.float32
    g = {k: np.ascontiguousarray(np.asarray(v, dtype=f32)) for k, v in inp.items()}
    ident, cosT, sinT, rsign = _consts()
    va = np.concatenate([g["g_norm1"].reshape(2, 8, 128), g["g_norm2"].reshape(2, 8, 128),
                         g["b_mod"].reshape(2, 48, 128), g["conv_b"].reshape(2, 32, 128),
                         g["b_glu"].reshape(2, 4, 128)], axis=1)
    vb = g["conv_w"].reshape(2, 96, 128)
    qkn = np.stack([g["q_norm"], g["k_norm"]], axis=1)
    lamv = np.stack([g["lambda_q1"], g["lambda_k1"], g["lambda_q2"], g["lambda_k2"]], axis=1)
    shared = dict(w_mod=g["w_mod"], w_in=g["w_in"], w_out=g["w_out"], w_glu=g["w_glu"], w_up=g["w_up"],
                  w_down=g["w_down"], va=np.ascontiguousarray(va), vb=np.ascontiguousarray(vb),
                  qkn=np.ascontiguousarray(qkn), lamv=np.ascontiguousarray(lamv), subg=g["subln_g"],
                  lre=g["ssm_lambda_re"].reshape(2, 32, 128), lim=g["ssm_lambda_im"].reshape(2, 32, 128),
                  lstep=g["ssm_log_step"].reshape(2, 32, 2), bre=g["ssm_b_re"], bim=g["ssm_b_im"],
                  cre=g["ssm_c_re"], cim=g["ssm_c_im"], ssmd=g["ssm_d"],
                  ident=ident, cosT=cosT, sinT=sinT, rsign=rsign)
    in_maps = []
    for c in range(8):
        m = dict(shared)
        m["xp"] = np.ascontiguousarray(g["x_prompt"][2 * c:2 * c + 2])
        m["xs"] = np.ascontiguousarray(g["x_sample"][c])
        m["ck"] = np.ascontiguousarray(g["cache_k"][c].reshape(2, 512, 512))
        m["cv"] = np.ascontiguousarray(g["cache_v"][c].reshape(2, 512, 512))
        m["st"] = np.ascontiguousarray(g["state_ssm"][c].reshape(2, 32, 256))
        m["cvec"] = np.ascontiguousarray(np.stack([g["c_ctx"], g["c"][c]], axis=0).reshape(16, 128))
        in_maps.append(m)
    return in_maps


def kernel(**inp):
    f32 = np.float32
    if "nc" not in _CACHE:
        _CACHE["nc"] = build_program()
    nc = _CACHE["nc"]
    in_maps = make_inmaps(inp)
    res = run_bass_kernel_spmd(nc, in_maps, core_ids=list(range(8)))
    R = res.results
    y_prompt = np.concatenate([R[c]["yp"] for c in range(8)], axis=0).astype(f32)
    y_sample = np.stack([R[c]["ys"] for c in range(8)], axis=0).astype(f32)
    new_k = np.concatenate([R[c]["nk"] for c in range(8)], axis=0).reshape(16, 2, 256, 4, 2, 64).astype(f32)
    new_v = np.concatenate([R[c]["nv"] for c in range(8)], axis=0).reshape(16, 2, 256, 4, 128).astype(f32)
    new_ssm = np.concatenate([R[c]["nssm"] for c in range(8)], axis=0).reshape(16, 2, 2, 32, 64, 2).astype(f32)
    return (y_prompt, y_sample, new_k, new_v, new_ssm)
```
